# Optimizing a Trainium2 kernel written in Bass

```python
import math
import jax, jax.numpy as jnp
from jax import lax
import numpy as np

D_MODEL = 1024
BATCH = 8
SEQ = 4096
DEPTH = 2
DEC_BATCH = 16
DEC_SEQ = 64
PAST_LEN = 1024

CHUNK = 64
Q_BLOCK = 128
N_MEM = 256
MEM_HEADS = 4
MEM_HD = 64
MEM_W = MEM_HEADS * MEM_HD
MIX_W = D_MODEL - MEM_W
RET_HEADS = 6
RET_HD = MIX_W // RET_HEADS
RET_THETA = 10000.0
DIFF_HEADS = 6
DIFF_HD = MIX_W // (2 * DIFF_HEADS)
ROPE_THETA = 500000.0
ROT_DIM = DIFF_HD // 4
D_FF = -(-8 * D_MODEL // (3 * 256)) * 256
ALPHA = (2 * DEPTH) ** 0.25
BETA = (8 * DEPTH) ** -0.25
N_RET = (DEPTH + 1) // 2
N_DIFF = DEPTH // 2
LN_EPS = 1e-5
NEG_INF = -1e30

kernel_name = "chunk_causal_retention_diffattn_memory_encoder_step"


def _lambda_init(layer_idx):
    return 0.8 - 0.6 * math.exp(-0.3 * layer_idx)


def _layer_norm(x, g, b):
    xf = x.astype(jnp.float32)
    mu = jnp.mean(xf, axis=-1, keepdims=True)
    var = jnp.mean(jnp.square(xf - mu), axis=-1, keepdims=True)
    return ((xf - mu) * lax.rsqrt(var + LN_EPS) * g.astype(jnp.float32) + b.astype(jnp.float32)).astype(x.dtype)


def _post_norm(x, y, g, b):
    return _layer_norm(ALPHA * x + y, g, b)


def _rope(x, pos, rot_dim, theta):
    half = rot_dim // 2
    inv_freq = jnp.exp(-math.log(theta) * jnp.arange(half, dtype=jnp.float32) * 2.0 / rot_dim)
    ang = pos.astype(jnp.float32)[:, None] * inv_freq[None, :]
    bshape = (pos.shape[0],) + (1,) * (x.ndim - 3) + (half,)
    cos = jnp.cos(ang).reshape(bshape)
    sin = jnp.sin(ang).reshape(bshape)
    xf = x.astype(jnp.float32)
    x1 = xf[..., :half]
    x2 = xf[..., half:rot_dim]
    out = jnp.concatenate([x1 * cos - x2 * sin, x1 * sin + x2 * cos, xf[..., rot_dim:]], axis=-1)
    return out.astype(x.dtype)


def _memory_attention(mq, mem_k, mem_v):
    B, S = mq.shape[:2]
    s = jnp.einsum('bshd,bmhd->bhsm', mq, mem_k).astype(jnp.float32) * MEM_HD ** -0.5
    p = jax.nn.softmax(s, axis=-1)
    o = jnp.einsum('bhsm,bmhd->bshd', p, mem_v.astype(jnp.float32))
    return o.reshape(B, S, MEM_W)


def _retention_scan(q, k, v, r0):
    B, S, H, DK = q.shape
    DV = v.shape[-1]
    cl = min(S, CHUNK)
    nc = S // cl
    log_g = jnp.log(1.0 - jnp.exp2(-5.0 - jnp.arange(H, dtype=jnp.float32)))
    idx = jnp.arange(cl, dtype=jnp.float32)
    decay_in = jnp.exp(jnp.abs(idx[:, None] - idx[None, :])[None] * log_g[:, None, None])
    xi = jnp.exp((idx + 1.0)[:, None] * log_g[None, :])
    zeta = jnp.exp((cl - 1.0 - idx)[:, None] * log_g[None, :])
    g_chunk = jnp.exp(cl * log_g)

    def to_chunks(t):
        return t.reshape(B, nc, cl, H, t.shape[-1]).transpose(1, 0, 2, 3, 4)

    def step(r, qkv):
        qc, kc, vc = qkv
        inner = jnp.einsum('bihd,bjhd->bhij', qc, kc) * decay_in
        o = (jnp.einsum('bhij,bjhe->bihe', inner, vc)
             + jnp.einsum('bihd,bhde->bihe', qc, r) * xi[None, :, :, None])
        r = g_chunk[None, :, None, None] * r + jnp.einsum('bjhd,bjhe,jh->bhde', kc, vc, zeta)
        return r, o

    r, o = lax.scan(step, r0, (to_chunks(q), to_chunks(k), to_chunks(v)))
    return o.transpose(1, 0, 2, 3, 4).reshape(B, S, H, DV), r


def _retention_mixer(x, pos, r0, mem_k, mem_v, w_in, gn_g):
    B, S, _ = x.shape
    proj = x @ w_in
    q, k, v, g, mq = jnp.split(proj, [MIX_W, 2 * MIX_W, 3 * MIX_W, 4 * MIX_W], axis=-1)
    q = _rope(q.reshape(B, S, RET_HEADS, RET_HD), pos, RET_HD, RET_THETA).astype(jnp.float32)
    k = _rope(k.reshape(B, S, RET_HEADS, RET_HD), pos, RET_HD, RET_THETA).astype(jnp.float32) * RET_HD ** -0.5
    v = v.reshape(B, S, RET_HEADS, RET_HD).astype(jnp.float32)
    o, r_new = _retention_scan(q, k, v, r0.astype(jnp.float32))
    mu = jnp.mean(o, axis=-1, keepdims=True)
    var = jnp.mean(jnp.square(o - mu), axis=-1, keepdims=True)
    o = ((o - mu) * lax.rsqrt(var + LN_EPS)).reshape(B, S, MIX_W) * gn_g.astype(jnp.float32)
    o = o * jax.nn.silu(g.astype(jnp.float32))
    m = _memory_attention(mq.reshape(B, S, MEM_HEADS, MEM_HD), mem_k, mem_v)
    return jnp.concatenate([o, m], axis=-1).astype(x.dtype), r_new


def _diff_project(x, pos, w_in):
    B, S, _ = x.shape
    proj = x @ w_in
    q, k, v, mq = jnp.split(proj, [MIX_W, 2 * MIX_W, 3 * MIX_W], axis=-1)
    q = _rope(q.reshape(B, S, DIFF_HEADS, 2, DIFF_HD), pos, ROT_DIM, ROPE_THETA)
    k = _rope(k.reshape(B, S, DIFF_HEADS, 2, DIFF_HD), pos, ROT_DIM, ROPE_THETA)
    v = v.reshape(B, S, DIFF_HEADS, 2 * DIFF_HD)
    mq = mq.reshape(B, S, MEM_HEADS, MEM_HD)
    return q, k, v, mq


def _diff_lambda(lq1, lk1, lq2, lk2, lam_init):
    f = jnp.float32
    return (jnp.exp(jnp.sum(lq1.astype(f) * lk1.astype(f)))
            - jnp.exp(jnp.sum(lq2.astype(f) * lk2.astype(f))) + lam_init)


def _diff_weighted_values(s, lam, vf):
    p = jax.nn.softmax(s, axis=-1)
    a = p[:, :, 0] - lam * p[:, :, 1]
    return jnp.einsum('bhqk,bkhe->bqhe', a, vf)


def _diff_attention_prompt(q, k, v, lam):
    B, S = q.shape[:2]
    nb = S // Q_BLOCK
    qb = q.reshape(B, nb, Q_BLOCK, DIFF_HEADS, 2, DIFF_HD).transpose(1, 0, 2, 3, 4, 5)
    key_chunk = jnp.arange(S) // CHUNK
    vf = v.astype(jnp.float32)

    def block(args):
        q_blk, b_idx = args
        s = jnp.einsum('bqhcd,bkhcd->bhcqk', q_blk, k).astype(jnp.float32) * DIFF_HD ** -0.5
        q_chunk = (b_idx * Q_BLOCK + jnp.arange(Q_BLOCK)) // CHUNK
        mask = key_chunk[None, :] <= q_chunk[:, None]
        s = jnp.where(mask, s, NEG_INF)
        return _diff_weighted_values(s, lam, vf)

    o = lax.map(block, (qb, jnp.arange(nb)))
    return o.transpose(1, 0, 2, 3, 4).reshape(B, S, DIFF_HEADS, 2 * DIFF_HD)


def _diff_attention_sample(q, k_all, v_all, lam):
    s = jnp.einsum('bqhcd,bkhcd->bhcqk', q, k_all).astype(jnp.float32) * DIFF_HD ** -0.5
    return _diff_weighted_values(s, lam, v_all.astype(jnp.float32))


def _diff_head_norm(o, subln_g, lam_init):
    B, S = o.shape[:2]
    ms = jnp.mean(jnp.square(o), axis=-1, keepdims=True)
    o = o * lax.rsqrt(ms + LN_EPS) * subln_g.astype(jnp.float32) * (1.0 - lam_init)
    return o.reshape(B, S, MIX_W)


def _swiglu(x, w_gate, w_up, w_down):
    return (jax.nn.silu(x @ w_gate) * (x @ w_up)) @ w_down


def setup_inputs(seed: int = 0) -> dict:
    key = jax.random.key(seed)
    ks = jax.random.split(key, 32)
    f = jnp.float32

    def nrm(k, shape, scale):
        return jax.random.normal(k, shape, f) * scale

    d_is = D_MODEL ** -0.5
    ret_cols = 4 * MIX_W + MEM_W
    ret_col_scale = jnp.ones((ret_cols,), f).at[2 * MIX_W:3 * MIX_W].set(BETA)
    diff_cols = 3 * MIX_W + MEM_W
    diff_col_scale = jnp.ones((diff_cols,), f).at[2 * MIX_W:3 * MIX_W].set(BETA)
    mem_col_scale = jnp.ones((2 * MEM_W,), f).at[MEM_W:].set(BETA)
    return {
        "x_prompt": nrm(ks[0], (BATCH, SEQ, D_MODEL), 1.0),
        "x_sample": nrm(ks[1], (DEC_BATCH, DEC_SEQ, D_MODEL), 1.0),
        "mem_prompt": nrm(ks[2], (BATCH, N_MEM, D_MODEL), 1.0),
        "cache_ret_state": nrm(ks[3], (N_RET, DEC_BATCH, RET_HEADS, RET_HD, RET_HD), 0.5),
        "cache_diff_k": nrm(ks[4], (N_DIFF, DEC_BATCH, PAST_LEN, DIFF_HEADS, 2 * DIFF_HD), 1.0),
        "cache_diff_v": nrm(ks[5], (N_DIFF, DEC_BATCH, PAST_LEN, DIFF_HEADS, 2 * DIFF_HD), BETA),
        "cache_mem_k": nrm(ks[6], (DEPTH, DEC_BATCH, N_MEM, MEM_HEADS, MEM_HD), 1.0),
        "cache_mem_v": nrm(ks[7], (DEPTH, DEC_BATCH, N_MEM, MEM_HEADS, MEM_HD), BETA),
        "ret_w_in": nrm(ks[8], (N_RET, D_MODEL, ret_cols), d_is) * ret_col_scale,
        "ret_gn_g": 1.0 + nrm(ks[9], (N_RET, MIX_W), 0.02),
        "diff_w_in": nrm(ks[10], (N_DIFF, D_MODEL, diff_cols), d_is) * diff_col_scale,
        "diff_lambda_q1": nrm(ks[11], (N_DIFF, DIFF_HD), 0.1),
        "diff_lambda_k1": nrm(ks[12], (N_DIFF, DIFF_HD), 0.1),
        "diff_lambda_q2": nrm(ks[13], (N_DIFF, DIFF_HD), 0.1),
        "diff_lambda_k2": nrm(ks[14], (N_DIFF, DIFF_HD), 0.1),
        "diff_subln_g": 1.0 + nrm(ks[15], (N_DIFF, 2 * DIFF_HD), 0.02),
        "w_mem_kv": nrm(ks[16], (DEPTH, D_MODEL, 2 * MEM_W), d_is) * mem_col_scale,
        "w_o": nrm(ks[17], (DEPTH, D_MODEL, D_MODEL), d_is * BETA),
        "ln1_g": 1.0 + nrm(ks[18], (DEPTH, D_MODEL), 0.02),
        "ln1_b": nrm(ks[19], (DEPTH, D_MODEL), 0.02),
        "w_gate": nrm(ks[20], (DEPTH, D_MODEL, D_FF), d_is),
        "w_up": nrm(ks[21], (DEPTH, D_MODEL, D_FF), d_is),
        "w_down": nrm(ks[22], (DEPTH, D_FF, D_MODEL), D_FF ** -0.5 * BETA),
        "ln2_g": 1.0 + nrm(ks[23], (DEPTH, D_MODEL), 0.02),
        "ln2_b": nrm(ks[24], (DEPTH, D_MODEL), 0.02),
    }


def reference(x_prompt, x_sample, mem_prompt, cache_ret_state, cache_diff_k, cache_diff_v,
              cache_mem_k, cache_mem_v, ret_w_in, ret_gn_g, diff_w_in, diff_lambda_q1,
              diff_lambda_k1, diff_lambda_q2, diff_lambda_k2, diff_subln_g, w_mem_kv, w_o,
              ln1_g, ln1_b, w_gate, w_up, w_down, ln2_g, ln2_b):
    Bp, Sp, _ = x_prompt.shape
    Bs, Ss, _ = x_sample.shape
    pos_p = jnp.arange(Sp)
    pos_s = PAST_LEN + jnp.arange(Ss)
    xp, xs = x_prompt, x_sample
    ret_p, ret_s, dkp, dvp, dks, dvs, mkp, mvp = [], [], [], [], [], [], [], []
    for i in range(DEPTH):
        j = i // 2
        mk_p, mv_p = jnp.split(mem_prompt @ w_mem_kv[i], 2, axis=-1)
        mk_p = mk_p.reshape(Bp, N_MEM, MEM_HEADS, MEM_HD)
        mv_p = mv_p.reshape(Bp, N_MEM, MEM_HEADS, MEM_HD)
        mkp.append(mk_p)
        mvp.append(mv_p)
        mk_s, mv_s = cache_mem_k[i], cache_mem_v[i]
        if i % 2 == 0:
            r0 = jnp.zeros((Bp, RET_HEADS, RET_HD, RET_HD), jnp.float32)
            hp, rp = _retention_mixer(xp, pos_p, r0, mk_p, mv_p, ret_w_in[j], ret_gn_g[j])
            hs, rs = _retention_mixer(xs, pos_s, cache_ret_state[j], mk_s, mv_s, ret_w_in[j], ret_gn_g[j])
            ret_p.append(rp.astype(xp.dtype))
            ret_s.append(rs.astype(xs.dtype))
        else:
            lam_init = _lambda_init(i)
            lam = _diff_lambda(diff_lambda_q1[j], diff_lambda_k1[j], diff_lambda_q2[j], diff_lambda_k2[j], lam_init)
            q, k, v, mq = _diff_project(xp, pos_p, diff_w_in[j])
            o = _diff_attention_prompt(q, k, v, lam)
            hp = jnp.concatenate([_diff_head_norm(o, diff_subln_g[j], lam_init),
                                  _memory_attention(mq, mk_p, mv_p)], axis=-1).astype(xp.dtype)
            dkp.append(k.reshape(Bp, Sp, DIFF_HEADS, 2 * DIFF_HD))
            dvp.append(v)
            q, k, v, mq = _diff_project(xs, pos_s, diff_w_in[j])
            k_all = jnp.concatenate(
                [cache_diff_k[j].reshape(Bs, PAST_LEN, DIFF_HEADS, 2, DIFF_HD).astype(k.dtype), k], axis=1)
            v_all = jnp.concatenate([cache_diff_v[j].astype(v.dtype), v], axis=1)
            o = _diff_attention_sample(q, k_all, v_all, lam)
            hs = jnp.concatenate([_diff_head_norm(o, diff_subln_g[j], lam_init),
                                  _memory_attention(mq, mk_s, mv_s)], axis=-1).astype(xs.dtype)
            dks.append(k.reshape(Bs, Ss, DIFF_HEADS, 2 * DIFF_HD))
            dvs.append(v)
        xp = _post_norm(xp, hp @ w_o[i], ln1_g[i], ln1_b[i])
        xp = _post_norm(xp, _swiglu(xp, w_gate[i], w_up[i], w_down[i]), ln2_g[i], ln2_b[i])
        xs = _post_norm(xs, hs @ w_o[i], ln1_g[i], ln1_b[i])
        xs = _post_norm(xs, _swiglu(xs, w_gate[i], w_up[i], w_down[i]), ln2_g[i], ln2_b[i])
    return (xp, xs, jnp.stack(ret_p), jnp.stack(ret_s), jnp.stack(dkp), jnp.stack(dvp),
            jnp.stack(dks), jnp.stack(dvs), jnp.stack(mkp), jnp.stack(mvp))
```

```python
import math
import numpy as np
import concourse.bass as bass
import concourse.mybir as mybir
from concourse.bass_utils import run_bass_kernel_spmd
from contextlib import ExitStack

F32 = mybir.dt.float32
BF16 = mybir.dt.bfloat16
AF = mybir.ActivationFunctionType
ALU = mybir.AluOpType
AX = mybir.AxisListType

PE, ACT, DVE, POOL, SP = "pe", "act", "dve", "pool", "sp"
PDEPTH = 3

D = 1024
KC = 8
SEQ = 4096
TT = 512
MIX = 768
H = 6
DFF = 2816
FC = 22
NMEM = 256
PAST = 1024
ALPHA = 4.0 ** 0.25
LN_EPS = 1e-5
LAM_INIT = 0.8 - 0.6 * math.exp(-0.3 * 1)
RET_COLS = 3328
DIFF_COLS = 2560
NROPE = 224


class Buf:
    __slots__ = ("name", "w", "r", "ov", "excl")

    def __init__(self, name, excl=False):
        self.name = name
        self.w = None
        self.r = {}
        self.ov = []
        self.excl = excl


def alias(a_list, b_list):
    for a in a_list:
        for b in b_list:
            a.ov.append(b)
            b.ov.append(a)


class Op:
    __slots__ = ("eng", "fn", "deps", "dma", "ev", "idx")


class Prog:
    def __init__(self, nc, es):
        self.nc = nc
        self.es = es
        self.ops = []
        self.engines = {PE: nc.tensor, ACT: nc.scalar, DVE: nc.vector, POOL: nc.gpsimd, SP: nc.sync}
        self.esem = {}
        for e in (PE, ACT, DVE, POOL):
            self.esem[e] = es.enter_context(nc.semaphore("ev_" + e))
        self.stores = []
        self.nsem = 0

    def new_dma_sem(self, name):
        self.nsem += 1
        return self.es.enter_context(self.nc.semaphore(f"dq_{name}_{self.nsem}"))

    def op(self, eng, fn, reads=(), writes=(), dma_sem=None, ndma=1, is_store=False):
        idx = len(self.ops)
        o = Op()
        o.eng = eng
        o.fn = fn
        o.idx = idx
        o.dma = (dma_sem, ndma) if dma_sem is not None else None
        deps = set()
        is_dma = dma_sem is not None

        def add(j, kind):
            if j is None:
                return
            d = self.ops[j]
            if (not is_dma) and d.dma is None and d.eng == eng:
                if eng == PE:
                    return
            deps.add(j)

        for b in reads:
            add(b.w, "raw")
            if b.excl:
                for k_, j in b.r.items():
                    if k_ != eng:
                        deps.add(j)
        for b in writes:
            for bb in [b] + b.ov:
                add(bb.w, "waw")
                for j in bb.r.values():
                    add(j, "war")
        rkey = ("dma", idx) if is_dma else eng
        for b in reads:
            b.r[rkey] = idx
        for b in writes:
            b.w = idx
            b.r = {}
        o.deps = deps
        o.ev = None
        self.ops.append(o)
        if is_store:
            self.stores.append(idx)
        return idx

    def emit(self):
        import os
        if os.environ.get("KTRUNC"):
            n = int(os.environ["KTRUNC"])
            self.ops = self.ops[:n]
            self.stores = [j for j in self.stores if j < n]
        needed = set()
        for o in self.ops:
            needed |= o.deps
        ecount = {e: 0 for e in self.esem}
        dcount = {}
        waited = {e: {} for e in self.engines}
        nwait = 0
        for o in self.ops:
            eng = self.engines[o.eng]
            w = {}
            for j in o.deps:
                sem, val = self.ops[j].ev
                k = id(sem)
                if k not in w or w[k][1] < val:
                    w[k] = (sem, val)
            wd = waited[o.eng]
            for k, (sem, val) in w.items():
                if wd.get(k, 0) >= val:
                    continue
                eng.wait_ge(sem, val)
                nwait += 1
                wd[k] = val
            if o.dma is not None:
                sem, n = o.dma
                o.fn(sem)
                c = dcount.get(id(sem), 0) + n
                dcount[id(sem)] = c
                o.ev = (sem, 16 * c)
            else:
                ins = o.fn()
                if o.idx in needed:
                    sem = self.esem[o.eng]
                    ins.then_inc(sem, 1)
                    ecount[o.eng] += 1
                    o.ev = (sem, ecount[o.eng])
        sp = self.engines[SP]
        w = {}
        for j in self.stores:
            sem, val = self.ops[j].ev
            k = id(sem)
            if k not in w or w[k][1] < val:
                w[k] = (sem, val)
        for k, (sem, val) in w.items():
            sp.wait_ge(sem, val)
        return dict(n_ops=len(self.ops), n_wait=nwait, ecount=ecount)


def _host_consts():
    c = {}
    c["ident"] = np.eye(128, dtype=np.float32)

    def rope_tab(pos):
        pos = pos.astype(np.float32)
        inv_r = np.exp(np.float32(-math.log(10000.0)) * np.arange(64, dtype=np.float32) * np.float32(2.0 / 128)).astype(np.float32)
        ang_r = (pos[:, None] * inv_r[None, :]).astype(np.float32).astype(np.float64)
        inv_d = np.exp(np.float32(-math.log(500000.0)) * np.arange(8, dtype=np.float32) * np.float32(2.0 / 16)).astype(np.float32)
        ang_d = (pos[:, None] * inv_d[None, :]).astype(np.float32).astype(np.float64)
        t = np.concatenate([np.cos(ang_r), -np.sin(ang_r), np.sin(ang_r),
                            np.cos(ang_d), np.cos(ang_d), -np.sin(ang_d), np.sin(ang_d)], axis=1)
        return np.ascontiguousarray(t.astype(np.float32))

    c["rope_p"] = rope_tab(np.arange(SEQ))
    ps = PAST + np.arange(64)
    c["rope_s"] = rope_tab(np.concatenate([ps, ps]))

    log_g = np.log(1.0 - np.exp2(-5.0 - np.arange(H, dtype=np.float64)))
    s = 128.0 ** -0.5
    i = np.arange(128)
    same = (i[:, None] // 64) == (i[None, :] // 64)
    Dp = np.zeros((H, 128, 128))
    for h in range(H):
        dd = np.exp(np.abs(i[:, None] - i[None, :]) * log_g[h])
        causal = np.where(i[:, None] >= i[None, :], dd, 0.0)
        Dp[h] = np.where(same, dd, causal)
    c["DT_p"] = np.ascontiguousarray((Dp.transpose(2, 0, 1) * s).reshape(128, H * 128).astype(np.float32))
    xi_p = np.exp((i[None, :] + 1.0) * log_g[:, None])
    c["xi_p"] = np.ascontiguousarray(xi_p.reshape(1, H * 128).astype(np.float32))
    zeta_p = np.exp((127.0 - i)[:, None] * log_g[None, :]) * s
    c["zeta_p"] = np.ascontiguousarray(np.repeat(zeta_p, 128, axis=1).astype(np.float32))
    c["dec_p"] = [float(np.exp(128.0 * log_g[h])) for h in range(H)]
    Ds = np.zeros((H, 128, 128))
    for h in range(H):
        dd = np.exp(np.abs(i[:, None] - i[None, :]) * log_g[h])
        Ds[h] = np.where(same, dd, 0.0)
    c["DT_s"] = np.ascontiguousarray((Ds.transpose(2, 0, 1) * s).reshape(128, H * 128).astype(np.float32))
    ii = i % 64
    xi_s = np.exp((ii[None, :] + 1.0) * log_g[:, None])
    xiA = np.where(i[None, :] < 64, xi_s, 0.0)
    xiB = np.where(i[None, :] >= 64, xi_s, 0.0)
    c["xi_sA"] = np.ascontiguousarray(xiA.reshape(1, H * 128).astype(np.float32))
    c["xi_sB"] = np.ascontiguousarray(xiB.reshape(1, H * 128).astype(np.float32))
    zeta_s = np.exp((63.0 - ii)[:, None] * log_g[None, :]) * s
    zA = np.where(i[:, None] < 64, zeta_s, 0.0)
    zB = np.where(i[:, None] >= 64, zeta_s, 0.0)
    c["zeta_sA"] = np.ascontiguousarray(np.repeat(zA, 128, axis=1).astype(np.float32))
    c["zeta_sB"] = np.ascontiguousarray(np.repeat(zB, 128, axis=1).astype(np.float32))
    c["dec_s"] = [float(np.exp(64.0 * log_g[h])) for h in range(H)]
    return c


CONST_SHAPES = {
    "ident": [128, 128], "rope_p": [SEQ, NROPE], "rope_s": [128, NROPE],
    "DT_p": [128, 768], "xi_p": [1, 768], "zeta_p": [128, 768],
    "DT_s": [128, 768], "xi_sA": [1, 768], "xi_sB": [1, 768], "zeta_sA": [128, 768], "zeta_sB": [128, 768],
}

IN_SHAPES = {
    "xp": [SEQ, D], "xs": [128, D], "memp": [NMEM, D],
    "c_ret": [2, H, 128, 128], "c_dk": [2, PAST, MIX], "c_dv": [2, PAST, MIX],
    "c_mk": [2, 2, NMEM, 256], "c_mv": [2, 2, NMEM, 256],
    "ret_w_in": [D, RET_COLS], "ret_gn_g": [1, MIX], "diff_w_in": [D, DIFF_COLS],
    "lamv": [4, 64], "subln": [128, 1],
    "w_mem_kv": [2, D, 512], "w_o": [2, D, D],
    "ln1_g": [2, D], "ln1_b": [2, D], "w_gate": [2, D, DFF], "w_up": [2, D, DFF], "w_down": [2, DFF, D],
    "ln2_g": [2, D], "ln2_b": [2, D],
}

OUT_SHAPES = {
    "y_p": [SEQ, D], "y_s": [128, D], "rs_p": [H, 128, 128], "rs_s": [2, H, 128, 128],
    "dk_p": [SEQ, MIX], "dv_p": [SEQ, MIX], "dk_s": [128, MIX], "dv_s": [128, MIX],
    "mk_p": [2, NMEM, 256], "mv_p": [2, NMEM, 256],
}


def build_program(n_ptiles=8, do_sample=True, n_layers=2, dbg=None):
    nc = bass.Bass("TRN2", target_bir_lowering=False)
    hc = _host_consts()
    I = {}
    for n, s in list(IN_SHAPES.items()) + list(CONST_SHAPES.items()):
        I[n] = nc.dram_tensor(n, s, F32, kind="ExternalInput").ap()
    O = {}
    for n, s in OUT_SHAPES.items():
        O[n] = nc.dram_tensor(n, s, F32, kind="ExternalOutput").ap()
    KTd = nc.dram_tensor("KTd", [H, 128, SEQ], BF16, kind="Internal").ap()
    Vd = nc.dram_tensor("Vd", [H, 128, 32, 128], BF16, kind="Internal").ap()
    NWBLK = 50
    Wd = nc.dram_tensor("Wd", [NWBLK, 128, 5632], BF16, kind="Internal").ap()
    Wd_B = [Buf(f"Wd{i}") for i in range(NWBLK)]
    KTd_B = [Buf(f"KTd{h}") for h in range(H)]
    Vd_B = [Buf(f"Vd{h}") for h in range(H)]
    DBG = {}
    if dbg:
        for n, s in dbg.items():
            DBG[n] = nc.dram_tensor(n, s, F32, kind="ExternalOutput").ap()

    es = ExitStack()
    with es:
        P = Prog(nc, es)

        def sb(name, shape, dt):
            return es.enter_context(nc.sbuf_tensor("sb_" + name, shape, dt))

        NKROT = 4
        xres = sb("xres", [128, 4, D], F32)
        xres_B = [[Buf(f"xres{s}_{n}") for n in range(4)] for s in range(4)]
        xbf = sb("xbf", [128, 2, D], BF16)
        xbf_B = [Buf("xbf0"), Buf("xbf1")]
        actT = sb("actT", [128, KC, TT], BF16)
        actT_B = [[Buf(f"actT{c}_{s}") for s in range(4)] for c in range(KC)]
        NW = 3
        wslot = [sb(f"wslot{i}", [128, 5632], BF16) for i in range(NW)]
        wslot_B = [Buf(f"wslot{i}") for i in range(NW)]
        wsem = [P.new_dma_sem(f"w{i}") for i in range(NW)]
        arenaA = sb("arenaA", [128, FC * TT], BF16)
        hff = arenaA[:, :].rearrange("p (f t) -> p f t", t=TT)
        hff_B = [Buf(f"hff{f}") for f in range(FC)]
        qT = arenaA[:, 0:3072].rearrange("p (h t) -> p h t", t=TT)
        qxT = arenaA[:, 3072:6144].rearrange("p (h t) -> p h t", t=TT)
        kT = arenaA[:, 6144:9216].rearrange("p (h t) -> p h t", t=TT)
        mqT = arenaA[:, 9216:10240].rearrange("p (m t) -> p m t", t=TT)
        qT_B = [[Buf(f"qT{h}_{s}") for s in range(4)] for h in range(H)]
        qxT_B = [[Buf(f"qxT{h}_{s}") for s in range(4)] for h in range(H)]
        kT_B = [[Buf(f"kT{h}_{s}") for s in range(4)] for h in range(H)]
        mqT_B = [Buf("mqT0"), Buf("mqT1")]
        krot_B = [Buf(f"krot{i}") for i in range(NKROT)]
        for h in range(H):
            alias(qT_B[h], [hff_B[h]])
            alias(qxT_B[h], [hff_B[6 + h]])
            alias(kT_B[h], [hff_B[12 + h]])
        alias([mqT_B[0]], [hff_B[18]])
        alias([mqT_B[1]], [hff_B[19]])
        arenaB = sb("arenaB", [128, 6144], BF16)
        kz = arenaB[:, 0:3072].rearrange("p (s c) -> p s c", s=4)
        sg = arenaB[:, 3072:6144].rearrange("p (s c) -> p s c", s=4)
        kz_B = [Buf(f"kz{s}") for s in range(4)]
        sg_B = [Buf(f"sg{s}") for s in range(4)]
        NKV = 3
        kvK = [arenaB[:, 2048 * i:2048 * i + 1024] for i in range(NKV)]
        kvV = [arenaB[:, 2048 * i + 1024:2048 * i + 2048].rearrange("p (k e) -> p k e", e=128) for i in range(NKV)]
        kv_B = [Buf(f"kv{i}") for i in range(NKV)]
        kvsem = [P.new_dma_sem(f"kv{i}") for i in range(NKV)]
        alias(kv_B, kz_B + sg_B)
        vtok = sb("vtok", [128, 4, MIX], BF16)
        vtok_B = [Buf(f"vtok{s}") for s in range(4)]
        hm = sb("hm", [128, 2, MIX], BF16)
        hm_B = [Buf("hm0"), Buf("hm1")]
        NT32 = 4
        t32 = sb("t32", [128, NT32, 384], F32)
        t32_B = [Buf(f"t32_{i}") for i in range(NT32)]
        t32sem = [P.new_dma_sem(f"t32_{i}") for i in range(NT32)]
        t32flat = t32[:, :, :].rearrange("p a c -> p (a c)")
        accA = t32flat[:, 0:512]
        accB = t32flat[:, 512:1024]
        accA_B, accB_B = Buf("accA"), Buf("accB")
        alias([accA_B], [t32_B[0], t32_B[1]])
        alias([accB_B], [t32_B[1], t32_B[2]])
        ropeA2 = sb("ropeA", [128, 2, 384], F32)
        ropeB2 = sb("ropeB", [128, 2, 384], F32)
        ropeA2_B = [Buf("ropeA0"), Buf("ropeA1")]
        ropeB2_B = [Buf("ropeB0"), Buf("ropeB1")]
        krot4 = sb("krot4", [128, NKROT, 384], BF16)
        gnt = sb("gnt", [128, 2, 384], F32)
        gnt_B = [Buf("gnt0"), Buf("gnt1")]
        xbfn = sb("xbfn", [128, 4, D], BF16)
        xbfn_B = [Buf(f"xbfn{s}") for s in range(4)]
        xbfnsem = P.new_dma_sem("xbfn")
        sgate = sb("sgate", [128, 2, TT], BF16)
        sgate_B = [Buf("sgate0"), Buf("sgate1")]
        Pb = sb("Pb", [128, 6, TT], BF16)
        Pb_B = [Buf(f"Pb{i}") for i in range(6)]
        MTsb = sb("MTsb", [128, 2, 384], BF16)
        MTsb_B = [Buf("MT0"), Buf("MT1")]
        att_r = sb("att_r", [128, 2, TT], F32)
        att_r_B = [Buf("att_r0"), Buf("att_r1")]
        att_o = sb("att_o", [128, 2, TT], F32)
        att_o_B = [Buf("att_o0"), Buf("att_o1")]
        lnp = sb("lnp", [128, 2, D], F32)
        lnp_B = Buf("lnp")
        lnsem = P.new_dma_sem("lnp")
        ropet = sb("ropet", [128, 2, 4, NROPE], F32)
        ropet_B = [Buf("ropet0"), Buf("ropet1")]
        ropesem = [P.new_dma_sem("rope0"), P.new_dma_sem("rope1")]
        xsem = [P.new_dma_sem(f"x{s}") for s in range(4)]
        ysem = [P.new_dma_sem(f"y{s}") for s in range(4)]
        krot = krot4
        ident = sb("ident", [128, 128], BF16)
        ones_bf = sb("ones_bf", [128, 128], BF16)
        ones_f = sb("ones_f", [128, 128], F32)
        epsc = sb("epsc", [128, 1], F32)
        cst_B = Buf("consts")
        DT = sb("DT", [128, 768], F32)
        xi = sb("xi", [128, 2, 768], F32)
        zeta = sb("zeta", [128, 2, 768], F32)
        rett_B = Buf("ret_tables")
        gng = sb("gng", [128, 768], F32)
        R32 = sb("R32", [128, 2, 768], F32)
        Rbf = sb("Rbf", [128, 2, 768], BF16)
        R_B = [[Buf(f"R{g}_{h}") for h in range(H)] for g in range(2)]
        Rbf_B = [[Buf(f"Rbf{g}_{h}") for h in range(H)] for g in range(2)]
        memKT = sb("memKT", [128, 2, 4, NMEM], BF16)
        memV = sb("memV", [128, 2, 2, 4, 128], BF16)
        mem_B = [Buf("mem0"), Buf("mem1")]
        memT = sb("memT", [128, KC, NMEM], BF16)
        memT_B = Buf("memT")
        memst = sb("memst", [128, 512], F32)
        memst_B = Buf("memst")
        onespad = sb("onespad", [128, 2, 128], BF16)
        lam = sb("lam", [128, 8], F32)
        lam_B = Buf("lam")
        lamt = sb("lamt", [128, 4, 64], F32)
        sublg = sb("sublg", [128, 2], F32)
        stat = sb("stat", [128, 4, 8, 6], F32)
        stat_B = [Buf(f"stat{s}") for s in range(4)]
        aggr = sb("aggr", [128, 4, 8, 4], F32)
        aggr_B = [Buf(f"aggr{s}") for s in range(4)]
        _bs = {}

        def bsem(name):
            if name not in _bs:
                _bs[name] = P.new_dma_sem(name)
            return _bs[name]
        kvwsem = P.new_dma_sem("kvw")

        ps = [es.enter_context(nc.psum_tensor(f"ps{i}", [128, 512], F32)) for i in range(8)]
        ps_B = [Buf(f"ps{i}", excl=True) for i in range(8)]

        def psbf(i):
            return ps[i][:, :].bitcast(BF16)

        eng = P.engines

        def dma(q, out, in_, reads, writes, sem, is_store=False):
            e = eng[q]
            P.op(q, lambda s: e.dma_start(out=out, in_=in_).then_inc(s, 16), reads=reads, writes=writes,
                 dma_sem=sem, ndma=1, is_store=is_store)

        def act(out, in_, func, reads, writes, scale=1.0, bias=None):
            if bias is None:
                P.op(ACT, lambda: nc.scalar.activation(out=out, in_=in_, func=func, scale=scale), reads=reads, writes=writes)
            else:
                P.op(ACT, lambda: nc.scalar.activation(out=out, in_=in_, func=func, scale=scale, bias=bias), reads=reads, writes=writes)

        def tt(e, out, in0, in1, op, reads, writes):
            en = eng[e]
            P.op(e, lambda: en.tensor_tensor(out=out, in0=in0, in1=in1, op=op), reads=reads, writes=writes)

        def stt(e, out, in0, scalar, in1, op0, op1, reads, writes):
            en = eng[e]
            P.op(e, lambda: en.scalar_tensor_tensor(out=out, in0=in0, scalar=scalar, in1=in1, op0=op0, op1=op1),
                 reads=reads, writes=writes)

        def cp(e, out, in_, reads, writes):
            if e == ACT:
                act(out, in_, AF.Copy, reads, writes)
            else:
                en = eng[e]
                P.op(e, lambda: en.tensor_copy(out=out, in_=in_), reads=reads, writes=writes)

        def mms(out, pairs, reads, writes):
            n = len(pairs)

            def f():
                for i, (l, r) in enumerate(pairs):
                    ins = nc.tensor.matmul(out, l, r, start=(i == 0), stop=(i == n - 1))
                return ins
            P.op(PE, f, reads=reads, writes=writes)

        def transposes(bank, items, reads):
            pv = psbf(bank)

            def f():
                for (co, a) in items:
                    ins = nc.tensor.transpose(out=pv[:, co:co + 128], in_=a, identity=ident[:, :])
                return ins
            P.op(PE, f, reads=reads + [cst_B], writes=[ps_B[bank]])

        rr = {"tp": 0, "mm": 0}

        def tp_bank():
            rr["tp"] ^= 1
            return rr["tp"]

        def mm_bank():
            rr["mm"] = (rr["mm"] + 1) % 3
            return 2 + rr["mm"]

        wq = []

        def wblock(name, loads, nelem):
            wq.append((name, loads, nelem))

        wstate = {"issued": 0, "next": 0}

        wosem = [P.new_dma_sem(f"wo{i}") for i in range(NW)]

        def w_issue_upto(n):
            while wstate["issued"] < min(n, len(wq)):
                i = wstate["issued"]
                name, loads, nelem = wq[i]
                si = i % NW
                bi = i % NWBLK
                npass = len(wq) // NWBLK
                two = npass >= 3
                from_f32 = (i < NWBLK) or (two and i < 2 * NWBLK and bi % 2 == 1)
                publish = (npass > 1) and ((i < NWBLK and not (two and bi % 2 == 1)) or (two and NWBLK <= i < 2 * NWBLK and bi % 2 == 1))
                if from_f32:
                    ld = loads(wslot[si])

                    def f(s, ld=ld):
                        for (o, a) in ld:
                            nc.gpsimd.dma_start(out=o, in_=a).then_inc(s, 16)
                    P.op(POOL, f, writes=[wslot_B[si]], dma_sem=wsem[si], ndma=len(ld))
                    if publish:
                        dma(SP, Wd[bi, :, 0:nelem], wslot[si][:, 0:nelem], [wslot_B[si]], [Wd_B[bi]], wosem[si])
                else:
                    def f(s, si=si, bi=bi, nelem=nelem):
                        nc.gpsimd.dma_start(out=wslot[si][:, 0:nelem], in_=Wd[bi, :, 0:nelem]).then_inc(s, 16)
                    P.op(POOL, f, reads=[Wd_B[bi]], writes=[wslot_B[si]], dma_sem=wsem[si], ndma=1)
                wstate["issued"] += 1

        def w_next(name):
            i = wstate["next"]
            assert wq[i][0] == name, (wq[i][0], name)
            assert n_layers == 1 or len(wq) % NWBLK == 0
            w_issue_upto(i + NW)
            wstate["next"] += 1
            return i % NW

        def kcview(w2d, c0, ncols):
            return w2d.rearrange("(k p) c -> p k c", p=128)[:, :, c0:c0 + ncols]

        def plan_weights(tiles):
            for tl in tiles:
                for L in range(n_layers):
                    if L == 0:
                        w = I["ret_w_in"]
                        for j in range(8):
                            wblock(f"in{L}_{j}", (lambda j=j, w=w: (lambda sl: [(sl[:, 0:8 * 384].rearrange("p (k c) -> p k c", k=8), kcview(w, 384 * j, 384))]))(), 8 * 384)
                        wblock(f"mq{L}", (lambda w=w: (lambda sl: [(sl[:, 0:8 * 256].rearrange("p (k c) -> p k c", k=8), kcview(w, 3072, 256))]))(), 8 * 256)
                    else:
                        w = I["diff_w_in"]
                        for j in range(6):
                            wblock(f"in{L}_{j}", (lambda j=j, w=w: (lambda sl: [(sl[:, 0:8 * 384].rearrange("p (k c) -> p k c", k=8), kcview(w, 384 * j, 384))]))(), 8 * 384)
                        wblock(f"mq{L}", (lambda w=w: (lambda sl: [(sl[:, 0:8 * 256].rearrange("p (k c) -> p k c", k=8), kcview(w, 2304, 256))]))(), 8 * 256)
                    for n in range(2):
                        wblock(f"wo{L}_{n}", (lambda n=n, L=L: (lambda sl: [(sl[:, 0:8 * 512].rearrange("p (k c) -> p k c", k=8), kcview(I["w_o"][L], 512 * n, 512))]))(), 8 * 512)
                    for b in range(11):
                        wblock(f"gu{L}_{b}", (lambda b=b, L=L: (lambda sl: [
                            (sl[:, 0:2048].rearrange("p (k c) -> p k c", k=8), kcview(I["w_gate"][L], 256 * b, 256)),
                            (sl[:, 2048:4096].rearrange("p (k c) -> p k c", k=8), kcview(I["w_up"][L], 256 * b, 256))]))(), 4096)
                    for n in range(4):
                        wblock(f"dn{L}_{n}", (lambda n=n, L=L: (lambda sl: [(sl[:, 0:FC * 256].rearrange("p (k c) -> p k c", k=FC), kcview(I["w_down"][L], 256 * n, 256))]))(), FC * 256)

        def startup():
            dma(POOL, ident[:, :], I["ident"], [], [cst_B], bsem("ident"))
            P.op(DVE, lambda: nc.vector.memset(ones_bf[:, :], 1.0), writes=[cst_B])
            P.op(DVE, lambda: nc.vector.memset(ones_f[:, :], 1.0), writes=[cst_B])
            P.op(DVE, lambda: nc.vector.memset(epsc[:, :], LN_EPS), writes=[cst_B])
            P.op(DVE, lambda: nc.vector.memset(onespad[:, :, :], 0.0), writes=[cst_B])
            P.op(DVE, lambda: nc.vector.memset(onespad[:, 0, 0:64], 1.0), writes=[cst_B])
            P.op(DVE, lambda: nc.vector.memset(onespad[:, 1, 64:128], 1.0), writes=[cst_B])
            P.op(DVE, lambda: nc.vector.memset(memKT[:, :, :, :], 0.0), writes=mem_B)
            P.op(DVE, lambda: nc.vector.memset(memV[:, :, :, :, :], 0.0), writes=mem_B)
            dma(SP, gng[:, :], I["ret_gn_g"][0].partition_broadcast(128), [], [cst_B], bsem("cst"))
            dma(SP, lamt[:, :, :], I["lamv"].rearrange("a d -> (a d)").partition_broadcast(128).rearrange("p (a d) -> p a d", a=4), [], [lam_B], bsem("lam"))
            dma(SP, sublg[:, 0:1], I["subln"], [], [lam_B], bsem("lam"))
            P.op(DVE, lambda: nc.vector.memset(lam[:, :], 0.0), writes=[lam_B])
            tt(DVE, lamt[:, 0, :], lamt[:, 0, :], lamt[:, 1, :], ALU.mult, [lam_B], [lam_B])
            tt(DVE, lamt[:, 2, :], lamt[:, 2, :], lamt[:, 3, :], ALU.mult, [lam_B], [lam_B])
            P.op(DVE, lambda: nc.vector.tensor_reduce(out=lam[:, 0:1], in_=lamt[:, 0, :], axis=AX.X, op=ALU.add), reads=[lam_B], writes=[lam_B])
            P.op(DVE, lambda: nc.vector.tensor_reduce(out=lam[:, 1:2], in_=lamt[:, 2, :], axis=AX.X, op=ALU.add), reads=[lam_B], writes=[lam_B])
            act(lam[:, 2:4], lam[:, 0:2], AF.Exp, [lam_B], [lam_B])
            tt(DVE, lam[:, 4:5], lam[:, 2:3], lam[:, 3:4], ALU.subtract, [lam_B], [lam_B])
            P.op(DVE, lambda: nc.vector.tensor_scalar(out=lam[:, 5:6], in0=lam[:, 4:5], scalar1=LAM_INIT, scalar2=-1.0, op0=ALU.add, op1=ALU.mult), reads=[lam_B], writes=[lam_B])
            P.op(DVE, lambda: nc.vector.tensor_scalar(out=sublg[:, 1:2], in0=sublg[:, 0:1], scalar1=(1.0 - LAM_INIT), scalar2=None, op0=ALU.mult), reads=[lam_B], writes=[lam_B])

        def load_ret_tables(kind):
            if kind == "p":
                dma(SP, DT[:, :], I["DT_p"], [], [rett_B], bsem("rett"))
                dma(SP, xi[:, 0, :], I["xi_p"][0].partition_broadcast(128), [], [rett_B], bsem("rett"))
                dma(SP, zeta[:, 0, :], I["zeta_p"], [], [rett_B], bsem("rett"))
            else:
                dma(SP, DT[:, :], I["DT_s"], [], [rett_B], bsem("rett"))
                dma(SP, xi[:, 0, :], I["xi_sA"][0].partition_broadcast(128), [], [rett_B], bsem("rett"))
                dma(SP, xi[:, 1, :], I["xi_sB"][0].partition_broadcast(128), [], [rett_B], bsem("rett"))
                dma(SP, zeta[:, 0, :], I["zeta_sA"], [], [rett_B], bsem("rett"))
                dma(SP, zeta[:, 1, :], I["zeta_sB"], [], [rett_B], bsem("rett"))

        def to_featmajor(src_ap, src_bufs, nchunk, dst_fn, dst_bufs_fn, cast_eng=DVE, evac_eng=ACT):
            xi_ = rr.setdefault("xbf", 0)
            rr["xbf"] = xi_ ^ 1
            xb = xbf[:, xi_, 0:128 * nchunk]
            cp(cast_eng, xb, src_ap, src_bufs, [xbf_B[xi_]])
            bank = tp_bank()
            transposes(bank, [(128 * c, xbf[:, xi_, 128 * c:128 * (c + 1)]) for c in range(nchunk)], [xbf_B[xi_]])
            pv = psbf(bank)[:, 0:128 * nchunk].rearrange("p (c t) -> p c t", c=nchunk)
            cp(evac_eng, dst_fn(), pv, [ps_B[bank]], dst_bufs_fn())

        def mem_setup_prompt():
            for mt in range(2):
                dma(SP, xres[:, mt, :], I["memp"][128 * mt:128 * (mt + 1), :], [], xres_B[mt], xsem[mt])
                to_featmajor(xres[:, mt, :], xres_B[mt], KC, lambda mt=mt: memT[:, :, 128 * mt:128 * (mt + 1)], lambda: [memT_B])
            for L in range(n_layers):
                wv = kcview(I["w_mem_kv"][L], 0, 512)
                mwB = hff_B[0:8]
                P.op(POOL, lambda s, wv=wv: nc.gpsimd.dma_start(out=arenaA[:, 0:4096].rearrange("p (k c) -> p k c", k=8), in_=wv).then_inc(s, 16),
                     writes=mwB, dma_sem=bsem("memw"), ndma=1)
                W = arenaA[:, 0:4096].rearrange("p (k c) -> p k c", k=8)
                for mt in range(2):
                    b = mm_bank()
                    mms(ps[b][:, :], [(memT[:, c, 128 * mt:128 * (mt + 1)], W[:, c, :]) for c in range(KC)], [memT_B] + mwB, [ps_B[b]])
                    cp(ACT, memst[:, :], ps[b][:, :], [ps_B[b]], [memst_B])
                    dma(SP, O["mk_p"][L, 128 * mt:128 * (mt + 1), :], memst[:, 0:256], [memst_B], [], bsem("memst"), is_store=True)
                    dma(SP, O["mv_p"][L, 128 * mt:128 * (mt + 1), :], memst[:, 256:512], [memst_B], [], bsem("memst"), is_store=True)
                    for m in range(4):
                        cp(DVE, memV[:, L, mt, m, 64 * (m % 2):64 * (m % 2) + 64], memst[:, 256 + 64 * m:256 + 64 * (m + 1)], [memst_B], [mem_B[L]])
                for mc in range(2):
                    b = mm_bank()
                    mms(ps[b][:, 0:256], [(W[:, c, 128 * mc:128 * (mc + 1)], memT[:, c, :]) for c in range(KC)], [memT_B] + mwB, [ps_B[b]])
                    for hh in range(2):
                        m = 2 * mc + hh
                        cp(ACT, memKT[64 * hh:64 * hh + 64, L, m, :], ps[b][64 * hh:64 * hh + 64, 0:256], [ps_B[b]], [mem_B[L]])

        def mem_setup_sample(L, bidx):
            for mt in range(2):
                dma(SP, memst[:, 0:256], I["c_mk"][L, bidx, 128 * mt:128 * (mt + 1), :], [], [memst_B], bsem("memst"))
                dma(SP, memst[:, 256:512], I["c_mv"][L, bidx, 128 * mt:128 * (mt + 1), :], [], [memst_B], bsem("memst"))
                for m in range(4):
                    cp(DVE, memV[:, L, mt, m, 64 * (m % 2):64 * (m % 2) + 64], memst[:, 256 + 64 * m:256 + 64 * (m + 1)], [memst_B], [mem_B[L]])
                xi_ = rr.setdefault("xbf", 0)
                rr["xbf"] = xi_ ^ 1
                cp(DVE, xbf[:, xi_, 0:256], memst[:, 0:256], [memst_B], [xbf_B[xi_]])
                bank = tp_bank()
                transposes(bank, [(128 * c, xbf[:, xi_, 128 * c:128 * (c + 1)]) for c in range(2)], [xbf_B[xi_]])
                pv = psbf(bank)
                for mc in range(2):
                    for hh in range(2):
                        m = 2 * mc + hh
                        cp(ACT, memKT[64 * hh:64 * hh + 64, L, m, 128 * mt:128 * (mt + 1)], pv[64 * hh:64 * hh + 64, 128 * mc:128 * (mc + 1)], [ps_B[bank]], [mem_B[L]])

        def mem_attention(L, NTs, colgroups):
            for (c0, ncol, setup) in colgroups:
                if setup is not None:
                    setup()
                for mc in range(2):
                    ob, sbk = 4 + mc, 6 + mc
                    first = True
                    pending = None
                    for hh in range(2):
                        m = 2 * mc + hh
                        for mt in range(2):
                            sbank = tp_bank()
                            mms(ps[sbank][:, 0:ncol], [(memKT[:, L, m, 128 * mt:128 * (mt + 1)], mqT[:, mc, c0:c0 + ncol])],
                                [mem_B[L], mqT_B[mc]], [ps_B[sbank]])
                            pi = rr.setdefault("pb", 0)
                            rr["pb"] = (pi + 1) % 4
                            act(Pb[:, pi, 0:ncol], ps[sbank][:, 0:ncol], AF.Exp, [ps_B[sbank]], [Pb_B[pi]], scale=0.125)
                            last = (hh == 1 and mt == 1)

                            def f(first=first, last=last, pi=pi, m=m, mt=mt, hh=hh, ob=ob, sbk=sbk, c0=c0, ncol=ncol):
                                nc.tensor.matmul(ps[ob][:, c0:c0 + ncol], memV[:, L, mt, m, :], Pb[:, pi, 0:ncol], start=first, stop=last)
                                return nc.tensor.matmul(ps[sbk][:, c0:c0 + ncol], onespad[:, hh, :], Pb[:, pi, 0:ncol], start=first, stop=last)
                            if pending is not None:
                                pending()
                            pending = (lambda f=f, pi=pi, ob=ob, sbk=sbk: P.op(PE, f, reads=[mem_B[L], Pb_B[pi], cst_B], writes=[ps_B[ob], ps_B[sbk]]))
                            first = False
                    pending()
                    ri = rr.setdefault("attr", 0)
                    rr["attr"] = ri ^ 1
                    act(att_r[:, ri, 0:ncol], ps[sbk][:, c0:c0 + ncol], AF.Ln, [ps_B[sbk]], [att_r_B[ri]])
                    act(att_r[:, ri, 0:ncol], att_r[:, ri, 0:ncol], AF.Exp, [att_r_B[ri]], [att_r_B[ri]], scale=-1.0)
                    subs = sorted(set(range(c0 // 128, (c0 + ncol - 1) // 128 + 1)))
                    tt(DVE, actT[:, 6 + mc, c0:c0 + ncol], ps[ob][:, c0:c0 + ncol], att_r[:, ri, 0:ncol], ALU.mult,
                       [ps_B[ob], att_r_B[ri]], [actT_B[6 + mc][s] for s in subs])

        def layer_norm(L, which, NTs, final_out=None):
            gk, bk = ("ln1_g", "ln1_b") if which == 1 else ("ln2_g", "ln2_b")
            dma(SP, lnp[:, 0, :], I[gk][L].partition_broadcast(128), [], [lnp_B], lnsem)
            dma(SP, lnp[:, 1, :], I[bk][L].partition_broadcast(128), [], [lnp_B], lnsem)
            for s in range(NTs):
                def bns(s=s):
                    nc.vector.bn_stats(out=stat[:, s, 0, :], in_=xres[:, s, 0:512])
                    return nc.vector.bn_stats(out=stat[:, s, 1, :], in_=xres[:, s, 512:1024])
                P.op(DVE, bns, reads=xres_B[s], writes=[stat_B[s]])
                P.op(DVE, lambda s=s: nc.vector.bn_aggr(out=aggr[:, s, 0, 0:2], in_=stat[:, s, 0:2, :]), reads=[stat_B[s]], writes=[aggr_B[s]])
            agv = aggr[:, 0:NTs, 0, :]
            act(agv[:, :, 2:3], agv[:, :, 1:2], AF.Ln, aggr_B[0:NTs] + [cst_B], aggr_B[0:NTs], bias=epsc[:, :])
            act(agv[:, :, 3:4], agv[:, :, 2:3], AF.Exp, aggr_B[0:NTs], aggr_B[0:NTs], scale=-0.5)
            for s in range(NTs):
                stt(DVE, xres[:, s, :], xres[:, s, :], aggr[:, s, 0, 0:1], lnp[:, 0, :], ALU.subtract, ALU.mult,
                    xres_B[s] + [aggr_B[s], lnp_B], xres_B[s])
                stt(DVE, xres[:, s, :], xres[:, s, :], aggr[:, s, 0, 3:4], lnp[:, 1, :], ALU.mult, ALU.add,
                    xres_B[s] + [aggr_B[s], lnp_B], xres_B[s])
                if final_out is not None:
                    dma(SP, final_out(s), xres[:, s, :], xres_B[s], [], ysem[s], is_store=True)
                else:
                    to_featmajor(xres[:, s, :], xres_B[s], KC, lambda s=s: actT[:, :, 128 * s:128 * (s + 1)],
                                 lambda s=s: [actT_B[c][s] for c in range(KC)])

        def wo_and_ffn(L, NTs, final_out):
            ncol = 128 * NTs
            for n in range(2):
                si = w_next(f"wo{L}_{n}")
                W = wslot[si][:, 0:4096].rearrange("p (k c) -> p k c", k=8)
                for s in range(NTs):
                    b = mm_bank()
                    mms(ps[b][:, :], [(actT[:, c, 128 * s:128 * (s + 1)], W[:, c, :]) for c in range(KC)],
                        [actT_B[c][s] for c in range(KC)] + [wslot_B[si]], [ps_B[b]])
                    xb_ = [xres_B[s][2 * n], xres_B[s][2 * n + 1]]
                    stt(DVE, xres[:, s, 512 * n:512 * (n + 1)], xres[:, s, 512 * n:512 * (n + 1)], ALPHA, ps[b][:, :],
                        ALU.mult, ALU.add, xb_ + [ps_B[b]], xb_)
            layer_norm(L, 1, NTs)
            allact = [actT_B[c][s] for c in range(KC) for s in range(NTs)]
            for bb in range(11):
                si = w_next(f"gu{L}_{bb}")
                Wg = wslot[si][:, 0:2048].rearrange("p (k c) -> p k c", k=8)
                Wu = wslot[si][:, 2048:4096].rearrange("p (k c) -> p k c", k=8)
                for j in range(2):
                    f_ = 2 * bb + j
                    gi = rr.setdefault("gu", 0)
                    rr["gu"] = gi ^ 1
                    gb_, ub_ = 4 + gi, 6 + gi
                    mms(ps[gb_][:, 0:ncol], [(Wg[:, c, 128 * j:128 * (j + 1)], actT[:, c, 0:ncol]) for c in range(KC)],
                        allact + [wslot_B[si]], [ps_B[gb_]])
                    mms(ps[ub_][:, 0:ncol], [(Wu[:, c, 128 * j:128 * (j + 1)], actT[:, c, 0:ncol]) for c in range(KC)],
                        allact + [wslot_B[si]], [ps_B[ub_]])
                    act(sgate[:, gi, 0:ncol], ps[gb_][:, 0:ncol], AF.Silu, [ps_B[gb_]], [sgate_B[gi]])
                    tt(DVE, hff[:, f_, 0:ncol], sgate[:, gi, 0:ncol], ps[ub_][:, 0:ncol], ALU.mult,
                       [sgate_B[gi], ps_B[ub_]], [hff_B[f_]])
            for n in range(4):
                si = w_next(f"dn{L}_{n}")
                W = wslot[si][:, 0:FC * 256].rearrange("p (k c) -> p k c", k=FC)
                for s in range(NTs):
                    b = mm_bank()
                    mms(ps[b][:, 0:256], [(hff[:, f_, 128 * s:128 * (s + 1)], W[:, f_, :]) for f_ in range(FC)],
                        hff_B + [wslot_B[si]], [ps_B[b]])
                    stt(DVE, xres[:, s, 256 * n:256 * (n + 1)], xres[:, s, 256 * n:256 * (n + 1)], ALPHA, ps[b][:, 0:256],
                        ALU.mult, ALU.add, [xres_B[s][n], ps_B[b]], [xres_B[s][n]])
            layer_norm(L, 2, NTs, final_out=final_out)

        def rope_ret(ti, rslot, s, out_bf, out_bufs):
            ri_ = rr.setdefault("ropeab", 0)
            rr["ropeab"] = ri_ ^ 1
            ropeA, ropeB, ropeA_B, ropeB_B = ropeA2[:, ri_, :], ropeB2[:, ri_, :], ropeA2_B[ri_], ropeB2_B[ri_]
            tv = t32[:, ti, :].rearrange("p (h a d) -> p h a d", h=3, a=2)
            tab = ropet[:, rslot, s, :]
            cosb = tab[:, 0:64].unsqueeze(1).unsqueeze(1).to_broadcast([128, 3, 2, 64])
            nsin = tab[:, 64:128].unsqueeze(1).to_broadcast([128, 3, 64])
            sin = tab[:, 128:192].unsqueeze(1).to_broadcast([128, 3, 64])
            Av = ropeA.rearrange("p (h a d) -> p h a d", h=3, a=2)
            Bv = ropeB.rearrange("p (h a d) -> p h a d", h=3, a=2)
            tt(DVE, Bv[:, :, 0, :], tv[:, :, 1, :], nsin, ALU.mult, [t32_B[ti], ropet_B[rslot]], [ropeB_B])
            tt(DVE, Bv[:, :, 1, :], tv[:, :, 0, :], sin, ALU.mult, [t32_B[ti], ropet_B[rslot]], [ropeB_B])
            tt(DVE, Av, tv, cosb, ALU.mult, [t32_B[ti], ropet_B[rslot]], [ropeA_B])
            tt(DVE, out_bf, ropeA, ropeB, ALU.add, [ropeA_B, ropeB_B], out_bufs)

        def rope_diff(ti, rslot, s):
            ri_ = rr.setdefault("ropeab", 0)
            rr["ropeab"] = ri_ ^ 1
            ropeA, ropeB, ropeA_B, ropeB_B = ropeA2[:, ri_, :], ropeB2[:, ri_, :], ropeA2_B[ri_], ropeB2_B[ri_]
            tv = t32[:, ti, :].rearrange("p (g d) -> p g d", g=6)
            tab = ropet[:, rslot, s, :]
            cc = tab[:, 192:208].unsqueeze(1).to_broadcast([128, 6, 16])
            nsin = tab[:, 208:216].unsqueeze(1).to_broadcast([128, 6, 8])
            sin = tab[:, 216:224].unsqueeze(1).to_broadcast([128, 6, 8])
            Av = ropeA[:, 0:96].rearrange("p (g d) -> p g d", g=6)
            Bv = ropeB[:, 0:96].rearrange("p (g d) -> p g d", g=6)
            tt(DVE, Av, tv[:, :, 0:16], cc, ALU.mult, [t32_B[ti], ropet_B[rslot]], [ropeA_B])
            tt(DVE, Bv[:, :, 0:8], tv[:, :, 8:16], nsin, ALU.mult, [t32_B[ti], ropet_B[rslot]], [ropeB_B])
            tt(DVE, Bv[:, :, 8:16], tv[:, :, 0:8], sin, ALU.mult, [t32_B[ti], ropet_B[rslot]], [ropeB_B])
            tt(DVE, tv[:, :, 0:16], Av, Bv, ALU.add, [ropeA_B, ropeB_B], [t32_B[ti]])

        def next_t32():
            ti = rr.setdefault("t32", 0)
            rr["t32"] = (ti + 1) % NT32
            return ti

        def next_krot():
            ki = rr.setdefault("krot", 0)
            rr["krot"] = (ki + 1) % NKROT
            return ki

        def layer0_mixer(tl):
            NTs, kind = tl["NT"], tl["kind"]
            ncol = 128 * NTs
            rslot = tl["rslot"]
            ngrp = 1 if kind == "p" else 2
            dec = hc["dec_p"] if kind == "p" else hc["dec_s"]
            pend = []
            for j in range(8):
                si = w_next(f"in0_{j}")
                W = wslot[si][:, 0:8 * 384].rearrange("p (k c) -> p k c", k=8)
                typ, half = "qkvg"[j // 2], j % 2
                for s in range(NTs):
                    b = mm_bank()
                    mms(ps[b][:, 0:384], [(actT[:, c, 128 * s:128 * (s + 1)], W[:, c, :]) for c in range(KC)],
                        [actT_B[c][s] for c in range(KC)] + [wslot_B[si]], [ps_B[b]])
                    if typ in "qk":
                        ti = next_t32()
                        cp(ACT, t32[:, ti, :], ps[b][:, 0:384], [ps_B[b]], [t32_B[ti]])
                        ki = next_krot()
                        rope_ret(ti, rslot, s, krot[:, ki, :], [krot_B[ki]])

                        def post(ki=ki, typ=typ, half=half, s=s):
                            bank = tp_bank()
                            transposes(bank, [(128 * hh, krot[:, ki, 128 * hh:128 * (hh + 1)]) for hh in range(3)], [krot_B[ki]])
                            pv = psbf(bank)[:, 0:384].rearrange("p (h t) -> p h t", h=3)
                            hs = range(3 * half, 3 * half + 3)
                            if typ == "q":
                                cp(ACT, qT[:, 3 * half:3 * half + 3, 128 * s:128 * (s + 1)], pv, [ps_B[bank]], [qT_B[h][s] for h in hs])
                                for g in range(ngrp):
                                    dst = qxT[:, 3 * half:3 * half + 3, 128 * (s + g):128 * (s + g + 1)]
                                    tt(DVE, dst, pv, xi[:, g, 384 * half:384 * (half + 1)].rearrange("p (h t) -> p h t", h=3), ALU.mult,
                                       [ps_B[bank], rett_B], [qxT_B[h][s + g] for h in hs])
                            else:
                                cp(ACT, kT[:, 3 * half:3 * half + 3, 128 * s:128 * (s + 1)], pv, [ps_B[bank]], [kT_B[h][s] for h in hs])
                        if typ == "k":
                            for g in range(ngrp):
                                tt(DVE, kz[:, s + g, 384 * half:384 * (half + 1)], krot[:, ki, :], zeta[:, g, 384 * half:384 * (half + 1)], ALU.mult,
                                   [krot_B[ki], rett_B], [kz_B[s + g]])
                        pend.append(post)
                        if len(pend) > PDEPTH:
                            pend.pop(0)()
                        continue
                    if pend:
                        pend.pop(0)()
                    if typ == "v":
                        cp(ACT, vtok[:, s, 384 * half:384 * (half + 1)], ps[b][:, 0:384], [ps_B[b]], [vtok_B[s]])
                    else:
                        act(sg[:, s, 384 * half:384 * (half + 1)], ps[b][:, 0:384], AF.Silu, [ps_B[b]], [sg_B[s]])
            while pend:
                pend.pop(0)()
            si = w_next("mq0")
            W = wslot[si][:, 0:2048].rearrange("p (k c) -> p k c", k=8)
            allact = [actT_B[c][s] for c in range(KC) for s in range(NTs)]
            for mc in range(2):
                b = mm_bank()
                mms(ps[b][:, 0:ncol], [(W[:, c, 128 * mc:128 * (mc + 1)], actT[:, c, 0:ncol]) for c in range(KC)], allact + [wslot_B[si]], [ps_B[b]])
                cp(ACT, mqT[:, mc, 0:ncol], ps[b][:, 0:ncol], [ps_B[b]], [mqT_B[mc]])
            def retA(s):
                for half in range(2):
                    hs = list(range(3 * half, 3 * half + 3))
                    mb = 5 + half

                    def fm(s=s, hs=hs, mb=mb):
                        for x_, h in enumerate(hs):
                            ins = nc.tensor.matmul(ps[mb][:, 128 * x_:128 * (x_ + 1)], kT[:, h, 128 * s:128 * (s + 1)], qT[:, h, 128 * s:128 * (s + 1)], start=True, stop=True)
                        return ins
                    P.op(PE, fm, reads=[kT_B[h][s] for h in hs] + [qT_B[h][s] for h in hs], writes=[ps_B[mb]])
                    tt(DVE, MTsb[:, half, :], ps[mb][:, 0:384], DT[:, 384 * half:384 * (half + 1)], ALU.mult, [ps_B[mb], rett_B], [MTsb_B[half]])

            def retB(s):
                for half in range(2):
                    hs = list(range(3 * half, 3 * half + 3))
                    ob, rb = 2 + half, (7 if half == 0 else 4)

                    def fo(s=s, hs=hs, ob=ob, half=half):
                        for x_, h in enumerate(hs):
                            nc.tensor.matmul(ps[ob][:, 128 * x_:128 * (x_ + 1)], MTsb[:, half, 128 * x_:128 * (x_ + 1)], vtok[:, s, 128 * h:128 * (h + 1)], start=True, stop=False)
                            for g in range(ngrp):
                                ins = nc.tensor.matmul(ps[ob][:, 128 * x_:128 * (x_ + 1)], qxT[:, h, 128 * (s + g):128 * (s + g + 1)], Rbf[:, g, 128 * h:128 * (h + 1)], start=False, stop=(g == ngrp - 1))
                        return ins
                    P.op(PE, fo, reads=[MTsb_B[half], vtok_B[s]] + [qxT_B[h][s + g] for h in hs for g in range(ngrp)] + [Rbf_B[g][h] for h in hs for g in range(ngrp)],
                         writes=[ps_B[ob]])
                    for g in range(ngrp):
                        rbank = rb if g == 0 else tp_bank()

                        def fr(s=s, hs=hs, g=g, rbank=rbank):
                            for x_, h in enumerate(hs):
                                ins = nc.tensor.matmul(ps[rbank][:, 128 * x_:128 * (x_ + 1)], kz[:, s + g, 128 * h:128 * (h + 1)], vtok[:, s, 128 * h:128 * (h + 1)], start=True, stop=True)
                            return ins
                        P.op(PE, fr, reads=[kz_B[s + g], vtok_B[s]], writes=[ps_B[rbank]])
                        for x_, h in enumerate(hs):
                            stt(DVE, R32[:, g, 128 * h:128 * (h + 1)], R32[:, g, 128 * h:128 * (h + 1)], dec[h], ps[rbank][:, 128 * x_:128 * (x_ + 1)],
                                ALU.mult, ALU.add, [R_B[g][h], ps_B[rbank]], [R_B[g][h]])
                        cp(ACT, Rbf[:, g, 384 * half:384 * (half + 1)], R32[:, g, 384 * half:384 * (half + 1)], [R_B[g][h] for h in hs], [Rbf_B[g][h] for h in hs])

            def retC1(s):
                def bns(s=s):
                    for h in range(H):
                        ins = nc.vector.bn_stats(out=stat[:, s, h, :], in_=ps[2 + h // 3][:, 128 * (h % 3):128 * (h % 3 + 1)])
                    return ins
                P.op(DVE, bns, reads=[ps_B[2], ps_B[3]], writes=[stat_B[s]])

                def bna(s=s):
                    for h in range(H):
                        ins = nc.vector.bn_aggr(out=aggr[:, s, h, 0:2], in_=stat[:, s, h:h + 1, :])
                    return ins
                P.op(DVE, bna, reads=[stat_B[s]], writes=[aggr_B[s]])
                av = aggr[:, s, 0:6, :]
                act(av[:, :, 2:3], av[:, :, 1:2], AF.Ln, [aggr_B[s], cst_B], [aggr_B[s]], bias=epsc[:, :])
                act(av[:, :, 3:4], av[:, :, 2:3], AF.Exp, [aggr_B[s]], [aggr_B[s]], scale=-0.5)
                stt(DVE, av[:, :, 2:3], av[:, :, 0:1], -1.0, av[:, :, 3:4], ALU.mult, ALU.mult, [aggr_B[s]], [aggr_B[s]])

            def retC2(s, hi):
                for half in range(2):
                    ob = 2 + half
                    def fn(s=s, ob=ob, half=half):
                        for x_ in range(3):
                            h = 3 * half + x_
                            ins = nc.scalar.activation(out=gnt[:, half, 128 * x_:128 * (x_ + 1)], in_=ps[ob][:, 128 * x_:128 * (x_ + 1)],
                                                       func=AF.Identity, scale=aggr[:, s, h, 3:4], bias=aggr[:, s, h, 2:3])
                        return ins
                    P.op(ACT, fn, reads=[ps_B[ob], aggr_B[s]], writes=[gnt_B[half]])
                    tt(DVE, gnt[:, half, :], gnt[:, half, :], gng[:, 384 * half:384 * (half + 1)], ALU.mult, [gnt_B[half], cst_B], [gnt_B[half]])
                    tt(DVE, hm[:, hi, 384 * half:384 * (half + 1)], gnt[:, half, :], sg[:, s, 384 * half:384 * (half + 1)], ALU.mult, [gnt_B[half], sg_B[s]], [hm_B[hi]])

            def retD(s, hi):
                bank = tp_bank()
                transposes(bank, [(128 * h, hm[:, hi, 128 * h:128 * (h + 1)]) for h in range(H)], [hm_B[hi]])
                pv = psbf(bank)[:, 0:768].rearrange("p (h t) -> p h t", h=H)
                cp(ACT, actT[:, 0:6, 128 * s:128 * (s + 1)], pv, [ps_B[bank]], [actT_B[c][s] for c in range(6)])

            retA(0)
            dpend = None
            for s in range(NTs):
                hi = rr.setdefault("hm", 0)
                rr["hm"] = hi ^ 1
                retB(s)
                retC1(s)
                if s + 1 < NTs:
                    retA(s + 1)
                if dpend is not None:
                    dpend()
                retC2(s, hi)
                dpend = (lambda s=s, hi=hi: retD(s, hi))
            dpend()

        S1BANKS, S2BANKS = (0, 1, 6), (2, 3, 7)

        def attention_head(h, ncol, keytiles, mid=None):
            nkt = len(keytiles)
            fifo = []
            for x_, kt in enumerate(keytiles):
                c0, ncl = kt["c0"], kt["nc"]
                r_ = rr.setdefault("att3", 0)
                rr["att3"] = (r_ + 1) % 3
                s1b, s2b = S1BANKS[r_], S2BANKS[r_]
                mms(ps[s1b][:, c0:c0 + ncl], [(kt["kt"], qT[:, h, c0:c0 + ncl])], kt["reads"] + [qT_B[h][s] for s in range(4)], [ps_B[s1b]])
                mms(ps[s2b][:, c0:c0 + ncl], [(kt["kt"], qxT[:, h, c0:c0 + ncl])], kt["reads"] + [qxT_B[h][s] for s in range(4)], [ps_B[s2b]])
                p1, p2 = r_, 3 + r_
                act(Pb[:, p1, c0:c0 + ncl], ps[s1b][:, c0:c0 + ncl], AF.Exp, [ps_B[s1b]], [Pb_B[p1]], scale=0.125)
                act(Pb[:, p2, c0:c0 + ncl], ps[s2b][:, c0:c0 + ncl], AF.Exp, [ps_B[s2b]], [Pb_B[p2]], scale=0.125)
                if kt.get("fix") is not None:
                    kt["fix"](p1)
                    kt["fix"](p2)
                first, last = (x_ == 0), (x_ == nkt - 1)
                fc0, fnc = kt.get("acc_c0", 0), kt.get("acc_nc", ncol)
                for (e_, acc, acc_B, p) in ((DVE, accA, accA_B, p1), (POOL, accB, accB_B, p2)):
                    if first:
                        cp(e_, acc[:, fc0:fc0 + fnc], Pb[:, p, fc0:fc0 + fnc], [Pb_B[p]], [acc_B])
                    else:
                        tt(e_, acc[:, fc0:fc0 + fnc], acc[:, fc0:fc0 + fnc], Pb[:, p, fc0:fc0 + fnc], ALU.add, [acc_B, Pb_B[p]], [acc_B])

                def f(p1=p1, p2=p2, kt=kt, first=first, last=last, fc0=fc0, fnc=fnc):
                    nc.tensor.matmul(ps[4][:, fc0:fc0 + fnc], kt["v"], Pb[:, p1, fc0:fc0 + fnc], start=first, stop=last)
                    return nc.tensor.matmul(ps[5][:, fc0:fc0 + fnc], kt["v"], Pb[:, p2, fc0:fc0 + fnc], start=first, stop=last)
                fifo.append(lambda f=f, kt=kt, p1=p1, p2=p2: P.op(PE, f, reads=kt["reads"] + [Pb_B[p1], Pb_B[p2]], writes=[ps_B[4], ps_B[5]]))
                if len(fifo) > 2:
                    fifo.pop(0)()
                if kt.get("pre") is not None:
                    kt["pre"]()
                if mid is not None and x_ == min(2, nkt - 1):
                    mid()
                    mid = None
            while fifo:
                fifo.pop(0)()
            mms(ps[6][:, fc0:fc0 + fnc], [(ones_f[:, :], accA[:, fc0:fc0 + fnc])], [accA_B, cst_B], [ps_B[6]])
            mms(ps[7][:, fc0:fc0 + fnc], [(ones_f[:, :], accB[:, fc0:fc0 + fnc])], [accB_B, cst_B], [ps_B[7]])

        def attention_finish_a(h, c0, ncol):
            act(att_r[:, 0, 0:ncol], ps[6][:, c0:c0 + ncol], AF.Ln, [ps_B[6]], [att_r_B[0]])
            cp(DVE, att_o[:, 0, 0:ncol], ps[4][:, c0:c0 + ncol], [ps_B[4]], [att_o_B[0]])
            act(att_r[:, 1, 0:ncol], ps[7][:, c0:c0 + ncol], AF.Ln, [ps_B[7]], [att_r_B[1]])
            cp(DVE, att_o[:, 1, 0:ncol], ps[5][:, c0:c0 + ncol], [ps_B[5]], [att_o_B[1]])
            act(att_r[:, 0, 0:ncol], att_r[:, 0, 0:ncol], AF.Exp, [att_r_B[0]], [att_r_B[0]], scale=-1.0)
            act(att_r[:, 1, 0:ncol], att_r[:, 1, 0:ncol], AF.Exp, [att_r_B[1]], [att_r_B[1]], scale=-1.0)
            tt(DVE, att_o[:, 0, 0:ncol], att_o[:, 0, 0:ncol], att_r[:, 0, 0:ncol], ALU.mult, [att_o_B[0], att_r_B[0]], [att_o_B[0]])
            tt(DVE, att_o[:, 1, 0:ncol], att_o[:, 1, 0:ncol], att_r[:, 1, 0:ncol], ALU.mult, [att_o_B[1], att_r_B[1]], [att_o_B[1]])
            stt(DVE, att_o[:, 0, 0:ncol], att_o[:, 1, 0:ncol], lam[:, 5:6], att_o[:, 0, 0:ncol], ALU.mult, ALU.add,
                [att_o_B[0], att_o_B[1], lam_B], [att_o_B[0]])

        def attention_finish_b(h, c0, ncol):
            act(att_o[:, 1, 0:ncol], att_o[:, 0, 0:ncol], AF.Square, [att_o_B[0]], [att_o_B[1]])
            b = S1BANKS[rr.setdefault("att3", 0)]
            mms(ps[b][:, 0:ncol], [(ones_f[:, :], att_o[:, 1, 0:ncol])], [att_o_B[1], cst_B], [ps_B[b]])
            act(att_r[:, 0, 0:ncol], ps[b][:, 0:ncol], AF.Ln, [ps_B[b], cst_B], [att_r_B[0]], scale=1.0 / 128, bias=epsc[:, :])
            act(att_r[:, 1, 0:ncol], att_r[:, 0, 0:ncol], AF.Exp, [att_r_B[0]], [att_r_B[1]], scale=-0.5)
            subs = sorted(set(range(c0 // 128, (c0 + ncol - 1) // 128 + 1)))
            stt(DVE, actT[:, h, c0:c0 + ncol], att_o[:, 0, 0:ncol], sublg[:, 1:2], att_r[:, 1, 0:ncol], ALU.mult, ALU.mult,
                [att_o_B[0], att_r_B[1], lam_B], [actT_B[h][s] for s in subs])

        def layer1_mixer(tl):
            NTs, kind = tl["NT"], tl["kind"]
            ncol = 128 * NTs
            rslot = tl["rslot"]
            t0 = tl["t0"]
            P.op(POOL, lambda: nc.gpsimd.memset(qT[64:128, :, 0:ncol], 0.0), writes=[b for h in range(H) for b in qT_B[h]])
            P.op(POOL, lambda: nc.gpsimd.memset(qxT[0:64, :, 0:ncol], 0.0), writes=[b for h in range(H) for b in qxT_B[h]])
            pend = []
            for j in range(6):
                si = w_next(f"in1_{j}")
                W = wslot[si][:, 0:8 * 384].rearrange("p (k c) -> p k c", k=8)
                typ, half = "qkv"[j // 2], j % 2
                hs = list(range(3 * half, 3 * half + 3))
                for s in range(NTs):
                    b = mm_bank()
                    mms(ps[b][:, 0:384], [(actT[:, c, 128 * s:128 * (s + 1)], W[:, c, :]) for c in range(KC)],
                        [actT_B[c][s] for c in range(KC)] + [wslot_B[si]], [ps_B[b]])
                    ti = next_t32()
                    cp(ACT, t32[:, ti, :], ps[b][:, 0:384], [ps_B[b]], [t32_B[ti]])
                    if typ in "qk":
                        rope_diff(ti, rslot, s)
                        ki = next_krot()
                        cp(DVE, krot[:, ki, :], t32[:, ti, :], [t32_B[ti]], [krot_B[ki]])
                        if typ == "k":
                            dma(SP, tl["dk_out"](s, half), t32[:, ti, :], [t32_B[ti]], [], t32sem[ti], is_store=True)

                        def post(ki=ki, typ=typ, half=half, s=s, hs=hs):
                            bank = tp_bank()
                            transposes(bank, [(128 * hh, krot[:, ki, 128 * hh:128 * (hh + 1)]) for hh in range(3)], [krot_B[ki]])
                            pv = psbf(bank)[:, 0:384].rearrange("p (h t) -> p h t", h=3)
                            if typ == "q":
                                cp(ACT, qT[0:64, 3 * half:3 * half + 3, 128 * s:128 * (s + 1)], pv[0:64], [ps_B[bank]], [qT_B[h][s] for h in hs])
                                cp(ACT, qxT[64:128, 3 * half:3 * half + 3, 128 * s:128 * (s + 1)], pv[64:128], [ps_B[bank]], [qxT_B[h][s] for h in hs])
                            else:
                                cp(ACT, kT[:, 3 * half:3 * half + 3, 128 * s:128 * (s + 1)], pv, [ps_B[bank]], [kT_B[h][s] for h in hs])
                        pend.append(post)
                        if len(pend) > PDEPTH:
                            pend.pop(0)()
                    else:
                        if pend:
                            pend.pop(0)()
                        cp(DVE, vtok[:, s, 384 * half:384 * (half + 1)], t32[:, ti, :], [t32_B[ti]], [vtok_B[s]])
                        dma(SP, tl["dv_out"](s, half), t32[:, ti, :], [t32_B[ti]], [], t32sem[ti], is_store=True)
            while pend:
                pend.pop(0)()
            si = w_next("mq1")
            W = wslot[si][:, 0:2048].rearrange("p (k c) -> p k c", k=8)
            allact = [actT_B[c][s] for c in range(KC) for s in range(NTs)]
            for mc in range(2):
                b = mm_bank()
                mms(ps[b][:, 0:ncol], [(W[:, c, 128 * mc:128 * (mc + 1)], actT[:, c, 0:ncol]) for c in range(KC)], allact + [wslot_B[si]], [ps_B[b]])
                cp(ACT, mqT[:, mc, 0:ncol], ps[b][:, 0:ncol], [ps_B[b]], [mqT_B[mc]])
            if kind == "p":
                tix = t0 // TT
                def fpub(sem, t0=t0, tix=tix):
                    for h in range(H):
                        nc.sync.dma_start(out=KTd[h, :, t0:t0 + TT], in_=kT[:, h, :]).then_inc(sem, 16)
                        nc.sync.dma_start(out=Vd[h, :, 4 * tix:4 * tix + 4, :], in_=vtok[:, :, 128 * h:128 * (h + 1)]).then_inc(sem, 16)
                P.op(SP, fpub, reads=[kT_B[h][s] for h in range(H) for s in range(4)] + vtok_B, writes=KTd_B + Vd_B, dma_sem=kvwsem, ndma=2 * H)
                nblk = (t0 + TT + 1023) // 1024
                blocks = [(h, kb) for h in range(H) for kb in range(nblk)]
                kvst = {"issued": 0}

                def kv_issue_upto(n, blocks=blocks):
                    while kvst["issued"] < min(n, len(blocks)):
                        i = kvst["issued"]
                        h_, kb = blocks[i]
                        nk = min(1024, t0 + TT - 1024 * kb)
                        ki = i % NKV

                        def f(sem, ki=ki, kb=kb, nk=nk, h_=h_):
                            nc.sync.dma_start(out=kvK[ki][:, 0:nk], in_=KTd[h_, :, 1024 * kb:1024 * kb + nk]).then_inc(sem, 16)
                            nc.sync.dma_start(out=kvV[ki][:, 0:nk // 128, :], in_=Vd[h_, :, 8 * kb:8 * kb + nk // 128, :]).then_inc(sem, 16)
                        P.op(SP, f, reads=[KTd_B[h_], Vd_B[h_]], writes=[kv_B[ki]], dma_sem=kvsem[ki], ndma=2)
                        kvst["issued"] += 1

                midcb = [None]
                kv_issue_upto(NKV - 1)
                for h in range(H):
                    kts = []
                    for kb in range(nblk):
                        gi = h * nblk + kb
                        nk = min(1024, t0 + TT - 1024 * kb)
                        ki = gi % NKV
                        for x_ in range(nk // 128):
                            key0 = 1024 * kb + 128 * x_
                            d = dict(kt=kvK[ki][:, 128 * x_:128 * (x_ + 1)], v=kvV[ki][:, x_, :], reads=[kv_B[ki]], c0=0, nc=ncol)
                            if x_ == 2:
                                d["pre"] = (lambda gi=gi: kv_issue_upto(gi + NKV))
                            if key0 >= t0:
                                jj = (key0 - t0) // 128
                                d["c0"], d["nc"] = 128 * jj, ncol - 128 * jj

                                def fix(p, jj=jj):
                                    if jj > 0:
                                        P.op(POOL, lambda: nc.gpsimd.memset(Pb[:, p, 0:128 * jj], 0.0), writes=[Pb_B[p]])
                                    P.op(POOL, lambda: nc.gpsimd.memset(Pb[64:128, p, 128 * jj:128 * jj + 64], 0.0), writes=[Pb_B[p]])
                                d["fix"] = fix
                            kts.append(d)
                    attention_head(h, ncol, kts, mid=midcb[0])
                    attention_finish_a(h, 0, ncol)
                    midcb[0] = (lambda h=h: attention_finish_b(h, 0, ncol))
                midcb[0]()
                midcb[0] = None
                mem_attention(1, NTs, [(0, ncol, None)])
            else:
                for bq in range(2):
                    for ktile in range(8):
                        for half in range(2):
                            ti = next_t32()
                            dma(SP, t32[:, ti, :], I["c_dk"][bq, 128 * ktile:128 * (ktile + 1), 384 * half:384 * (half + 1)], [], [t32_B[ti]], t32sem[ti])
                            ki = next_krot()
                            cp(DVE, krot[:, ki, :], t32[:, ti, :], [t32_B[ti]], [krot_B[ki]])
                            bank = tp_bank()
                            transposes(bank, [(128 * hh, krot[:, ki, 128 * hh:128 * (hh + 1)]) for hh in range(3)], [krot_B[ki]])
                            pv = psbf(bank)[:, 0:384].rearrange("p (h t) -> p h t", h=3)
                            cp(ACT, sKT[:, 3 * half:3 * half + 3, 128 * ktile:128 * (ktile + 1)], pv, [ps_B[bank]], [sKT_B])
                    P.op(POOL, lambda s, bq=bq: nc.gpsimd.dma_start(out=sV[:, :, :], in_=I["c_dv"][bq].rearrange("(k p) c -> p k c", p=128)).then_inc(s, 16),
                         writes=[sV_B], dma_sem=bsem("sV"), ndma=1)
                    for h in range(H):
                        kts = []
                        for ktile in range(8):
                            kts.append(dict(kt=sKT[:, h, 128 * ktile:128 * (ktile + 1)], v=sV[:, ktile, 128 * h:128 * (h + 1)], reads=[sKT_B, sV_B],
                                            c0=64 * bq, nc=64, acc_c0=64 * bq, acc_nc=64))

                        def fix(p, bq=bq):
                            lo = 64 * (1 - bq)
                            P.op(POOL, lambda: nc.gpsimd.memset(Pb[lo:lo + 64, p, 64 * bq:64 * bq + 64], 0.0), writes=[Pb_B[p]])
                        kts.append(dict(kt=kT[:, h, 0:128], v=vtok[:, 0, 128 * h:128 * (h + 1)], reads=[kT_B[h][0], vtok_B[0]],
                                        c0=64 * bq, nc=64, acc_c0=64 * bq, acc_nc=64, fix=fix))
                        attention_head(h, 64, kts)
                        attention_finish_a(h, 64 * bq, 64)
                        attention_finish_b(h, 64 * bq, 64)
                mem_attention(1, 1, [(64 * bq, 64, (lambda bq=bq: mem_setup_sample(1, bq))) for bq in range(2)])

        def x_prefetch(tl):
            tl["xpre"] = True
            n = tl["NT"]

            def f(sem, tl=tl, n=n):
                for s in range(n):
                    nc.gpsimd.dma_start(out=xbfn[:, s, :], in_=tl["x_in"](s)).then_inc(sem, 16)
            P.op(POOL, f, writes=xbfn_B[0:n], dma_sem=xbfnsem, ndma=n)

        def run_tile(tl):
            NTs, kind, rslot = tl["NT"], tl["kind"], tl["rslot"]
            for s in range(NTs):
                dma(SP, xres[:, s, :], tl["x_in"](s), [], xres_B[s], xsem[s])
            dma(SP, ropet[:, rslot, 0:NTs, :], tl["rope_in"], [], [ropet_B[rslot]], ropesem[rslot])
            if not tl.get("xpre"):
                x_prefetch(tl)
            for s in range(NTs):
                bank = tp_bank()
                transposes(bank, [(128 * c, xbfn[:, s, 128 * c:128 * (c + 1)]) for c in range(KC)], [xbfn_B[s]])
                pv = psbf(bank)[:, 0:1024].rearrange("p (c t) -> p c t", c=KC)
                cp(ACT if s % 2 == 0 else DVE, actT[:, :, 128 * s:128 * (s + 1)], pv, [ps_B[bank]], [actT_B[c][s] for c in range(KC)])
            if kind == "s":
                load_ret_tables("s")
                for g in range(2):
                    dma(SP, R32[:, g, :].rearrange("p (h e) -> p h e", h=H), I["c_ret"][g].rearrange("h d e -> d h e"), [], R_B[g], bsem(f"R{g}"))
                    cp(DVE, Rbf[:, g, :], R32[:, g, :], R_B[g], Rbf_B[g])
            layer0_mixer(tl)
            if kind == "p":
                mem_attention(0, NTs, [(0, 128 * NTs, None)])
                if tl["last"]:
                    dma(SP, O["rs_p"].rearrange("h d e -> d h e"), R32[:, 0, :].rearrange("p (h e) -> p h e", h=H), R_B[0], [], bsem("R0"), is_store=True)
            else:
                mem_attention(0, 1, [(64 * bq, 64, (lambda bq=bq: mem_setup_sample(0, bq))) for bq in range(2)])
                for g in range(2):
                    dma(SP, O["rs_s"][g].rearrange("h d e -> d h e"), R32[:, g, :].rearrange("p (h e) -> p h e", h=H), R_B[g], [], bsem(f"R{g}"), is_store=True)
            if n_layers == 1:
                wo_and_ffn(0, NTs, tl["y_out"])
                return
            wo_and_ffn(0, NTs, None)
            layer1_mixer(tl)
            if tl.get("next") is not None:
                x_prefetch(tl["next"])
            wo_and_ffn(1, NTs, tl["y_out"])

        sKT = None
        sV = None
        sKT_B = Buf("sKT")
        sV_B = Buf("sV")

        ptiles = []
        for t in range(n_ptiles):
            t0 = t * TT
            ptiles.append(dict(kind="p", NT=4, t0=t0, rslot=t % 2, last=(t == n_ptiles - 1),
                               x_in=(lambda s, t0=t0: I["xp"][t0 + 128 * s:t0 + 128 * (s + 1), :]),
                               rope_in=I["rope_p"][t0:t0 + TT, :].rearrange("(s p) c -> p s c", p=128),
                               y_out=(lambda s, t0=t0: O["y_p"][t0 + 128 * s:t0 + 128 * (s + 1), :]),
                               dk_out=(lambda s, half, t0=t0: O["dk_p"][t0 + 128 * s:t0 + 128 * (s + 1), 384 * half:384 * (half + 1)]),
                               dv_out=(lambda s, half, t0=t0: O["dv_p"][t0 + 128 * s:t0 + 128 * (s + 1), 384 * half:384 * (half + 1)])))
        tiles = list(ptiles)
        if do_sample:
            tiles.append(dict(kind="s", NT=1, t0=0, rslot=n_ptiles % 2, last=True,
                              x_in=(lambda s: I["xs"][:, :]),
                              rope_in=I["rope_s"].rearrange("(s p) c -> p s c", p=128),
                              y_out=(lambda s: O["y_s"][:, :]),
                              dk_out=(lambda s, half: O["dk_s"][:, 384 * half:384 * (half + 1)]),
                              dv_out=(lambda s, half: O["dv_s"][:, 384 * half:384 * (half + 1)])))
            sKT = xres[:, 1:4, :].rearrange("p s d -> p (s d)").bitcast(BF16).rearrange("p (h t) -> p h t", h=H)
            sV = arenaB[:, :].rearrange("p (k c) -> p k c", k=8)
            alias([sKT_B], [b for s_ in range(1, 4) for b in xres_B[s_]])
            alias([sV_B], kz_B + sg_B + kv_B)
        for a_, b_ in zip(tiles[:-1], tiles[1:]):
            a_["next"] = b_
        plan_weights(tiles)
        startup()

        def prompt_setup():
            mem_setup_prompt()
            load_ret_tables("p")
            P.op(DVE, lambda: nc.vector.memset(R32[:, :, :], 0.0), writes=R_B[0] + R_B[1])
            P.op(DVE, lambda: nc.vector.memset(Rbf[:, :, :], 0.0), writes=Rbf_B[0] + Rbf_B[1])
        for tl in tiles:
            if tl["kind"] == "p" and tl["t0"] == 0:
                prompt_setup()
            run_tile(tl)
        info = P.emit()
    return nc, info


_CACHE = {}


def _core_inputs(c, inp, hc):
    f = lambda a: np.ascontiguousarray(np.asarray(a, dtype=np.float32))
    m = {
        "xp": f(inp["x_prompt"][c]),
        "xs": f(inp["x_sample"][2 * c:2 * c + 2].reshape(128, D)),
        "memp": f(inp["mem_prompt"][c]),
        "c_ret": f(inp["cache_ret_state"][0, 2 * c:2 * c + 2]),
        "c_dk": f(inp["cache_diff_k"][0, 2 * c:2 * c + 2].reshape(2, PAST, MIX)),
        "c_dv": f(inp["cache_diff_v"][0, 2 * c:2 * c + 2].reshape(2, PAST, MIX)),
        "c_mk": f(inp["cache_mem_k"][:, 2 * c:2 * c + 2].reshape(2, 2, NMEM, 256)),
        "c_mv": f(inp["cache_mem_v"][:, 2 * c:2 * c + 2].reshape(2, 2, NMEM, 256)),
    }
    return m


def kernel(**inp):
    hc = _host_consts()
    f = lambda a: np.ascontiguousarray(np.asarray(a, dtype=np.float32))
    shared = {
        "ret_w_in": f(inp["ret_w_in"][0]), "ret_gn_g": f(inp["ret_gn_g"]), "diff_w_in": f(inp["diff_w_in"][0]),
        "lamv": f(np.stack([np.asarray(inp["diff_lambda_q1"])[0], np.asarray(inp["diff_lambda_k1"])[0],
                            np.asarray(inp["diff_lambda_q2"])[0], np.asarray(inp["diff_lambda_k2"])[0]])),
        "subln": f(np.asarray(inp["diff_subln_g"])[0].reshape(128, 1)),
        "w_mem_kv": f(inp["w_mem_kv"]), "w_o": f(inp["w_o"]),
        "ln1_g": f(inp["ln1_g"]), "ln1_b": f(inp["ln1_b"]), "w_gate": f(inp["w_gate"]), "w_up": f(inp["w_up"]),
        "w_down": f(inp["w_down"]), "ln2_g": f(inp["ln2_g"]), "ln2_b": f(inp["ln2_b"]),
    }
    for n in CONST_SHAPES:
        shared[n] = hc[n]
    if "nc" not in _CACHE:
        _CACHE["nc"] = build_program()[0]
    nc = _CACHE["nc"]
    in_maps = []
    for c in range(8):
        m = dict(shared)
        m.update(_core_inputs(c, inp, hc))
        in_maps.append(m)
    res = run_bass_kernel_spmd(nc, in_maps, core_ids=list(range(8)))
    R = res.results
    y_p = np.stack([R[c]["y_p"] for c in range(8)])
    y_s = np.concatenate([R[c]["y_s"].reshape(2, 64, D) for c in range(8)])
    rs_p = np.stack([R[c]["rs_p"] for c in range(8)])[None]
    rs_s = np.concatenate([R[c]["rs_s"] for c in range(8)])[None]
    dk_p = np.stack([R[c]["dk_p"].reshape(SEQ, H, 128) for c in range(8)])[None]
    dv_p = np.stack([R[c]["dv_p"].reshape(SEQ, H, 128) for c in range(8)])[None]
    dk_s = np.concatenate([R[c]["dk_s"].reshape(2, 64, H, 128) for c in range(8)])[None]
    dv_s = np.concatenate([R[c]["dv_s"].reshape(2, 64, H, 128) for c in range(8)])[None]
    mk_p = np.stack([R[c]["mk_p"].reshape(2, NMEM, 4, 64) for c in range(8)], axis=1)
    mv_p = np.stack([R[c]["mv_p"].reshape(2, NMEM, 4, 64) for c in range(8)], axis=1)
    outs = (y_p, y_s, rs_p, rs_s, dk_p, dv_p, dk_s, dv_s, mk_p, mv_p)
    return tuple(np.ascontiguousarray(o, dtype=np.float32) for o in outs)
```

```python
import math
import numpy as np
import concourse.bass as bass
import concourse.mybir as mybir
from concourse.bass_utils import run_bass_kernel_spmd
from contextlib import ExitStack

F32 = mybir.dt.float32
BF16 = mybir.dt.bfloat16
AF = mybir.ActivationFunctionType
ALU = mybir.AluOpType
AX = mybir.AxisListType

PE, ACT, DVE, POOL, SP = "pe", "act", "dve", "pool", "sp"
PDEPTH = 3

D = 1024
KC = 8
SEQ = 4096
TT = 512
MIX = 768
H = 6
DFF = 2816
FC = 22
NMEM = 256
PAST = 1024
ALPHA = 4.0 ** 0.25
LN_EPS = 1e-5
LAM_INIT = 0.8 - 0.6 * math.exp(-0.3 * 1)
RET_COLS = 3328
DIFF_COLS = 2560
NROPE = 224


class Buf:
    __slots__ = ("name", "w", "r", "ov", "excl")

    def __init__(self, name, excl=False):
        self.name = name
        self.w = None
        self.r = {}
        self.ov = []
        self.excl = excl


def alias(a_list, b_list):
    for a in a_list:
        for b in b_list:
            a.ov.append(b)
            b.ov.append(a)


class Op:
    __slots__ = ("eng", "fn", "deps", "dma", "ev", "idx")


class Prog:
    def __init__(self, nc, es):
        self.nc = nc
        self.es = es
        self.ops = []
        self.engines = {PE: nc.tensor, ACT: nc.scalar, DVE: nc.vector, POOL: nc.gpsimd, SP: nc.sync}
        self.esem = {}
        for e in (PE, ACT, DVE, POOL):
            self.esem[e] = es.enter_context(nc.semaphore("ev_" + e))
        self.stores = []
        self.nsem = 0

    def new_dma_sem(self, name):
        self.nsem += 1
        return self.es.enter_context(self.nc.semaphore(f"dq_{name}_{self.nsem}"))

    def op(self, eng, fn, reads=(), writes=(), dma_sem=None, ndma=1, is_store=False):
        idx = len(self.ops)
        o = Op()
        o.eng = eng
        o.fn = fn
        o.idx = idx
        o.dma = (dma_sem, ndma) if dma_sem is not None else None
        deps = set()
        is_dma = dma_sem is not None

        def add(j, kind):
            if j is None:
                return
            d = self.ops[j]
            if (not is_dma) and d.dma is None and d.eng == eng:
                if eng == PE:
                    return
            deps.add(j)

        for b in reads:
            add(b.w, "raw")
            if b.excl:
                for k_, j in b.r.items():
                    if k_ != eng:
                        deps.add(j)
        for b in writes:
            for bb in [b] + b.ov:
                add(bb.w, "waw")
                for j in bb.r.values():
                    add(j, "war")
        rkey = ("dma", idx) if is_dma else eng
        for b in reads:
            b.r[rkey] = idx
        for b in writes:
            b.w = idx
            b.r = {}
        o.deps = deps
        o.ev = None
        self.ops.append(o)
        if is_store:
            self.stores.append(idx)
        return idx

    def emit(self):
        import os
        if os.environ.get("KTRUNC"):
            n = int(os.environ["KTRUNC"])
            self.ops = self.ops[:n]
            self.stores = [j for j in self.stores if j < n]
        needed = set()
        for o in self.ops:
            needed |= o.deps
        ecount = {e: 0 for e in self.esem}
        dcount = {}
        waited = {e: {} for e in self.engines}
        nwait = 0
        for o in self.ops:
            eng = self.engines[o.eng]
            w = {}
            for j in o.deps:
                sem, val = self.ops[j].ev
                k = id(sem)
                if k not in w or w[k][1] < val:
                    w[k] = (sem, val)
            wd = waited[o.eng]
            for k, (sem, val) in w.items():
                if wd.get(k, 0) >= val:
                    continue
                eng.wait_ge(sem, val)
                nwait += 1
                wd[k] = val
            if o.dma is not None:
                sem, n = o.dma
                o.fn(sem)
                c = dcount.get(id(sem), 0) + n
                dcount[id(sem)] = c
                o.ev = (sem, 16 * c)
            else:
                ins = o.fn()
                if o.idx in needed:
                    sem = self.esem[o.eng]
                    ins.then_inc(sem, 1)
                    ecount[o.eng] += 1
                    o.ev = (sem, ecount[o.eng])
        sp = self.engines[SP]
        w = {}
        for j in self.stores:
            sem, val = self.ops[j].ev
            k = id(sem)
            if k not in w or w[k][1] < val:
                w[k] = (sem, val)
        for k, (sem, val) in w.items():
            sp.wait_ge(sem, val)
        return dict(n_ops=len(self.ops), n_wait=nwait, ecount=ecount)


def _host_consts():
    c = {}
    c["ident"] = np.eye(128, dtype=np.float32)

    def rope_tab(pos):
        pos = pos.astype(np.float32)
        inv_r = np.exp(np.float32(-math.log(10000.0)) * np.arange(64, dtype=np.float32) * np.float32(2.0 / 128)).astype(np.float32)
        ang_r = (pos[:, None] * inv_r[None, :]).astype(np.float32).astype(np.float64)
        inv_d = np.exp(np.float32(-math.log(500000.0)) * np.arange(8, dtype=np.float32) * np.float32(2.0 / 16)).astype(np.float32)
        ang_d = (pos[:, None] * inv_d[None, :]).astype(np.float32).astype(np.float64)
        t = np.concatenate([np.cos(ang_r), -np.sin(ang_r), np.sin(ang_r),
                            np.cos(ang_d), np.cos(ang_d), -np.sin(ang_d), np.sin(ang_d)], axis=1)
        return np.ascontiguousarray(t.astype(np.float32))

    c["rope_p"] = rope_tab(np.arange(SEQ))
    ps = PAST + np.arange(64)
    c["rope_s"] = rope_tab(np.concatenate([ps, ps]))

    log_g = np.log(1.0 - np.exp2(-5.0 - np.arange(H, dtype=np.float64)))
    s = 128.0 ** -0.5
    i = np.arange(128)
    same = (i[:, None] // 64) == (i[None, :] // 64)
    Dp = np.zeros((H, 128, 128))
    for h in range(H):
        dd = np.exp(np.abs(i[:, None] - i[None, :]) * log_g[h])
        causal = np.where(i[:, None] >= i[None, :], dd, 0.0)
        Dp[h] = np.where(same, dd, causal)
    c["DT_p"] = np.ascontiguousarray((Dp.transpose(2, 0, 1) * s).reshape(128, H * 128).astype(np.float32))
    xi_p = np.exp((i[None, :] + 1.0) * log_g[:, None])
    c["xi_p"] = np.ascontiguousarray(xi_p.reshape(1, H * 128).astype(np.float32))
    zeta_p = np.exp((127.0 - i)[:, None] * log_g[None, :]) * s
    c["zeta_p"] = np.ascontiguousarray(np.repeat(zeta_p, 128, axis=1).astype(np.float32))
    c["dec_p"] = [float(np.exp(128.0 * log_g[h])) for h in range(H)]
    Ds = np.zeros((H, 128, 128))
    for h in range(H):
        dd = np.exp(np.abs(i[:, None] - i[None, :]) * log_g[h])
        Ds[h] = np.where(same, dd, 0.0)
    c["DT_s"] = np.ascontiguousarray((Ds.transpose(2, 0, 1) * s).reshape(128, H * 128).astype(np.float32))
    ii = i % 64
    xi_s = np.exp((ii[None, :] + 1.0) * log_g[:, None])
    xiA = np.where(i[None, :] < 64, xi_s, 0.0)
    xiB = np.where(i[None, :] >= 64, xi_s, 0.0)
    c["xi_sA"] = np.ascontiguousarray(xiA.reshape(1, H * 128).astype(np.float32))
    c["xi_sB"] = np.ascontiguousarray(xiB.reshape(1, H * 128).astype(np.float32))
    zeta_s = np.exp((63.0 - ii)[:, None] * log_g[None, :]) * s
    zA = np.where(i[:, None] < 64, zeta_s, 0.0)
    zB = np.where(i[:, None] >= 64, zeta_s, 0.0)
    c["zeta_sA"] = np.ascontiguousarray(np.repeat(zA, 128, axis=1).astype(np.float32))
    c["zeta_sB"] = np.ascontiguousarray(np.repeat(zB, 128, axis=1).astype(np.float32))
    c["dec_s"] = [float(np.exp(64.0 * log_g[h])) for h in range(H)]
    return c


CONST_SHAPES = {
    "ident": [128, 128], "rope_p": [SEQ, NROPE], "rope_s": [128, NROPE],
    "DT_p": [128, 768], "xi_p": [1, 768], "zeta_p": [128, 768],
    "DT_s": [128, 768], "xi_sA": [1, 768], "xi_sB": [1, 768], "zeta_sA": [128, 768], "zeta_sB": [128, 768],
}

IN_SHAPES = {
    "xp": [SEQ, D], "xs": [128, D], "memp": [NMEM, D],
    "c_ret": [2, H, 128, 128], "c_dk": [2, PAST, MIX], "c_dv": [2, PAST, MIX],
    "c_mk": [2, 2, NMEM, 256], "c_mv": [2, 2, NMEM, 256],
    "ret_w_in": [D, RET_COLS], "ret_gn_g": [1, MIX], "diff_w_in": [D, DIFF_COLS],
    "lamv": [4, 64], "subln": [128, 1],
    "w_mem_kv": [2, D, 512], "w_o": [2, D, D],
    "ln1_g": [2, D], "ln1_b": [2, D], "w_gate": [2, D, DFF], "w_up": [2, D, DFF], "w_down": [2, DFF, D],
    "ln2_g": [2, D], "ln2_b": [2, D],
}

OUT_SHAPES = {
    "y_p": [SEQ, D], "y_s": [128, D], "rs_p": [H, 128, 128], "rs_s": [2, H, 128, 128],
    "dk_p": [SEQ, MIX], "dv_p": [SEQ, MIX], "dk_s": [128, MIX], "dv_s": [128, MIX],
    "mk_p": [2, NMEM, 256], "mv_p": [2, NMEM, 256],
}


def build_program(n_ptiles=8, do_sample=True, n_layers=2, dbg=None):
    nc = bass.Bass("TRN2", target_bir_lowering=False)
    hc = _host_consts()
    I = {}
    for n, s in list(IN_SHAPES.items()) + list(CONST_SHAPES.items()):
        I[n] = nc.dram_tensor(n, s, F32, kind="ExternalInput").ap()
    O = {}
    for n, s in OUT_SHAPES.items():
        O[n] = nc.dram_tensor(n, s, F32, kind="ExternalOutput").ap()
    KTd = nc.dram_tensor("KTd", [H, 128, SEQ], BF16, kind="Internal").ap()
    Vd = nc.dram_tensor("Vd", [H, 128, 32, 128], BF16, kind="Internal").ap()
    NWBLK = 50
    Wd = nc.dram_tensor("Wd", [NWBLK, 128, 5632], BF16, kind="Internal").ap()
    Wd_B = [Buf(f"Wd{i}") for i in range(NWBLK)]
    KTd_B = [Buf(f"KTd{h}") for h in range(H)]
    Vd_B = [Buf(f"Vd{h}") for h in range(H)]
    DBG = {}
    if dbg:
        for n, s in dbg.items():
            DBG[n] = nc.dram_tensor(n, s, F32, kind="ExternalOutput").ap()

    es = ExitStack()
    with es:
        P = Prog(nc, es)

        def sb(name, shape, dt):
            return es.enter_context(nc.sbuf_tensor("sb_" + name, shape, dt))

        NKROT = 4
        xres = sb("xres", [128, 4, D], F32)
        xres_B = [[Buf(f"xres{s}_{n}") for n in range(4)] for s in range(4)]
        xbf = sb("xbf", [128, 2, D], BF16)
        xbf_B = [Buf("xbf0"), Buf("xbf1")]
        actT = sb("actT", [128, KC, TT], BF16)
        actT_B = [[Buf(f"actT{c}_{s}") for s in range(4)] for c in range(KC)]
        NW = 3
        wslot = [sb(f"wslot{i}", [128, 5632], BF16) for i in range(NW)]
        wslot_B = [Buf(f"wslot{i}") for i in range(NW)]
        wsem = [P.new_dma_sem(f"w{i}") for i in range(NW)]
        arenaA = sb("arenaA", [128, FC * TT], BF16)
        hff = arenaA[:, :].rearrange("p (f t) -> p f t", t=TT)
        hff_B = [Buf(f"hff{f}") for f in range(FC)]
        qT = arenaA[:, 0:3072].rearrange("p (h t) -> p h t", t=TT)
        qxT = arenaA[:, 3072:6144].rearrange("p (h t) -> p h t", t=TT)
        kT = arenaA[:, 6144:9216].rearrange("p (h t) -> p h t", t=TT)
        mqT = arenaA[:, 9216:10240].rearrange("p (m t) -> p m t", t=TT)
        qT_B = [[Buf(f"qT{h}_{s}") for s in range(4)] for h in range(H)]
        qxT_B = [[Buf(f"qxT{h}_{s}") for s in range(4)] for h in range(H)]
        kT_B = [[Buf(f"kT{h}_{s}") for s in range(4)] for h in range(H)]
        mqT_B = [Buf("mqT0"), Buf("mqT1")]
        krot_B = [Buf(f"krot{i}") for i in range(NKROT)]
        for h in range(H):
            alias(qT_B[h], [hff_B[h]])
            alias(qxT_B[h], [hff_B[6 + h]])
            alias(kT_B[h], [hff_B[12 + h]])
        alias([mqT_B[0]], [hff_B[18]])
        alias([mqT_B[1]], [hff_B[19]])
        arenaB = sb("arenaB", [128, 6144], BF16)
        kz = arenaB[:, 0:3072].rearrange("p (s c) -> p s c", s=4)
        sg = arenaB[:, 3072:6144].rearrange("p (s c) -> p s c", s=4)
        kz_B = [Buf(f"kz{s}") for s in range(4)]
        sg_B = [Buf(f"sg{s}") for s in range(4)]
        NKV = 3
        kvK = [arenaB[:, 2048 * i:2048 * i + 1024] for i in range(NKV)]
        kvV = [arenaB[:, 2048 * i + 1024:2048 * i + 2048].rearrange("p (k e) -> p k e", e=128) for i in range(NKV)]
        kv_B = [Buf(f"kv{i}") for i in range(NKV)]
        kvsem = [P.new_dma_sem(f"kv{i}") for i in range(NKV)]
        alias(kv_B, kz_B + sg_B)
        vtok = sb("vtok", [128, 4, MIX], BF16)
        vtok_B = [Buf(f"vtok{s}") for s in range(4)]
        hm = sb("hm", [128, 2, MIX], BF16)
        hm_B = [Buf("hm0"), Buf("hm1")]
        NT32 = 4
        t32 = sb("t32", [128, NT32, 384], F32)
        t32_B = [Buf(f"t32_{i}") for i in range(NT32)]
        t32sem = [P.new_dma_sem(f"t32_{i}") for i in range(NT32)]
        t32flat = t32[:, :, :].rearrange("p a c -> p (a c)")
        accA = t32flat[:, 0:512]
        accB = t32flat[:, 512:1024]
        accA_B, accB_B = Buf("accA"), Buf("accB")
        alias([accA_B], [t32_B[0], t32_B[1]])
        alias([accB_B], [t32_B[1], t32_B[2]])
        ropeA2 = sb("ropeA", [128, 2, 384], F32)
        ropeB2 = sb("ropeB", [128, 2, 384], F32)
        ropeA2_B = [Buf("ropeA0"), Buf("ropeA1")]
        ropeB2_B = [Buf("ropeB0"), Buf("ropeB1")]
        krot4 = sb("krot4", [128, NKROT, 384], BF16)
        gnt = sb("gnt", [128, 2, 384], F32)
        gnt_B = [Buf("gnt0"), Buf("gnt1")]
        xbfn = sb("xbfn", [128, 4, D], BF16)
        xbfn_B = [Buf(f"xbfn{s}") for s in range(4)]
        xbfnsem = P.new_dma_sem("xbfn")
        sgate = sb("sgate", [128, 2, TT], BF16)
        sgate_B = [Buf("sgate0"), Buf("sgate1")]
        Pb = sb("Pb", [128, 6, TT], BF16)
        Pb_B = [Buf(f"Pb{i}") for i in range(6)]
        MTsb = sb("MTsb", [128, 2, 384], BF16)
        MTsb_B = [Buf("MT0"), Buf("MT1")]
        att_r = sb("att_r", [128, 2, TT], F32)
        att_r_B = [Buf("att_r0"), Buf("att_r1")]
        att_o = sb("att_o", [128, 2, TT], F32)
        att_o_B = [Buf("att_o0"), Buf("att_o1")]
        lnp = sb("lnp", [128, 2, D], F32)
        lnp_B = Buf("lnp")
        lnsem = P.new_dma_sem("lnp")
        ropet = sb("ropet", [128, 2, 4, NROPE], F32)
        ropet_B = [Buf("ropet0"), Buf("ropet1")]
        ropesem = [P.new_dma_sem("rope0"), P.new_dma_sem("rope1")]
        xsem = [P.new_dma_sem(f"x{s}") for s in range(4)]
        ysem = [P.new_dma_sem(f"y{s}") for s in range(4)]
        krot = krot4
        ident = sb("ident", [128, 128], BF16)
        ones_bf = sb("ones_bf", [128, 128], BF16)
        ones_f = sb("ones_f", [128, 128], F32)
        epsc = sb("epsc", [128, 1], F32)
        cst_B = Buf("consts")
        DT = sb("DT", [128, 768], F32)
        xi = sb("xi", [128, 2, 768], F32)
        zeta = sb("zeta", [128, 2, 768], F32)
        rett_B = Buf("ret_tables")
        gng = sb("gng", [128, 768], F32)
        R32 = sb("R32", [128, 2, 768], F32)
        Rbf = sb("Rbf", [128, 2, 768], BF16)
        R_B = [[Buf(f"R{g}_{h}") for h in range(H)] for g in range(2)]
        Rbf_B = [[Buf(f"Rbf{g}_{h}") for h in range(H)] for g in range(2)]
        memKT = sb("memKT", [128, 2, 4, NMEM], BF16)
        memV = sb("memV", [128, 2, 2, 4, 128], BF16)
        mem_B = [Buf("mem0"), Buf("mem1")]
        memT = sb("memT", [128, KC, NMEM], BF16)
        memT_B = Buf("memT")
        memst = sb("memst", [128, 512], F32)
        memst_B = Buf("memst")
        onespad = sb("onespad", [128, 2, 128], BF16)
        lam = sb("lam", [128, 8], F32)
        lam_B = Buf("lam")
        lamt = sb("lamt", [128, 4, 64], F32)
        sublg = sb("sublg", [128, 2], F32)
        stat = sb("stat", [128, 4, 8, 6], F32)
        stat_B = [Buf(f"stat{s}") for s in range(4)]
        aggr = sb("aggr", [128, 4, 8, 4], F32)
        aggr_B = [Buf(f"aggr{s}") for s in range(4)]
        _bs = {}

        def bsem(name):
            if name not in _bs:
                _bs[name] = P.new_dma_sem(name)
            return _bs[name]
        kvwsem = P.new_dma_sem("kvw")

        ps = [es.enter_context(nc.psum_tensor(f"ps{i}", [128, 512], F32)) for i in range(8)]
        ps_B = [Buf(f"ps{i}", excl=True) for i in range(8)]

        def psbf(i):
            return ps[i][:, :].bitcast(BF16)

        eng = P.engines

        def dma(q, out, in_, reads, writes, sem, is_store=False):
            e = eng[q]
            P.op(q, lambda s: e.dma_start(out=out, in_=in_).then_inc(s, 16), reads=reads, writes=writes,
                 dma_sem=sem, ndma=1, is_store=is_store)

        def act(out, in_, func, reads, writes, scale=1.0, bias=None):
            if bias is None:
                P.op(ACT, lambda: nc.scalar.activation(out=out, in_=in_, func=func, scale=scale), reads=reads, writes=writes)
            else:
                P.op(ACT, lambda: nc.scalar.activation(out=out, in_=in_, func=func, scale=scale, bias=bias), reads=reads, writes=writes)

        def tt(e, out, in0, in1, op, reads, writes):
            en = eng[e]
            P.op(e, lambda: en.tensor_tensor(out=out, in0=in0, in1=in1, op=op), reads=reads, writes=writes)

        def stt(e, out, in0, scalar, in1, op0, op1, reads, writes):
            en = eng[e]
            P.op(e, lambda: en.scalar_tensor_tensor(out=out, in0=in0, scalar=scalar, in1=in1, op0=op0, op1=op1),
                 reads=reads, writes=writes)

        def cp(e, out, in_, reads, writes):
            if e == ACT:
                act(out, in_, AF.Copy, reads, writes)
            else:
                en = eng[e]
                P.op(e, lambda: en.tensor_copy(out=out, in_=in_), reads=reads, writes=writes)

        def mms(out, pairs, reads, writes):
            n = len(pairs)

            def f():
                for i, (l, r) in enumerate(pairs):
                    ins = nc.tensor.matmul(out, l, r, start=(i == 0), stop=(i == n - 1))
                return ins
            P.op(PE, f, reads=reads, writes=writes)

        def transposes(bank, items, reads):
            pv = psbf(bank)

            def f():
                for (co, a) in items:
                    ins = nc.tensor.transpose(out=pv[:, co:co + 128], in_=a, identity=ident[:, :])
                return ins
            P.op(PE, f, reads=reads + [cst_B], writes=[ps_B[bank]])

        rr = {"tp": 0, "mm": 0}

        def tp_bank():
            rr["tp"] ^= 1
            return rr["tp"]

        def mm_bank():
            rr["mm"] = (rr["mm"] + 1) % 3
            return 2 + rr["mm"]

        wq = []

        def wblock(name, loads, nelem):
            wq.append((name, loads, nelem))

        wstate = {"issued": 0, "next": 0}

        wosem = [P.new_dma_sem(f"wo{i}") for i in range(NW)]

        def w_issue_upto(n):
            while wstate["issued"] < min(n, len(wq)):
                i = wstate["issued"]
                name, loads, nelem = wq[i]
                si = i % NW
                bi = i % NWBLK
                npass = len(wq) // NWBLK
                two = npass >= 3
                from_f32 = (i < NWBLK) or (two and i < 2 * NWBLK and bi % 2 == 1)
                publish = (npass > 1) and ((i < NWBLK and not (two and bi % 2 == 1)) or (two and NWBLK <= i < 2 * NWBLK and bi % 2 == 1))
                if from_f32:
                    ld = loads(wslot[si])

                    def f(s, ld=ld):
                        for (o, a) in ld:
                            nc.gpsimd.dma_start(out=o, in_=a).then_inc(s, 16)
                    P.op(POOL, f, writes=[wslot_B[si]], dma_sem=wsem[si], ndma=len(ld))
                    if publish:
                        dma(SP, Wd[bi, :, 0:nelem], wslot[si][:, 0:nelem], [wslot_B[si]], [Wd_B[bi]], wosem[si])
                else:
                    def f(s, si=si, bi=bi, nelem=nelem):
                        nc.gpsimd.dma_start(out=wslot[si][:, 0:nelem], in_=Wd[bi, :, 0:nelem]).then_inc(s, 16)
                    P.op(POOL, f, reads=[Wd_B[bi]], writes=[wslot_B[si]], dma_sem=wsem[si], ndma=1)
                wstate["issued"] += 1

        def w_next(name):
            i = wstate["next"]
            assert wq[i][0] == name, (wq[i][0], name)
            assert n_layers == 1 or len(wq) % NWBLK == 0
            w_issue_upto(i + NW)
            wstate["next"] += 1
            return i % NW

        def kcview(w2d, c0, ncols):
            return w2d.rearrange("(k p) c -> p k c", p=128)[:, :, c0:c0 + ncols]

        def plan_weights(tiles):
            for tl in tiles:
                for L in range(n_layers):
                    if L == 0:
                        w = I["ret_w_in"]
                        for j in range(8):
                            wblock(f"in{L}_{j}", (lambda j=j, w=w: (lambda sl: [(sl[:, 0:8 * 384].rearrange("p (k c) -> p k c", k=8), kcview(w, 384 * j, 384))]))(), 8 * 384)
                        wblock(f"mq{L}", (lambda w=w: (lambda sl: [(sl[:, 0:8 * 256].rearrange("p (k c) -> p k c", k=8), kcview(w, 3072, 256))]))(), 8 * 256)
                    else:
                        w = I["diff_w_in"]
                        for j in range(6):
                            wblock(f"in{L}_{j}", (lambda j=j, w=w: (lambda sl: [(sl[:, 0:8 * 384].rearrange("p (k c) -> p k c", k=8), kcview(w, 384 * j, 384))]))(), 8 * 384)
                        wblock(f"mq{L}", (lambda w=w: (lambda sl: [(sl[:, 0:8 * 256].rearrange("p (k c) -> p k c", k=8), kcview(w, 2304, 256))]))(), 8 * 256)
                    for n in range(2):
                        wblock(f"wo{L}_{n}", (lambda n=n, L=L: (lambda sl: [(sl[:, 0:8 * 512].rearrange("p (k c) -> p k c", k=8), kcview(I["w_o"][L], 512 * n, 512))]))(), 8 * 512)
                    for b in range(11):
                        wblock(f"gu{L}_{b}", (lambda b=b, L=L: (lambda sl: [
                            (sl[:, 0:2048].rearrange("p (k c) -> p k c", k=8), kcview(I["w_gate"][L], 256 * b, 256)),
                            (sl[:, 2048:4096].rearrange("p (k c) -> p k c", k=8), kcview(I["w_up"][L], 256 * b, 256))]))(), 4096)
                    for n in range(4):
                        wblock(f"dn{L}_{n}", (lambda n=n, L=L: (lambda sl: [(sl[:, 0:FC * 256].rearrange("p (k c) -> p k c", k=FC), kcview(I["w_down"][L], 256 * n, 256))]))(), FC * 256)

        def startup():
            dma(POOL, ident[:, :], I["ident"], [], [cst_B], bsem("ident"))
            P.op(DVE, lambda: nc.vector.memset(ones_bf[:, :], 1.0), writes=[cst_B])
            P.op(DVE, lambda: nc.vector.memset(ones_f[:, :], 1.0), writes=[cst_B])
            P.op(DVE, lambda: nc.vector.memset(epsc[:, :], LN_EPS), writes=[cst_B])
            P.op(DVE, lambda: nc.vector.memset(onespad[:, :, :], 0.0), writes=[cst_B])
            P.op(DVE, lambda: nc.vector.memset(onespad[:, 0, 0:64], 1.0), writes=[cst_B])
            P.op(DVE, lambda: nc.vector.memset(onespad[:, 1, 64:128], 1.0), writes=[cst_B])
            P.op(DVE, lambda: nc.vector.memset(memKT[:, :, :, :], 0.0), writes=mem_B)
            P.op(DVE, lambda: nc.vector.memset(memV[:, :, :, :, :], 0.0), writes=mem_B)
            dma(SP, gng[:, :], I["ret_gn_g"][0].partition_broadcast(128), [], [cst_B], bsem("cst"))
            dma(SP, lamt[:, :, :], I["lamv"].rearrange("a d -> (a d)").partition_broadcast(128).rearrange("p (a d) -> p a d", a=4), [], [lam_B], bsem("lam"))
            dma(SP, sublg[:, 0:1], I["subln"], [], [lam_B], bsem("lam"))
            P.op(DVE, lambda: nc.vector.memset(lam[:, :], 0.0), writes=[lam_B])
            tt(DVE, lamt[:, 0, :], lamt[:, 0, :], lamt[:, 1, :], ALU.mult, [lam_B], [lam_B])
            tt(DVE, lamt[:, 2, :], lamt[:, 2, :], lamt[:, 3, :], ALU.mult, [lam_B], [lam_B])
            P.op(DVE, lambda: nc.vector.tensor_reduce(out=lam[:, 0:1], in_=lamt[:, 0, :], axis=AX.X, op=ALU.add), reads=[lam_B], writes=[lam_B])
            P.op(DVE, lambda: nc.vector.tensor_reduce(out=lam[:, 1:2], in_=lamt[:, 2, :], axis=AX.X, op=ALU.add), reads=[lam_B], writes=[lam_B])
            act(lam[:, 2:4], lam[:, 0:2], AF.Exp, [lam_B], [lam_B])
            tt(DVE, lam[:, 4:5], lam[:, 2:3], lam[:, 3:4], ALU.subtract, [lam_B], [lam_B])
            P.op(DVE, lambda: nc.vector.tensor_scalar(out=lam[:, 5:6], in0=lam[:, 4:5], scalar1=LAM_INIT, scalar2=-1.0, op0=ALU.add, op1=ALU.mult), reads=[lam_B], writes=[lam_B])
            P.op(DVE, lambda: nc.vector.tensor_scalar(out=sublg[:, 1:2], in0=sublg[:, 0:1], scalar1=(1.0 - LAM_INIT), scalar2=None, op0=ALU.mult), reads=[lam_B], writes=[lam_B])

        def load_ret_tables(kind):
            if kind == "p":
                dma(SP, DT[:, :], I["DT_p"], [], [rett_B], bsem("rett"))
                dma(SP, xi[:, 0, :], I["xi_p"][0].partition_broadcast(128), [], [rett_B], bsem("rett"))
                dma(SP, zeta[:, 0, :], I["zeta_p"], [], [rett_B], bsem("rett"))
            else:
                dma(SP, DT[:, :], I["DT_s"], [], [rett_B], bsem("rett"))
                dma(SP, xi[:, 0, :], I["xi_sA"][0].partition_broadcast(128), [], [rett_B], bsem("rett"))
                dma(SP, xi[:, 1, :], I["xi_sB"][0].partition_broadcast(128), [], [rett_B], bsem("rett"))
                dma(SP, zeta[:, 0, :], I["zeta_sA"], [], [rett_B], bsem("rett"))
                dma(SP, zeta[:, 1, :], I["zeta_sB"], [], [rett_B], bsem("rett"))

        def to_featmajor(src_ap, src_bufs, nchunk, dst_fn, dst_bufs_fn, cast_eng=DVE, evac_eng=ACT):
            xi_ = rr.setdefault("xbf", 0)
            rr["xbf"] = xi_ ^ 1
            xb = xbf[:, xi_, 0:128 * nchunk]
            cp(cast_eng, xb, src_ap, src_bufs, [xbf_B[xi_]])
            bank = tp_bank()
            transposes(bank, [(128 * c, xbf[:, xi_, 128 * c:128 * (c + 1)]) for c in range(nchunk)], [xbf_B[xi_]])
            pv = psbf(bank)[:, 0:128 * nchunk].rearrange("p (c t) -> p c t", c=nchunk)
            cp(evac_eng, dst_fn(), pv, [ps_B[bank]], dst_bufs_fn())

        def mem_setup_prompt():
            for mt in range(2):
                dma(SP, xres[:, mt, :], I["memp"][128 * mt:128 * (mt + 1), :], [], xres_B[mt], xsem[mt])
                to_featmajor(xres[:, mt, :], xres_B[mt], KC, lambda mt=mt: memT[:, :, 128 * mt:128 * (mt + 1)], lambda: [memT_B])
            for L in range(n_layers):
                wv = kcview(I["w_mem_kv"][L], 0, 512)
                mwB = hff_B[0:8]
                P.op(POOL, lambda s, wv=wv: nc.gpsimd.dma_start(out=arenaA[:, 0:4096].rearrange("p (k c) -> p k c", k=8), in_=wv).then_inc(s, 16),
                     writes=mwB, dma_sem=bsem("memw"), ndma=1)
                W = arenaA[:, 0:4096].rearrange("p (k c) -> p k c", k=8)
                for mt in range(2):
                    b = mm_bank()
                    mms(ps[b][:, :], [(memT[:, c, 128 * mt:128 * (mt + 1)], W[:, c, :]) for c in range(KC)], [memT_B] + mwB, [ps_B[b]])
                    cp(ACT, memst[:, :], ps[b][:, :], [ps_B[b]], [memst_B])
                    dma(SP, O["mk_p"][L, 128 * mt:128 * (mt + 1), :], memst[:, 0:256], [memst_B], [], bsem("memst"), is_store=True)
                    dma(SP, O["mv_p"][L, 128 * mt:128 * (mt + 1), :], memst[:, 256:512], [memst_B], [], bsem("memst"), is_store=True)
                    for m in range(4):
                        cp(DVE, memV[:, L, mt, m, 64 * (m % 2):64 * (m % 2) + 64], memst[:, 256 + 64 * m:256 + 64 * (m + 1)], [memst_B], [mem_B[L]])
                for mc in range(2):
                    b = mm_bank()
                    mms(ps[b][:, 0:256], [(W[:, c, 128 * mc:128 * (mc + 1)], memT[:, c, :]) for c in range(KC)], [memT_B] + mwB, [ps_B[b]])
                    for hh in range(2):
                        m = 2 * mc + hh
                        cp(ACT, memKT[64 * hh:64 * hh + 64, L, m, :], ps[b][64 * hh:64 * hh + 64, 0:256], [ps_B[b]], [mem_B[L]])

        def mem_setup_sample(L, bidx):
            for mt in range(2):
                dma(SP, memst[:, 0:256], I["c_mk"][L, bidx, 128 * mt:128 * (mt + 1), :], [], [memst_B], bsem("memst"))
                dma(SP, memst[:, 256:512], I["c_mv"][L, bidx, 128 * mt:128 * (mt + 1), :], [], [memst_B], bsem("memst"))
                for m in range(4):
                    cp(DVE, memV[:, L, mt, m, 64 * (m % 2):64 * (m % 2) + 64], memst[:, 256 + 64 * m:256 + 64 * (m + 1)], [memst_B], [mem_B[L]])
                xi_ = rr.setdefault("xbf", 0)
                rr["xbf"] = xi_ ^ 1
                cp(DVE, xbf[:, xi_, 0:256], memst[:, 0:256], [memst_B], [xbf_B[xi_]])
                bank = tp_bank()
                transposes(bank, [(128 * c, xbf[:, xi_, 128 * c:128 * (c + 1)]) for c in range(2)], [xbf_B[xi_]])
                pv = psbf(bank)
                for mc in range(2):
                    for hh in range(2):
                        m = 2 * mc + hh
                        cp(ACT, memKT[64 * hh:64 * hh + 64, L, m, 128 * mt:128 * (mt + 1)], pv[64 * hh:64 * hh + 64, 128 * mc:128 * (mc + 1)], [ps_B[bank]], [mem_B[L]])

        def mem_attention(L, NTs, colgroups):
            for (c0, ncol, setup) in colgroups:
                if setup is not None:
                    setup()
                for mc in range(2):
                    ob, sbk = 4 + mc, 6 + mc
                    first = True
                    pending = None
                    for hh in range(2):
                        m = 2 * mc + hh
                        for mt in range(2):
                            sbank = tp_bank()
                            mms(ps[sbank][:, 0:ncol], [(memKT[:, L, m, 128 * mt:128 * (mt + 1)], mqT[:, mc, c0:c0 + ncol])],
                                [mem_B[L], mqT_B[mc]], [ps_B[sbank]])
                            pi = rr.setdefault("pb", 0)
                            rr["pb"] = (pi + 1) % 4
                            act(Pb[:, pi, 0:ncol], ps[sbank][:, 0:ncol], AF.Exp, [ps_B[sbank]], [Pb_B[pi]], scale=0.125)
                            last = (hh == 1 and mt == 1)

                            def f(first=first, last=last, pi=pi, m=m, mt=mt, hh=hh, ob=ob, sbk=sbk, c0=c0, ncol=ncol):
                                nc.tensor.matmul(ps[ob][:, c0:c0 + ncol], memV[:, L, mt, m, :], Pb[:, pi, 0:ncol], start=first, stop=last)
                                return nc.tensor.matmul(ps[sbk][:, c0:c0 + ncol], onespad[:, hh, :], Pb[:, pi, 0:ncol], start=first, stop=last)
                            if pending is not None:
                                pending()
                            pending = (lambda f=f, pi=pi, ob=ob, sbk=sbk: P.op(PE, f, reads=[mem_B[L], Pb_B[pi], cst_B], writes=[ps_B[ob], ps_B[sbk]]))
                            first = False
                    pending()
                    ri = rr.setdefault("attr", 0)
                    rr["attr"] = ri ^ 1
                    act(att_r[:, ri, 0:ncol], ps[sbk][:, c0:c0 + ncol], AF.Ln, [ps_B[sbk]], [att_r_B[ri]])
                    act(att_r[:, ri, 0:ncol], att_r[:, ri, 0:ncol], AF.Exp, [att_r_B[ri]], [att_r_B[ri]], scale=-1.0)
                    subs = sorted(set(range(c0 // 128, (c0 + ncol - 1) // 128 + 1)))
                    tt(DVE, actT[:, 6 + mc, c0:c0 + ncol], ps[ob][:, c0:c0 + ncol], att_r[:, ri, 0:ncol], ALU.mult,
                       [ps_B[ob], att_r_B[ri]], [actT_B[6 + mc][s] for s in subs])

        def layer_norm(L, which, NTs, final_out=None):
            gk, bk = ("ln1_g", "ln1_b") if which == 1 else ("ln2_g", "ln2_b")
            dma(SP, lnp[:, 0, :], I[gk][L].partition_broadcast(128), [], [lnp_B], lnsem)
            dma(SP, lnp[:, 1, :], I[bk][L].partition_broadcast(128), [], [lnp_B], lnsem)
            for s in range(NTs):
                def bns(s=s):
                    nc.vector.bn_stats(out=stat[:, s, 0, :], in_=xres[:, s, 0:512])
                    return nc.vector.bn_stats(out=stat[:, s, 1, :], in_=xres[:, s, 512:1024])
                P.op(DVE, bns, reads=xres_B[s], writes=[stat_B[s]])
                P.op(DVE, lambda s=s: nc.vector.bn_aggr(out=aggr[:, s, 0, 0:2], in_=stat[:, s, 0:2, :]), reads=[stat_B[s]], writes=[aggr_B[s]])
            agv = aggr[:, 0:NTs, 0, :]
            act(agv[:, :, 2:3], agv[:, :, 1:2], AF.Ln, aggr_B[0:NTs] + [cst_B], aggr_B[0:NTs], bias=epsc[:, :])
            act(agv[:, :, 3:4], agv[:, :, 2:3], AF.Exp, aggr_B[0:NTs], aggr_B[0:NTs], scale=-0.5)
            for s in range(NTs):
                stt(DVE, xres[:, s, :], xres[:, s, :], aggr[:, s, 0, 0:1], lnp[:, 0, :], ALU.subtract, ALU.mult,
                    xres_B[s] + [aggr_B[s], lnp_B], xres_B[s])
                stt(DVE, xres[:, s, :], xres[:, s, :], aggr[:, s, 0, 3:4], lnp[:, 1, :], ALU.mult, ALU.add,
                    xres_B[s] + [aggr_B[s], lnp_B], xres_B[s])
                if final_out is not None:
                    dma(SP, final_out(s), xres[:, s, :], xres_B[s], [], ysem[s], is_store=True)
                else:
                    to_featmajor(xres[:, s, :], xres_B[s], KC, lambda s=s: actT[:, :, 128 * s:128 * (s + 1)],
                                 lambda s=s: [actT_B[c][s] for c in range(KC)])

        def wo_and_ffn(L, NTs, final_out):
            ncol = 128 * NTs
            for n in range(2):
                si = w_next(f"wo{L}_{n}")
                W = wslot[si][:, 0:4096].rearrange("p (k c) -> p k c", k=8)
                for s in range(NTs):
                    b = mm_bank()
                    mms(ps[b][:, :], [(actT[:, c, 128 * s:128 * (s + 1)], W[:, c, :]) for c in range(KC)],
                        [actT_B[c][s] for c in range(KC)] + [wslot_B[si]], [ps_B[b]])
                    xb_ = [xres_B[s][2 * n], xres_B[s][2 * n + 1]]
                    stt(DVE, xres[:, s, 512 * n:512 * (n + 1)], xres[:, s, 512 * n:512 * (n + 1)], ALPHA, ps[b][:, :],
                        ALU.mult, ALU.add, xb_ + [ps_B[b]], xb_)
            layer_norm(L, 1, NTs)
            allact = [actT_B[c][s] for c in range(KC) for s in range(NTs)]
            for bb in range(11):
                si = w_next(f"gu{L}_{bb}")
                Wg = wslot[si][:, 0:2048].rearrange("p (k c) -> p k c", k=8)
                Wu = wslot[si][:, 2048:4096].rearrange("p (k c) -> p k c", k=8)
                for j in range(2):
                    f_ = 2 * bb + j
                    gi = rr.setdefault("gu", 0)
                    rr["gu"] = gi ^ 1
                    gb_, ub_ = 4 + gi, 6 + gi
                    mms(ps[gb_][:, 0:ncol], [(Wg[:, c, 128 * j:128 * (j + 1)], actT[:, c, 0:ncol]) for c in range(KC)],
                        allact + [wslot_B[si]], [ps_B[gb_]])
                    mms(ps[ub_][:, 0:ncol], [(Wu[:, c, 128 * j:128 * (j + 1)], actT[:, c, 0:ncol]) for c in range(KC)],
                        allact + [wslot_B[si]], [ps_B[ub_]])
                    act(sgate[:, gi, 0:ncol], ps[gb_][:, 0:ncol], AF.Silu, [ps_B[gb_]], [sgate_B[gi]])
                    tt(DVE, hff[:, f_, 0:ncol], sgate[:, gi, 0:ncol], ps[ub_][:, 0:ncol], ALU.mult,
                       [sgate_B[gi], ps_B[ub_]], [hff_B[f_]])
            for n in range(4):
                si = w_next(f"dn{L}_{n}")
                W = wslot[si][:, 0:FC * 256].rearrange("p (k c) -> p k c", k=FC)
                for s in range(NTs):
                    b = mm_bank()
                    mms(ps[b][:, 0:256], [(hff[:, f_, 128 * s:128 * (s + 1)], W[:, f_, :]) for f_ in range(FC)],
                        hff_B + [wslot_B[si]], [ps_B[b]])
                    stt(DVE, xres[:, s, 256 * n:256 * (n + 1)], xres[:, s, 256 * n:256 * (n + 1)], ALPHA, ps[b][:, 0:256],
                        ALU.mult, ALU.add, [xres_B[s][n], ps_B[b]], [xres_B[s][n]])
            layer_norm(L, 2, NTs, final_out=final_out)

        def rope_ret(ti, rslot, s, out_bf, out_bufs):
            ri_ = rr.setdefault("ropeab", 0)
            rr["ropeab"] = ri_ ^ 1
            ropeA, ropeB, ropeA_B, ropeB_B = ropeA2[:, ri_, :], ropeB2[:, ri_, :], ropeA2_B[ri_], ropeB2_B[ri_]
            tv = t32[:, ti, :].rearrange("p (h a d) -> p h a d", h=3, a=2)
            tab = ropet[:, rslot, s, :]
            cosb = tab[:, 0:64].unsqueeze(1).unsqueeze(1).to_broadcast([128, 3, 2, 64])
            nsin = tab[:, 64:128].unsqueeze(1).to_broadcast([128, 3, 64])
            sin = tab[:, 128:192].unsqueeze(1).to_broadcast([128, 3, 64])
            Av = ropeA.rearrange("p (h a d) -> p h a d", h=3, a=2)
            Bv = ropeB.rearrange("p (h a d) -> p h a d", h=3, a=2)
            tt(DVE, Bv[:, :, 0, :], tv[:, :, 1, :], nsin, ALU.mult, [t32_B[ti], ropet_B[rslot]], [ropeB_B])
            tt(DVE, Bv[:, :, 1, :], tv[:, :, 0, :], sin, ALU.mult, [t32_B[ti], ropet_B[rslot]], [ropeB_B])
            tt(DVE, Av, tv, cosb, ALU.mult, [t32_B[ti], ropet_B[rslot]], [ropeA_B])
            tt(DVE, out_bf, ropeA, ropeB, ALU.add, [ropeA_B, ropeB_B], out_bufs)

        def rope_diff(ti, rslot, s):
            ri_ = rr.setdefault("ropeab", 0)
            rr["ropeab"] = ri_ ^ 1
            ropeA, ropeB, ropeA_B, ropeB_B = ropeA2[:, ri_, :], ropeB2[:, ri_, :], ropeA2_B[ri_], ropeB2_B[ri_]
            tv = t32[:, ti, :].rearrange("p (g d) -> p g d", g=6)
            tab = ropet[:, rslot, s, :]
            cc = tab[:, 192:208].unsqueeze(1).to_broadcast([128, 6, 16])
            nsin = tab[:, 208:216].unsqueeze(1).to_broadcast([128, 6, 8])
            sin = tab[:, 216:224].unsqueeze(1).to_broadcast([128, 6, 8])
            Av = ropeA[:, 0:96].rearrange("p (g d) -> p g d", g=6)
            Bv = ropeB[:, 0:96].rearrange("p (g d) -> p g d", g=6)
            tt(DVE, Av, tv[:, :, 0:16], cc, ALU.mult, [t32_B[ti], ropet_B[rslot]], [ropeA_B])
            tt(DVE, Bv[:, :, 0:8], tv[:, :, 8:16], nsin, ALU.mult, [t32_B[ti], ropet_B[rslot]], [ropeB_B])
            tt(DVE, Bv[:, :, 8:16], tv[:, :, 0:8], sin, ALU.mult, [t32_B[ti], ropet_B[rslot]], [ropeB_B])
            tt(DVE, tv[:, :, 0:16], Av, Bv, ALU.add, [ropeA_B, ropeB_B], [t32_B[ti]])

        def next_t32():
            ti = rr.setdefault("t32", 0)
            rr["t32"] = (ti + 1) % NT32
            return ti

        def next_krot():
            ki = rr.setdefault("krot", 0)
            rr["krot"] = (ki + 1) % NKROT
            return ki

        def layer0_mixer(tl):
            NTs, kind = tl["NT"], tl["kind"]
            ncol = 128 * NTs
            rslot = tl["rslot"]
            ngrp = 1 if kind == "p" else 2
            dec = hc["dec_p"] if kind == "p" else hc["dec_s"]
            pend = []
            for j in range(8):
                si = w_next(f"in0_{j}")
                W = wslot[si][:, 0:8 * 384].rearrange("p (k c) -> p k c", k=8)
                typ, half = "qkvg"[j // 2], j % 2
                for s in range(NTs):
                    b = mm_bank()
                    mms(ps[b][:, 0:384], [(actT[:, c, 128 * s:128 * (s + 1)], W[:, c, :]) for c in range(KC)],
                        [actT_B[c][s] for c in range(KC)] + [wslot_B[si]], [ps_B[b]])
                    if typ in "qk":
                        ti = next_t32()
                        cp(ACT, t32[:, ti, :], ps[b][:, 0:384], [ps_B[b]], [t32_B[ti]])
                        ki = next_krot()
                        rope_ret(ti, rslot, s, krot[:, ki, :], [krot_B[ki]])

                        def post(ki=ki, typ=typ, half=half, s=s):
                            bank = tp_bank()
                            transposes(bank, [(128 * hh, krot[:, ki, 128 * hh:128 * (hh + 1)]) for hh in range(3)], [krot_B[ki]])
                            pv = psbf(bank)[:, 0:384].rearrange("p (h t) -> p h t", h=3)
                            hs = range(3 * half, 3 * half + 3)
                            if typ == "q":
                                cp(ACT, qT[:, 3 * half:3 * half + 3, 128 * s:128 * (s + 1)], pv, [ps_B[bank]], [qT_B[h][s] for h in hs])
                                for g in range(ngrp):
                                    dst = qxT[:, 3 * half:3 * half + 3, 128 * (s + g):128 * (s + g + 1)]
                                    tt(DVE, dst, pv, xi[:, g, 384 * half:384 * (half + 1)].rearrange("p (h t) -> p h t", h=3), ALU.mult,
                                       [ps_B[bank], rett_B], [qxT_B[h][s + g] for h in hs])
                            else:
                                cp(ACT, kT[:, 3 * half:3 * half + 3, 128 * s:128 * (s + 1)], pv, [ps_B[bank]], [kT_B[h][s] for h in hs])
                        if typ == "k":
                            for g in range(ngrp):
                                tt(DVE, kz[:, s + g, 384 * half:384 * (half + 1)], krot[:, ki, :], zeta[:, g, 384 * half:384 * (half + 1)], ALU.mult,
                                   [krot_B[ki], rett_B], [kz_B[s + g]])
                        pend.append(post)
                        if len(pend) > PDEPTH:
                            pend.pop(0)()
                        continue
                    if pend:
                        pend.pop(0)()
                    if typ == "v":
                        cp(ACT, vtok[:, s, 384 * half:384 * (half + 1)], ps[b][:, 0:384], [ps_B[b]], [vtok_B[s]])
                    else:
                        act(sg[:, s, 384 * half:384 * (half + 1)], ps[b][:, 0:384], AF.Silu, [ps_B[b]], [sg_B[s]])
            while pend:
                pend.pop(0)()
            si = w_next("mq0")
            W = wslot[si][:, 0:2048].rearrange("p (k c) -> p k c", k=8)
            allact = [actT_B[c][s] for c in range(KC) for s in range(NTs)]
            for mc in range(2):
                b = mm_bank()
                mms(ps[b][:, 0:ncol], [(W[:, c, 128 * mc:128 * (mc + 1)], actT[:, c, 0:ncol]) for c in range(KC)], allact + [wslot_B[si]], [ps_B[b]])
                cp(ACT, mqT[:, mc, 0:ncol], ps[b][:, 0:ncol], [ps_B[b]], [mqT_B[mc]])
            def retA(s):
                for half in range(2):
                    hs = list(range(3 * half, 3 * half + 3))
                    mb = 5 + half

                    def fm(s=s, hs=hs, mb=mb):
                        for x_, h in enumerate(hs):
                            ins = nc.tensor.matmul(ps[mb][:, 128 * x_:128 * (x_ + 1)], kT[:, h, 128 * s:128 * (s + 1)], qT[:, h, 128 * s:128 * (s + 1)], start=True, stop=True)
                        return ins
                    P.op(PE, fm, reads=[kT_B[h][s] for h in hs] + [qT_B[h][s] for h in hs], writes=[ps_B[mb]])
                    tt(DVE, MTsb[:, half, :], ps[mb][:, 0:384], DT[:, 384 * half:384 * (half + 1)], ALU.mult, [ps_B[mb], rett_B], [MTsb_B[half]])

            def retB(s):
                for half in range(2):
                    hs = list(range(3 * half, 3 * half + 3))
                    ob, rb = 2 + half, (7 if half == 0 else 4)

                    def fo(s=s, hs=hs, ob=ob, half=half):
                        for x_, h in enumerate(hs):
                            nc.tensor.matmul(ps[ob][:, 128 * x_:128 * (x_ + 1)], MTsb[:, half, 128 * x_:128 * (x_ + 1)], vtok[:, s, 128 * h:128 * (h + 1)], start=True, stop=False)
                            for g in range(ngrp):
                                ins = nc.tensor.matmul(ps[ob][:, 128 * x_:128 * (x_ + 1)], qxT[:, h, 128 * (s + g):128 * (s + g + 1)], Rbf[:, g, 128 * h:128 * (h + 1)], start=False, stop=(g == ngrp - 1))
                        return ins
                    P.op(PE, fo, reads=[MTsb_B[half], vtok_B[s]] + [qxT_B[h][s + g] for h in hs for g in range(ngrp)] + [Rbf_B[g][h] for h in hs for g in range(ngrp)],
                         writes=[ps_B[ob]])
                    for g in range(ngrp):
                        rbank = rb if g == 0 else tp_bank()

                        def fr(s=s, hs=hs, g=g, rbank=rbank):
                            for x_, h in enumerate(hs):
                                ins = nc.tensor.matmul(ps[rbank][:, 128 * x_:128 * (x_ + 1)], kz[:, s + g, 128 * h:128 * (h + 1)], vtok[:, s, 128 * h:128 * (h + 1)], start=True, stop=True)
                            return ins
                        P.op(PE, fr, reads=[kz_B[s + g], vtok_B[s]], writes=[ps_B[rbank]])
                        for x_, h in enumerate(hs):
                            stt(DVE, R32[:, g, 128 * h:128 * (h + 1)], R32[:, g, 128 * h:128 * (h + 1)], dec[h], ps[rbank][:, 128 * x_:128 * (x_ + 1)],
                                ALU.mult, ALU.add, [R_B[g][h], ps_B[rbank]], [R_B[g][h]])
                        cp(ACT, Rbf[:, g, 384 * half:384 * (half + 1)], R32[:, g, 384 * half:384 * (half + 1)], [R_B[g][h] for h in hs], [Rbf_B[g][h] for h in hs])

            def retC1(s):
                def bns(s=s):
                    for h in range(H):
                        ins = nc.vector.bn_stats(out=stat[:, s, h, :], in_=ps[2 + h // 3][:, 128 * (h % 3):128 * (h % 3 + 1)])
                    return ins
                P.op(DVE, bns, reads=[ps_B[2], ps_B[3]], writes=[stat_B[s]])

                def bna(s=s):
                    for h in range(H):
                        ins = nc.vector.bn_aggr(out=aggr[:, s, h, 0:2], in_=stat[:, s, h:h + 1, :])
                    return ins
                P.op(DVE, bna, reads=[stat_B[s]], writes=[aggr_B[s]])
                av = aggr[:, s, 0:6, :]
                act(av[:, :, 2:3], av[:, :, 1:2], AF.Ln, [aggr_B[s], cst_B], [aggr_B[s]], bias=epsc[:, :])
                act(av[:, :, 3:4], av[:, :, 2:3], AF.Exp, [aggr_B[s]], [aggr_B[s]], scale=-0.5)
                stt(DVE, av[:, :, 2:3], av[:, :, 0:1], -1.0, av[:, :, 3:4], ALU.mult, ALU.mult, [aggr_B[s]], [aggr_B[s]])

            def retC2(s, hi):
                for half in range(2):
                    ob = 2 + half
                    def fn(s=s, ob=ob, half=half):
                        for x_ in range(3):
                            h = 3 * half + x_
                            ins = nc.scalar.activation(out=gnt[:, half, 128 * x_:128 * (x_ + 1)], in_=ps[ob][:, 128 * x_:128 * (x_ + 1)],
                                                       func=AF.Identity, scale=aggr[:, s, h, 3:4], bias=aggr[:, s, h, 2:3])
                        return ins
                    P.op(ACT, fn, reads=[ps_B[ob], aggr_B[s]], writes=[gnt_B[half]])
                    tt(DVE, gnt[:, half, :], gnt[:, half, :], gng[:, 384 * half:384 * (half + 1)], ALU.mult, [gnt_B[half], cst_B], [gnt_B[half]])
                    tt(DVE, hm[:, hi, 384 * half:384 * (half + 1)], gnt[:, half, :], sg[:, s, 384 * half:384 * (half + 1)], ALU.mult, [gnt_B[half], sg_B[s]], [hm_B[hi]])

            def retD(s, hi):
                bank = tp_bank()
                transposes(bank, [(128 * h, hm[:, hi, 128 * h:128 * (h + 1)]) for h in range(H)], [hm_B[hi]])
                pv = psbf(bank)[:, 0:768].rearrange("p (h t) -> p h t", h=H)
                cp(ACT, actT[:, 0:6, 128 * s:128 * (s + 1)], pv, [ps_B[bank]], [actT_B[c][s] for c in range(6)])

            retA(0)
            dpend = None
            for s in range(NTs):
                hi = rr.setdefault("hm", 0)
                rr["hm"] = hi ^ 1
                retB(s)
                retC1(s)
                if s + 1 < NTs:
                    retA(s + 1)
                if dpend is not None:
                    dpend()
                retC2(s, hi)
                dpend = (lambda s=s, hi=hi: retD(s, hi))
            dpend()

        S1BANKS, S2BANKS = (0, 1, 6), (2, 3, 7)

        def attention_head(h, ncol, keytiles, mid=None):
            nkt = len(keytiles)
            fifo = []
            for x_, kt in enumerate(keytiles):
                c0, ncl = kt["c0"], kt["nc"]
                r_ = rr.setdefault("att3", 0)
                rr["att3"] = (r_ + 1) % 3
                s1b, s2b = S1BANKS[r_], S2BANKS[r_]
                mms(ps[s1b][:, c0:c0 + ncl], [(kt["kt"], qT[:, h, c0:c0 + ncl])], kt["reads"] + [qT_B[h][s] for s in range(4)], [ps_B[s1b]])
                mms(ps[s2b][:, c0:c0 + ncl], [(kt["kt"], qxT[:, h, c0:c0 + ncl])], kt["reads"] + [qxT_B[h][s] for s in range(4)], [ps_B[s2b]])
                p1, p2 = r_, 3 + r_
                act(Pb[:, p1, c0:c0 + ncl], ps[s1b][:, c0:c0 + ncl], AF.Exp, [ps_B[s1b]], [Pb_B[p1]], scale=0.125)
                act(Pb[:, p2, c0:c0 + ncl], ps[s2b][:, c0:c0 + ncl], AF.Exp, [ps_B[s2b]], [Pb_B[p2]], scale=0.125)
                if kt.get("fix") is not None:
                    kt["fix"](p1)
                    kt["fix"](p2)
                first, last = (x_ == 0), (x_ == nkt - 1)
                fc0, fnc = kt.get("acc_c0", 0), kt.get("acc_nc", ncol)
                for (acc, acc_B, p) in ((accA, accA_B, p1), (accB, accB_B, p2)):
                    if first:
                        cp(DVE, acc[:, fc0:fc0 + fnc], Pb[:, p, fc0:fc0 + fnc], [Pb_B[p]], [acc_B])
                    else:
                        tt(DVE, acc[:, fc0:fc0 + fnc], acc[:, fc0:fc0 + fnc], Pb[:, p, fc0:fc0 + fnc], ALU.add, [acc_B, Pb_B[p]], [acc_B])

                def f(p1=p1, p2=p2, kt=kt, first=first, last=last, fc0=fc0, fnc=fnc):
                    nc.tensor.matmul(ps[4][:, fc0:fc0 + fnc], kt["v"], Pb[:, p1, fc0:fc0 + fnc], start=first, stop=last)
                    return nc.tensor.matmul(ps[5][:, fc0:fc0 + fnc], kt["v"], Pb[:, p2, fc0:fc0 + fnc], start=first, stop=last)
                fifo.append(lambda f=f, kt=kt, p1=p1, p2=p2: P.op(PE, f, reads=kt["reads"] + [Pb_B[p1], Pb_B[p2]], writes=[ps_B[4], ps_B[5]]))
                if len(fifo) > 2:
                    fifo.pop(0)()
                if kt.get("pre") is not None:
                    kt["pre"]()
                if mid is not None and x_ == min(2, nkt - 1):
                    mid()
                    mid = None
            while fifo:
                fifo.pop(0)()
            mms(ps[6][:, fc0:fc0 + fnc], [(ones_f[:, :], accA[:, fc0:fc0 + fnc])], [accA_B, cst_B], [ps_B[6]])
            mms(ps[7][:, fc0:fc0 + fnc], [(ones_f[:, :], accB[:, fc0:fc0 + fnc])], [accB_B, cst_B], [ps_B[7]])

        def attention_finish_a(h, c0, ncol):
            act(att_r[:, 0, 0:ncol], ps[6][:, c0:c0 + ncol], AF.Ln, [ps_B[6]], [att_r_B[0]])
            cp(DVE, att_o[:, 0, 0:ncol], ps[4][:, c0:c0 + ncol], [ps_B[4]], [att_o_B[0]])
            act(att_r[:, 1, 0:ncol], ps[7][:, c0:c0 + ncol], AF.Ln, [ps_B[7]], [att_r_B[1]])
            cp(DVE, att_o[:, 1, 0:ncol], ps[5][:, c0:c0 + ncol], [ps_B[5]], [att_o_B[1]])
            act(att_r[:, 0, 0:ncol], att_r[:, 0, 0:ncol], AF.Exp, [att_r_B[0]], [att_r_B[0]], scale=-1.0)
            act(att_r[:, 1, 0:ncol], att_r[:, 1, 0:ncol], AF.Exp, [att_r_B[1]], [att_r_B[1]], scale=-1.0)
            tt(DVE, att_o[:, 0, 0:ncol], att_o[:, 0, 0:ncol], att_r[:, 0, 0:ncol], ALU.mult, [att_o_B[0], att_r_B[0]], [att_o_B[0]])
            tt(DVE, att_o[:, 1, 0:ncol], att_o[:, 1, 0:ncol], att_r[:, 1, 0:ncol], ALU.mult, [att_o_B[1], att_r_B[1]], [att_o_B[1]])
            stt(DVE, att_o[:, 0, 0:ncol], att_o[:, 1, 0:ncol], lam[:, 5:6], att_o[:, 0, 0:ncol], ALU.mult, ALU.add,
                [att_o_B[0], att_o_B[1], lam_B], [att_o_B[0]])

        def attention_finish_b(h, c0, ncol):
            act(att_o[:, 1, 0:ncol], att_o[:, 0, 0:ncol], AF.Square, [att_o_B[0]], [att_o_B[1]])
            b = S1BANKS[rr.setdefault("att3", 0)]
            mms(ps[b][:, 0:ncol], [(ones_f[:, :], att_o[:, 1, 0:ncol])], [att_o_B[1], cst_B], [ps_B[b]])
            act(att_r[:, 0, 0:ncol], ps[b][:, 0:ncol], AF.Ln, [ps_B[b], cst_B], [att_r_B[0]], scale=1.0 / 128, bias=epsc[:, :])
            act(att_r[:, 1, 0:ncol], att_r[:, 0, 0:ncol], AF.Exp, [att_r_B[0]], [att_r_B[1]], scale=-0.5)
            subs = sorted(set(range(c0 // 128, (c0 + ncol - 1) // 128 + 1)))
            stt(DVE, actT[:, h, c0:c0 + ncol], att_o[:, 0, 0:ncol], sublg[:, 1:2], att_r[:, 1, 0:ncol], ALU.mult, ALU.mult,
                [att_o_B[0], att_r_B[1], lam_B], [actT_B[h][s] for s in subs])

        def layer1_mixer(tl):
            NTs, kind = tl["NT"], tl["kind"]
            ncol = 128 * NTs
            rslot = tl["rslot"]
            t0 = tl["t0"]
            P.op(POOL, lambda: nc.gpsimd.memset(qT[64:128, :, 0:ncol], 0.0), writes=[b for h in range(H) for b in qT_B[h]])
            P.op(POOL, lambda: nc.gpsimd.memset(qxT[0:64, :, 0:ncol], 0.0), writes=[b for h in range(H) for b in qxT_B[h]])
            pend = []
            for j in range(6):
                si = w_next(f"in1_{j}")
                W = wslot[si][:, 0:8 * 384].rearrange("p (k c) -> p k c", k=8)
                typ, half = "qkv"[j // 2], j % 2
                hs = list(range(3 * half, 3 * half + 3))
                for s in range(NTs):
                    b = mm_bank()
                    mms(ps[b][:, 0:384], [(actT[:, c, 128 * s:128 * (s + 1)], W[:, c, :]) for c in range(KC)],
                        [actT_B[c][s] for c in range(KC)] + [wslot_B[si]], [ps_B[b]])
                    ti = next_t32()
                    cp(ACT, t32[:, ti, :], ps[b][:, 0:384], [ps_B[b]], [t32_B[ti]])
                    if typ in "qk":
                        rope_diff(ti, rslot, s)
                        ki = next_krot()
                        cp(DVE, krot[:, ki, :], t32[:, ti, :], [t32_B[ti]], [krot_B[ki]])
                        if typ == "k":
                            dma(SP, tl["dk_out"](s, half), t32[:, ti, :], [t32_B[ti]], [], t32sem[ti], is_store=True)

                        def post(ki=ki, typ=typ, half=half, s=s, hs=hs):
                            bank = tp_bank()
                            transposes(bank, [(128 * hh, krot[:, ki, 128 * hh:128 * (hh + 1)]) for hh in range(3)], [krot_B[ki]])
                            pv = psbf(bank)[:, 0:384].rearrange("p (h t) -> p h t", h=3)
                            if typ == "q":
                                cp(ACT, qT[0:64, 3 * half:3 * half + 3, 128 * s:128 * (s + 1)], pv[0:64], [ps_B[bank]], [qT_B[h][s] for h in hs])
                                cp(ACT, qxT[64:128, 3 * half:3 * half + 3, 128 * s:128 * (s + 1)], pv[64:128], [ps_B[bank]], [qxT_B[h][s] for h in hs])
                            else:
                                cp(ACT, kT[:, 3 * half:3 * half + 3, 128 * s:128 * (s + 1)], pv, [ps_B[bank]], [kT_B[h][s] for h in hs])
                        pend.append(post)
                        if len(pend) > PDEPTH:
                            pend.pop(0)()
                    else:
                        if pend:
                            pend.pop(0)()
                        cp(DVE, vtok[:, s, 384 * half:384 * (half + 1)], t32[:, ti, :], [t32_B[ti]], [vtok_B[s]])
                        dma(SP, tl["dv_out"](s, half), t32[:, ti, :], [t32_B[ti]], [], t32sem[ti], is_store=True)
            while pend:
                pend.pop(0)()
            si = w_next("mq1")
            W = wslot[si][:, 0:2048].rearrange("p (k c) -> p k c", k=8)
            allact = [actT_B[c][s] for c in range(KC) for s in range(NTs)]
            for mc in range(2):
                b = mm_bank()
                mms(ps[b][:, 0:ncol], [(W[:, c, 128 * mc:128 * (mc + 1)], actT[:, c, 0:ncol]) for c in range(KC)], allact + [wslot_B[si]], [ps_B[b]])
                cp(ACT, mqT[:, mc, 0:ncol], ps[b][:, 0:ncol], [ps_B[b]], [mqT_B[mc]])
            if kind == "p":
                tix = t0 // TT
                nblk = (t0 + TT + 1023) // 1024
                blocks = [(h, kb) for h in range(H) for kb in range(nblk)]
                kvst = {"issued": 0}

                def kv_issue_upto(n, blocks=blocks):
                    while kvst["issued"] < min(n, len(blocks)):
                        i = kvst["issued"]
                        h_, kb = blocks[i]
                        nk = min(1024, t0 + TT - 1024 * kb)
                        ki = i % NKV

                        def f(sem, ki=ki, kb=kb, nk=nk, h_=h_):
                            nc.sync.dma_start(out=kvK[ki][:, 0:nk], in_=KTd[h_, :, 1024 * kb:1024 * kb + nk]).then_inc(sem, 16)
                            nc.sync.dma_start(out=kvV[ki][:, 0:nk // 128, :], in_=Vd[h_, :, 8 * kb:8 * kb + nk // 128, :]).then_inc(sem, 16)
                        P.op(SP, f, reads=[KTd_B[h_], Vd_B[h_]], writes=[kv_B[ki]], dma_sem=kvsem[ki], ndma=2)
                        kvst["issued"] += 1

                kv_issue_upto(min(NKV - 1, t0 // 1024))
                def fpub(sem, t0=t0, tix=tix):
                    for h in range(H):
                        nc.sync.dma_start(out=KTd[h, :, t0:t0 + TT], in_=kT[:, h, :]).then_inc(sem, 16)
                        nc.sync.dma_start(out=Vd[h, :, 4 * tix:4 * tix + 4, :], in_=vtok[:, :, 128 * h:128 * (h + 1)]).then_inc(sem, 16)
                P.op(SP, fpub, reads=[kT_B[h][s] for h in range(H) for s in range(4)] + vtok_B, writes=KTd_B + Vd_B, dma_sem=kvwsem, ndma=2 * H)
                midcb = [None]
                kv_issue_upto(NKV - 1)
                for h in range(H):
                    kts = []
                    for kb in range(nblk):
                        gi = h * nblk + kb
                        nk = min(1024, t0 + TT - 1024 * kb)
                        ki = gi % NKV
                        for x_ in range(nk // 128):
                            key0 = 1024 * kb + 128 * x_
                            d = dict(kt=kvK[ki][:, 128 * x_:128 * (x_ + 1)], v=kvV[ki][:, x_, :], reads=[kv_B[ki]], c0=0, nc=ncol)
                            if x_ == 2:
                                d["pre"] = (lambda gi=gi: kv_issue_upto(gi + NKV))
                            if key0 >= t0:
                                jj = (key0 - t0) // 128
                                d["c0"], d["nc"] = 128 * jj, ncol - 128 * jj

                                def fix(p, jj=jj):
                                    if jj > 0:
                                        P.op(POOL, lambda: nc.gpsimd.memset(Pb[:, p, 0:128 * jj], 0.0), writes=[Pb_B[p]])
                                    P.op(POOL, lambda: nc.gpsimd.memset(Pb[64:128, p, 128 * jj:128 * jj + 64], 0.0), writes=[Pb_B[p]])
                                d["fix"] = fix
                            kts.append(d)
                    attention_head(h, ncol, kts, mid=midcb[0])
                    attention_finish_a(h, 0, ncol)
                    midcb[0] = (lambda h=h: attention_finish_b(h, 0, ncol))
                midcb[0]()
                midcb[0] = None
                mem_attention(1, NTs, [(0, ncol, None)])
            else:
                for bq in range(2):
                    for ktile in range(8):
                        for half in range(2):
                            ti = next_t32()
                            dma(SP, t32[:, ti, :], I["c_dk"][bq, 128 * ktile:128 * (ktile + 1), 384 * half:384 * (half + 1)], [], [t32_B[ti]], t32sem[ti])
                            ki = next_krot()
                            cp(DVE, krot[:, ki, :], t32[:, ti, :], [t32_B[ti]], [krot_B[ki]])
                            bank = tp_bank()
                            transposes(bank, [(128 * hh, krot[:, ki, 128 * hh:128 * (hh + 1)]) for hh in range(3)], [krot_B[ki]])
                            pv = psbf(bank)[:, 0:384].rearrange("p (h t) -> p h t", h=3)
                            cp(ACT, sKT[:, 3 * half:3 * half + 3, 128 * ktile:128 * (ktile + 1)], pv, [ps_B[bank]], [sKT_B])
                    P.op(POOL, lambda s, bq=bq: nc.gpsimd.dma_start(out=sV[:, :, :], in_=I["c_dv"][bq].rearrange("(k p) c -> p k c", p=128)).then_inc(s, 16),
                         writes=[sV_B], dma_sem=bsem("sV"), ndma=1)
                    for h in range(H):
                        kts = []
                        for ktile in range(8):
                            kts.append(dict(kt=sKT[:, h, 128 * ktile:128 * (ktile + 1)], v=sV[:, ktile, 128 * h:128 * (h + 1)], reads=[sKT_B, sV_B],
                                            c0=64 * bq, nc=64, acc_c0=64 * bq, acc_nc=64))

                        def fix(p, bq=bq):
                            lo = 64 * (1 - bq)
                            P.op(POOL, lambda: nc.gpsimd.memset(Pb[lo:lo + 64, p, 64 * bq:64 * bq + 64], 0.0), writes=[Pb_B[p]])
                        kts.append(dict(kt=kT[:, h, 0:128], v=vtok[:, 0, 128 * h:128 * (h + 1)], reads=[kT_B[h][0], vtok_B[0]],
                                        c0=64 * bq, nc=64, acc_c0=64 * bq, acc_nc=64, fix=fix))
                        attention_head(h, 64, kts)
                        attention_finish_a(h, 64 * bq, 64)
                        attention_finish_b(h, 64 * bq, 64)
                mem_attention(1, 1, [(64 * bq, 64, (lambda bq=bq: mem_setup_sample(1, bq))) for bq in range(2)])

        def x_prefetch(tl):
            tl["xpre"] = True
            n = tl["NT"]

            def f(sem, tl=tl, n=n):
                for s in range(n):
                    nc.gpsimd.dma_start(out=xbfn[:, s, :], in_=tl["x_in"](s)).then_inc(sem, 16)
            P.op(POOL, f, writes=xbfn_B[0:n], dma_sem=xbfnsem, ndma=n)

        def run_tile(tl):
            NTs, kind, rslot = tl["NT"], tl["kind"], tl["rslot"]
            for s in range(NTs):
                dma(SP, xres[:, s, :], tl["x_in"](s), [], xres_B[s], xsem[s])
            dma(SP, ropet[:, rslot, 0:NTs, :], tl["rope_in"], [], [ropet_B[rslot]], ropesem[rslot])
            if not tl.get("xpre"):
                x_prefetch(tl)
            for s in range(NTs):
                bank = tp_bank()
                transposes(bank, [(128 * c, xbfn[:, s, 128 * c:128 * (c + 1)]) for c in range(KC)], [xbfn_B[s]])
                pv = psbf(bank)[:, 0:1024].rearrange("p (c t) -> p c t", c=KC)
                cp(ACT if s % 2 == 0 else DVE, actT[:, :, 128 * s:128 * (s + 1)], pv, [ps_B[bank]], [actT_B[c][s] for c in range(KC)])
            if kind == "s":
                load_ret_tables("s")
                for g in range(2):
                    dma(SP, R32[:, g, :].rearrange("p (h e) -> p h e", h=H), I["c_ret"][g].rearrange("h d e -> d h e"), [], R_B[g], bsem(f"R{g}"))
                    cp(DVE, Rbf[:, g, :], R32[:, g, :], R_B[g], Rbf_B[g])
            layer0_mixer(tl)
            if kind == "p":
                mem_attention(0, NTs, [(0, 128 * NTs, None)])
                if tl["last"]:
                    dma(SP, O["rs_p"].rearrange("h d e -> d h e"), R32[:, 0, :].rearrange("p (h e) -> p h e", h=H), R_B[0], [], bsem("R0"), is_store=True)
            else:
                mem_attention(0, 1, [(64 * bq, 64, (lambda bq=bq: mem_setup_sample(0, bq))) for bq in range(2)])
                for g in range(2):
                    dma(SP, O["rs_s"][g].rearrange("h d e -> d h e"), R32[:, g, :].rearrange("p (h e) -> p h e", h=H), R_B[g], [], bsem(f"R{g}"), is_store=True)
            if n_layers == 1:
                wo_and_ffn(0, NTs, tl["y_out"])
                return
            wo_and_ffn(0, NTs, None)
            layer1_mixer(tl)
            if tl.get("next") is not None:
                x_prefetch(tl["next"])
            wo_and_ffn(1, NTs, tl["y_out"])

        sKT = None
        sV = None
        sKT_B = Buf("sKT")
        sV_B = Buf("sV")

        ptiles = []
        for t in range(n_ptiles):
            t0 = t * TT
            ptiles.append(dict(kind="p", NT=4, t0=t0, rslot=t % 2, last=(t == n_ptiles - 1),
                               x_in=(lambda s, t0=t0: I["xp"][t0 + 128 * s:t0 + 128 * (s + 1), :]),
                               rope_in=I["rope_p"][t0:t0 + TT, :].rearrange("(s p) c -> p s c", p=128),
                               y_out=(lambda s, t0=t0: O["y_p"][t0 + 128 * s:t0 + 128 * (s + 1), :]),
                               dk_out=(lambda s, half, t0=t0: O["dk_p"][t0 + 128 * s:t0 + 128 * (s + 1), 384 * half:384 * (half + 1)]),
                               dv_out=(lambda s, half, t0=t0: O["dv_p"][t0 + 128 * s:t0 + 128 * (s + 1), 384 * half:384 * (half + 1)])))
        tiles = list(ptiles)
        if do_sample:
            tiles.append(dict(kind="s", NT=1, t0=0, rslot=n_ptiles % 2, last=True,
                              x_in=(lambda s: I["xs"][:, :]),
                              rope_in=I["rope_s"].rearrange("(s p) c -> p s c", p=128),
                              y_out=(lambda s: O["y_s"][:, :]),
                              dk_out=(lambda s, half: O["dk_s"][:, 384 * half:384 * (half + 1)]),
                              dv_out=(lambda s, half: O["dv_s"][:, 384 * half:384 * (half + 1)])))
            sKT = xres[:, 1:4, :].rearrange("p s d -> p (s d)").bitcast(BF16).rearrange("p (h t) -> p h t", h=H)
            sV = arenaB[:, :].rearrange("p (k c) -> p k c", k=8)
            alias([sKT_B], [b for s_ in range(1, 4) for b in xres_B[s_]])
            alias([sV_B], kz_B + sg_B + kv_B)
        for a_, b_ in zip(tiles[:-1], tiles[1:]):
            a_["next"] = b_
        plan_weights(tiles)
        startup()

        def prompt_setup():
            mem_setup_prompt()
            load_ret_tables("p")
            P.op(DVE, lambda: nc.vector.memset(R32[:, :, :], 0.0), writes=R_B[0] + R_B[1])
            P.op(DVE, lambda: nc.vector.memset(Rbf[:, :, :], 0.0), writes=Rbf_B[0] + Rbf_B[1])
        for tl in tiles:
            if tl["kind"] == "p" and tl["t0"] == 0:
                prompt_setup()
            run_tile(tl)
        info = P.emit()
    return nc, info


_CACHE = {}


def _core_inputs(c, inp, hc):
    f = lambda a: np.ascontiguousarray(np.asarray(a, dtype=np.float32))
    m = {
        "xp": f(inp["x_prompt"][c]),
        "xs": f(inp["x_sample"][2 * c:2 * c + 2].reshape(128, D)),
        "memp": f(inp["mem_prompt"][c]),
        "c_ret": f(inp["cache_ret_state"][0, 2 * c:2 * c + 2]),
        "c_dk": f(inp["cache_diff_k"][0, 2 * c:2 * c + 2].reshape(2, PAST, MIX)),
        "c_dv": f(inp["cache_diff_v"][0, 2 * c:2 * c + 2].reshape(2, PAST, MIX)),
        "c_mk": f(inp["cache_mem_k"][:, 2 * c:2 * c + 2].reshape(2, 2, NMEM, 256)),
        "c_mv": f(inp["cache_mem_v"][:, 2 * c:2 * c + 2].reshape(2, 2, NMEM, 256)),
    }
    return m


def kernel(**inp):
    hc = _host_consts()
    f = lambda a: np.ascontiguousarray(np.asarray(a, dtype=np.float32))
    shared = {
        "ret_w_in": f(inp["ret_w_in"][0]), "ret_gn_g": f(inp["ret_gn_g"]), "diff_w_in": f(inp["diff_w_in"][0]),
        "lamv": f(np.stack([np.asarray(inp["diff_lambda_q1"])[0], np.asarray(inp["diff_lambda_k1"])[0],
                            np.asarray(inp["diff_lambda_q2"])[0], np.asarray(inp["diff_lambda_k2"])[0]])),
        "subln": f(np.asarray(inp["diff_subln_g"])[0].reshape(128, 1)),
        "w_mem_kv": f(inp["w_mem_kv"]), "w_o": f(inp["w_o"]),
        "ln1_g": f(inp["ln1_g"]), "ln1_b": f(inp["ln1_b"]), "w_gate": f(inp["w_gate"]), "w_up": f(inp["w_up"]),
        "w_down": f(inp["w_down"]), "ln2_g": f(inp["ln2_g"]), "ln2_b": f(inp["ln2_b"]),
    }
    for n in CONST_SHAPES:
        shared[n] = hc[n]
    if "nc" not in _CACHE:
        _CACHE["nc"] = build_program()[0]
    nc = _CACHE["nc"]
    in_maps = []
    for c in range(8):
        m = dict(shared)
        m.update(_core_inputs(c, inp, hc))
        in_maps.append(m)
    res = run_bass_kernel_spmd(nc, in_maps, core_ids=list(range(8)))
    R = res.results
    y_p = np.stack([R[c]["y_p"] for c in range(8)])
    y_s = np.concatenate([R[c]["y_s"].reshape(2, 64, D) for c in range(8)])
    rs_p = np.stack([R[c]["rs_p"] for c in range(8)])[None]
    rs_s = np.concatenate([R[c]["rs_s"] for c in range(8)])[None]
    dk_p = np.stack([R[c]["dk_p"].reshape(SEQ, H, 128) for c in range(8)])[None]
    dv_p = np.stack([R[c]["dv_p"].reshape(SEQ, H, 128) for c in range(8)])[None]
    dk_s = np.concatenate([R[c]["dk_s"].reshape(2, 64, H, 128) for c in range(8)])[None]
    dv_s = np.concatenate([R[c]["dv_s"].reshape(2, 64, H, 128) for c in range(8)])[None]
    mk_p = np.stack([R[c]["mk_p"].reshape(2, NMEM, 4, 64) for c in range(8)], axis=1)
    mv_p = np.stack([R[c]["mv_p"].reshape(2, NMEM, 4, 64) for c in range(8)], axis=1)
    outs = (y_p, y_s, rs_p, rs_s, dk_p, dv_p, dk_s, dv_s, mk_p, mv_p)
    return tuple(np.ascontiguousarray(o, dtype=np.float32) for o in outs)
```

```python
import math
import numpy as np
import concourse.bass as bass
import concourse.mybir as mybir
from concourse.bass_utils import run_bass_kernel_spmd
from contextlib import ExitStack

F32 = mybir.dt.float32
BF16 = mybir.dt.bfloat16
AF = mybir.ActivationFunctionType
ALU = mybir.AluOpType
AX = mybir.AxisListType

PE, ACT, DVE, POOL, SP = "pe", "act", "dve", "pool", "sp"
PDEPTH = 3

D = 1024
KC = 8
SEQ = 4096
TT = 512
MIX = 768
H = 6
DFF = 2816
FC = 22
NMEM = 256
PAST = 1024
ALPHA = 4.0 ** 0.25
LN_EPS = 1e-5
LAM_INIT = 0.8 - 0.6 * math.exp(-0.3 * 1)
RET_COLS = 3328
DIFF_COLS = 2560
NROPE = 224


class Buf:
    __slots__ = ("name", "w", "r", "ov", "excl")

    def __init__(self, name, excl=False):
        self.name = name
        self.w = None
        self.r = {}
        self.ov = []
        self.excl = excl


def alias(a_list, b_list):
    for a in a_list:
        for b in b_list:
            a.ov.append(b)
            b.ov.append(a)


class Op:
    __slots__ = ("eng", "fn", "deps", "dma", "ev", "idx")


class Prog:
    def __init__(self, nc, es):
        self.nc = nc
        self.es = es
        self.ops = []
        self.engines = {PE: nc.tensor, ACT: nc.scalar, DVE: nc.vector, POOL: nc.gpsimd, SP: nc.sync}
        self.esem = {}
        for e in (PE, ACT, DVE, POOL):
            self.esem[e] = es.enter_context(nc.semaphore("ev_" + e))
        self.stores = []
        self.nsem = 0

    def new_dma_sem(self, name):
        self.nsem += 1
        return self.es.enter_context(self.nc.semaphore(f"dq_{name}_{self.nsem}"))

    def op(self, eng, fn, reads=(), writes=(), dma_sem=None, ndma=1, is_store=False):
        idx = len(self.ops)
        o = Op()
        o.eng = eng
        o.fn = fn
        o.idx = idx
        o.dma = (dma_sem, ndma) if dma_sem is not None else None
        deps = set()
        is_dma = dma_sem is not None

        def add(j, kind):
            if j is None:
                return
            d = self.ops[j]
            if (not is_dma) and d.dma is None and d.eng == eng:
                if eng == PE:
                    return
            deps.add(j)

        for b in reads:
            add(b.w, "raw")
            if b.excl:
                for k_, j in b.r.items():
                    if k_ != eng:
                        deps.add(j)
        for b in writes:
            for bb in [b] + b.ov:
                add(bb.w, "waw")
                for j in bb.r.values():
                    add(j, "war")
        rkey = ("dma", idx) if is_dma else eng
        for b in reads:
            b.r[rkey] = idx
        for b in writes:
            b.w = idx
            b.r = {}
        o.deps = deps
        o.ev = None
        self.ops.append(o)
        if is_store:
            self.stores.append(idx)
        return idx

    def emit(self):
        import os
        if os.environ.get("KTRUNC"):
            n = int(os.environ["KTRUNC"])
            self.ops = self.ops[:n]
            self.stores = [j for j in self.stores if j < n]
        needed = set()
        for o in self.ops:
            needed |= o.deps
        ecount = {e: 0 for e in self.esem}
        dcount = {}
        waited = {e: {} for e in self.engines}
        nwait = 0
        for o in self.ops:
            eng = self.engines[o.eng]
            w = {}
            for j in o.deps:
                sem, val = self.ops[j].ev
                k = id(sem)
                if k not in w or w[k][1] < val:
                    w[k] = (sem, val)
            wd = waited[o.eng]
            for k, (sem, val) in w.items():
                if wd.get(k, 0) >= val:
                    continue
                eng.wait_ge(sem, val)
                nwait += 1
                wd[k] = val
            if o.dma is not None:
                sem, n = o.dma
                o.fn(sem)
                c = dcount.get(id(sem), 0) + n
                dcount[id(sem)] = c
                o.ev = (sem, 16 * c)
            else:
                ins = o.fn()
                if o.idx in needed:
                    sem = self.esem[o.eng]
                    ins.then_inc(sem, 1)
                    ecount[o.eng] += 1
                    o.ev = (sem, ecount[o.eng])
        sp = self.engines[SP]
        w = {}
        for j in self.stores:
            sem, val = self.ops[j].ev
            k = id(sem)
            if k not in w or w[k][1] < val:
                w[k] = (sem, val)
        for k, (sem, val) in w.items():
            sp.wait_ge(sem, val)
        return dict(n_ops=len(self.ops), n_wait=nwait, ecount=ecount)


def _host_consts():
    c = {}
    c["ident"] = np.eye(128, dtype=np.float32)

    def rope_tab(pos):
        pos = pos.astype(np.float32)
        inv_r = np.exp(np.float32(-math.log(10000.0)) * np.arange(64, dtype=np.float32) * np.float32(2.0 / 128)).astype(np.float32)
        ang_r = (pos[:, None] * inv_r[None, :]).astype(np.float32).astype(np.float64)
        inv_d = np.exp(np.float32(-math.log(500000.0)) * np.arange(8, dtype=np.float32) * np.float32(2.0 / 16)).astype(np.float32)
        ang_d = (pos[:, None] * inv_d[None, :]).astype(np.float32).astype(np.float64)
        t = np.concatenate([np.cos(ang_r), -np.sin(ang_r), np.sin(ang_r),
                            np.cos(ang_d), np.cos(ang_d), -np.sin(ang_d), np.sin(ang_d)], axis=1)
        return np.ascontiguousarray(t.astype(np.float32))

    c["rope_p"] = rope_tab(np.arange(SEQ))
    ps = PAST + np.arange(64)
    c["rope_s"] = rope_tab(np.concatenate([ps, ps]))

    log_g = np.log(1.0 - np.exp2(-5.0 - np.arange(H, dtype=np.float64)))
    s = 128.0 ** -0.5
    i = np.arange(128)
    same = (i[:, None] // 64) == (i[None, :] // 64)
    Dp = np.zeros((H, 128, 128))
    for h in range(H):
        dd = np.exp(np.abs(i[:, None] - i[None, :]) * log_g[h])
        causal = np.where(i[:, None] >= i[None, :], dd, 0.0)
        Dp[h] = np.where(same, dd, causal)
    c["DT_p"] = np.ascontiguousarray((Dp.transpose(2, 0, 1) * s).reshape(128, H * 128).astype(np.float32))
    xi_p = np.exp((i[None, :] + 1.0) * log_g[:, None])
    c["xi_p"] = np.ascontiguousarray(xi_p.reshape(1, H * 128).astype(np.float32))
    zeta_p = np.exp((127.0 - i)[:, None] * log_g[None, :]) * s
    c["zeta_p"] = np.ascontiguousarray(np.repeat(zeta_p, 128, axis=1).astype(np.float32))
    c["dec_p"] = [float(np.exp(128.0 * log_g[h])) for h in range(H)]
    Ds = np.zeros((H, 128, 128))
    for h in range(H):
        dd = np.exp(np.abs(i[:, None] - i[None, :]) * log_g[h])
        Ds[h] = np.where(same, dd, 0.0)
    c["DT_s"] = np.ascontiguousarray((Ds.transpose(2, 0, 1) * s).reshape(128, H * 128).astype(np.float32))
    ii = i % 64
    xi_s = np.exp((ii[None, :] + 1.0) * log_g[:, None])
    xiA = np.where(i[None, :] < 64, xi_s, 0.0)
    xiB = np.where(i[None, :] >= 64, xi_s, 0.0)
    c["xi_sA"] = np.ascontiguousarray(xiA.reshape(1, H * 128).astype(np.float32))
    c["xi_sB"] = np.ascontiguousarray(xiB.reshape(1, H * 128).astype(np.float32))
    zeta_s = np.exp((63.0 - ii)[:, None] * log_g[None, :]) * s
    zA = np.where(i[:, None] < 64, zeta_s, 0.0)
    zB = np.where(i[:, None] >= 64, zeta_s, 0.0)
    c["zeta_sA"] = np.ascontiguousarray(np.repeat(zA, 128, axis=1).astype(np.float32))
    c["zeta_sB"] = np.ascontiguousarray(np.repeat(zB, 128, axis=1).astype(np.float32))
    c["dec_s"] = [float(np.exp(64.0 * log_g[h])) for h in range(H)]
    return c


CONST_SHAPES = {
    "ident": [128, 128], "rope_p": [SEQ, NROPE], "rope_s": [128, NROPE],
    "DT_p": [128, 768], "xi_p": [1, 768], "zeta_p": [128, 768],
    "DT_s": [128, 768], "xi_sA": [1, 768], "xi_sB": [1, 768], "zeta_sA": [128, 768], "zeta_sB": [128, 768],
}

IN_SHAPES = {
    "xp": [SEQ, D], "xs": [128, D], "memp": [NMEM, D],
    "c_ret": [2, H, 128, 128], "c_dk": [2, PAST, MIX], "c_dv": [2, PAST, MIX],
    "c_mk": [2, 2, NMEM, 256], "c_mv": [2, 2, NMEM, 256],
    "ret_w_in": [D, RET_COLS], "ret_gn_g": [1, MIX], "diff_w_in": [D, DIFF_COLS],
    "lamv": [4, 64], "subln": [128, 1],
    "w_mem_kv": [2, D, 512], "w_o": [2, D, D],
    "ln1_g": [2, D], "ln1_b": [2, D], "w_gate": [2, D, DFF], "w_up": [2, D, DFF], "w_down": [2, DFF, D],
    "ln2_g": [2, D], "ln2_b": [2, D],
}

OUT_SHAPES = {
    "y_p": [SEQ, D], "y_s": [128, D], "rs_p": [H, 128, 128], "rs_s": [2, H, 128, 128],
    "dk_p": [SEQ, MIX], "dv_p": [SEQ, MIX], "dk_s": [128, MIX], "dv_s": [128, MIX],
    "mk_p": [2, NMEM, 256], "mv_p": [2, NMEM, 256],
}


def build_program(n_ptiles=8, do_sample=True, n_layers=2, dbg=None):
    nc = bass.Bass("TRN2", target_bir_lowering=False)
    hc = _host_consts()
    I = {}
    for n, s in list(IN_SHAPES.items()) + list(CONST_SHAPES.items()):
        I[n] = nc.dram_tensor(n, s, F32, kind="ExternalInput").ap()
    O = {}
    for n, s in OUT_SHAPES.items():
        O[n] = nc.dram_tensor(n, s, F32, kind="ExternalOutput").ap()
    KTd = nc.dram_tensor("KTd", [H, 128, SEQ], BF16, kind="Internal").ap()
    Vd = nc.dram_tensor("Vd", [H, 128, 32, 128], BF16, kind="Internal").ap()
    NWBLK = 50
    Wd = nc.dram_tensor("Wd", [NWBLK, 128, 5632], BF16, kind="Internal").ap()
    Wd_B = [Buf(f"Wd{i}") for i in range(NWBLK)]
    KTd_B = [Buf(f"KTd{h}") for h in range(H)]
    Vd_B = [Buf(f"Vd{h}") for h in range(H)]
    DBG = {}
    if dbg:
        for n, s in dbg.items():
            DBG[n] = nc.dram_tensor(n, s, F32, kind="ExternalOutput").ap()

    es = ExitStack()
    with es:
        P = Prog(nc, es)

        def sb(name, shape, dt):
            return es.enter_context(nc.sbuf_tensor("sb_" + name, shape, dt))

        NKROT = 4
        xres = sb("xres", [128, 4, D], F32)
        xres_B = [[Buf(f"xres{s}_{n}") for n in range(4)] for s in range(4)]
        xbf = sb("xbf", [128, 2, D], BF16)
        xbf_B = [Buf("xbf0"), Buf("xbf1")]
        actT = sb("actT", [128, KC, TT], BF16)
        actT_B = [[Buf(f"actT{c}_{s}") for s in range(4)] for c in range(KC)]
        NW = 3
        wslot = [sb(f"wslot{i}", [128, 5632], BF16) for i in range(NW)]
        wslot_B = [Buf(f"wslot{i}") for i in range(NW)]
        wsem = [P.new_dma_sem(f"w{i}") for i in range(NW)]
        arenaA = sb("arenaA", [128, FC * TT], BF16)
        hff = arenaA[:, :].rearrange("p (f t) -> p f t", t=TT)
        hff_B = [Buf(f"hff{f}") for f in range(FC)]
        qT = arenaA[:, 0:3072].rearrange("p (h t) -> p h t", t=TT)
        qxT = arenaA[:, 3072:6144].rearrange("p (h t) -> p h t", t=TT)
        kT = arenaA[:, 6144:9216].rearrange("p (h t) -> p h t", t=TT)
        mqT = arenaA[:, 9216:10240].rearrange("p (m t) -> p m t", t=TT)
        qT_B = [[Buf(f"qT{h}_{s}") for s in range(4)] for h in range(H)]
        qxT_B = [[Buf(f"qxT{h}_{s}") for s in range(4)] for h in range(H)]
        kT_B = [[Buf(f"kT{h}_{s}") for s in range(4)] for h in range(H)]
        mqT_B = [Buf("mqT0"), Buf("mqT1")]
        krot_B = [Buf(f"krot{i}") for i in range(NKROT)]
        for h in range(H):
            alias(qT_B[h], [hff_B[h]])
            alias(qxT_B[h], [hff_B[6 + h]])
            alias(kT_B[h], [hff_B[12 + h]])
        alias([mqT_B[0]], [hff_B[18]])
        alias([mqT_B[1]], [hff_B[19]])
        arenaB = sb("arenaB", [128, 6144], BF16)
        kz = arenaB[:, 0:3072].rearrange("p (s c) -> p s c", s=4)
        sg = arenaB[:, 3072:6144].rearrange("p (s c) -> p s c", s=4)
        kz_B = [Buf(f"kz{s}") for s in range(4)]
        sg_B = [Buf(f"sg{s}") for s in range(4)]
        NKV = 3
        kvK = [arenaB[:, 2048 * i:2048 * i + 1024] for i in range(NKV)]
        kvV = [arenaB[:, 2048 * i + 1024:2048 * i + 2048].rearrange("p (k e) -> p k e", e=128) for i in range(NKV)]
        kv_B = [Buf(f"kv{i}") for i in range(NKV)]
        kvsem = [P.new_dma_sem(f"kv{i}") for i in range(NKV)]
        alias(kv_B, kz_B + sg_B)
        vtok = sb("vtok", [128, 4, MIX], BF16)
        vtok_B = [Buf(f"vtok{s}") for s in range(4)]
        hm = sb("hm", [128, 2, MIX], BF16)
        hm_B = [Buf("hm0"), Buf("hm1")]
        NT32 = 4
        t32 = sb("t32", [128, NT32, 384], F32)
        t32_B = [Buf(f"t32_{i}") for i in range(NT32)]
        t32sem = [P.new_dma_sem(f"t32_{i}") for i in range(NT32)]
        t32flat = t32[:, :, :].rearrange("p a c -> p (a c)")
        accA = t32flat[:, 0:512]
        accB = t32flat[:, 512:1024]
        accA_B, accB_B = Buf("accA"), Buf("accB")
        alias([accA_B], [t32_B[0], t32_B[1]])
        alias([accB_B], [t32_B[1], t32_B[2]])
        ropeA2 = sb("ropeA", [128, 2, 384], F32)
        ropeB2 = sb("ropeB", [128, 2, 384], F32)
        ropeA2_B = [Buf("ropeA0"), Buf("ropeA1")]
        ropeB2_B = [Buf("ropeB0"), Buf("ropeB1")]
        krot4 = sb("krot4", [128, NKROT, 384], BF16)
        gnt = sb("gnt", [128, 2, 384], F32)
        gnt_B = [Buf("gnt0"), Buf("gnt1")]
        xbfn = sb("xbfn", [128, 4, D], BF16)
        xbfn_B = [Buf(f"xbfn{s}") for s in range(4)]
        xbfnsem = P.new_dma_sem("xbfn")
        sgate = sb("sgate", [128, 2, TT], BF16)
        sgate_B = [Buf("sgate0"), Buf("sgate1")]
        Pb = sb("Pb", [128, 6, TT], BF16)
        Pb_B = [Buf(f"Pb{i}") for i in range(6)]
        MTsb = sb("MTsb", [128, 2, 384], BF16)
        MTsb_B = [Buf("MT0"), Buf("MT1")]
        att_r = sb("att_r", [128, 2, TT], F32)
        att_r_B = [Buf("att_r0"), Buf("att_r1")]
        att_o = sb("att_o", [128, 2, TT], F32)
        att_o_B = [Buf("att_o0"), Buf("att_o1")]
        lnp = sb("lnp", [128, 2, D], F32)
        lnp_B = Buf("lnp")
        lnsem = P.new_dma_sem("lnp")
        ropet = sb("ropet", [128, 2, 4, NROPE], F32)
        ropet_B = [Buf("ropet0"), Buf("ropet1")]
        ropesem = [P.new_dma_sem("rope0"), P.new_dma_sem("rope1")]
        xsem = [P.new_dma_sem(f"x{s}") for s in range(4)]
        ysem = [P.new_dma_sem(f"y{s}") for s in range(4)]
        krot = krot4
        ident = sb("ident", [128, 128], BF16)
        ones_bf = sb("ones_bf", [128, 128], BF16)
        ones_f = sb("ones_f", [128, 128], F32)
        epsc = sb("epsc", [128, 1], F32)
        cst_B = Buf("consts")
        DT = sb("DT", [128, 768], F32)
        xi = sb("xi", [128, 2, 768], F32)
        zeta = sb("zeta", [128, 2, 768], F32)
        rett_B = Buf("ret_tables")
        gng = sb("gng", [128, 768], F32)
        R32 = sb("R32", [128, 2, 768], F32)
        Rbf = sb("Rbf", [128, 2, 768], BF16)
        R_B = [[Buf(f"R{g}_{h}") for h in range(H)] for g in range(2)]
        Rbf_B = [[Buf(f"Rbf{g}_{h}") for h in range(H)] for g in range(2)]
        memKT = sb("memKT", [128, 2, 4, NMEM], BF16)
        memV = sb("memV", [128, 2, 2, 4, 128], BF16)
        mem_B = [Buf("mem0"), Buf("mem1")]
        memT = sb("memT", [128, KC, NMEM], BF16)
        memT_B = Buf("memT")
        memst = sb("memst", [128, 512], F32)
        memst_B = Buf("memst")
        onespad = sb("onespad", [128, 2, 128], BF16)
        lam = sb("lam", [128, 8], F32)
        lam_B = Buf("lam")
        lamt = sb("lamt", [128, 4, 64], F32)
        sublg = sb("sublg", [128, 2], F32)
        stat = sb("stat", [128, 4, 8, 6], F32)
        stat_B = [Buf(f"stat{s}") for s in range(4)]
        aggr = sb("aggr", [128, 4, 8, 4], F32)
        aggr_B = [Buf(f"aggr{s}") for s in range(4)]
        _bs = {}

        def bsem(name):
            if name not in _bs:
                _bs[name] = P.new_dma_sem(name)
            return _bs[name]
        kvwsem = P.new_dma_sem("kvw")

        ps = [es.enter_context(nc.psum_tensor(f"ps{i}", [128, 512], F32)) for i in range(8)]
        ps_B = [Buf(f"ps{i}", excl=True) for i in range(8)]

        def psbf(i):
            return ps[i][:, :].bitcast(BF16)

        eng = P.engines

        def dma(q, out, in_, reads, writes, sem, is_store=False):
            e = eng[q]
            P.op(q, lambda s: e.dma_start(out=out, in_=in_).then_inc(s, 16), reads=reads, writes=writes,
                 dma_sem=sem, ndma=1, is_store=is_store)

        def act(out, in_, func, reads, writes, scale=1.0, bias=None):
            if bias is None:
                P.op(ACT, lambda: nc.scalar.activation(out=out, in_=in_, func=func, scale=scale), reads=reads, writes=writes)
            else:
                P.op(ACT, lambda: nc.scalar.activation(out=out, in_=in_, func=func, scale=scale, bias=bias), reads=reads, writes=writes)

        def tt(e, out, in0, in1, op, reads, writes):
            en = eng[e]
            P.op(e, lambda: en.tensor_tensor(out=out, in0=in0, in1=in1, op=op), reads=reads, writes=writes)

        def stt(e, out, in0, scalar, in1, op0, op1, reads, writes):
            en = eng[e]
            P.op(e, lambda: en.scalar_tensor_tensor(out=out, in0=in0, scalar=scalar, in1=in1, op0=op0, op1=op1),
                 reads=reads, writes=writes)

        def cp(e, out, in_, reads, writes):
            if e == ACT:
                act(out, in_, AF.Copy, reads, writes)
            else:
                en = eng[e]
                P.op(e, lambda: en.tensor_copy(out=out, in_=in_), reads=reads, writes=writes)

        def mms(out, pairs, reads, writes):
            n = len(pairs)

            def f():
                for i, (l, r) in enumerate(pairs):
                    ins = nc.tensor.matmul(out, l, r, start=(i == 0), stop=(i == n - 1))
                return ins
            P.op(PE, f, reads=reads, writes=writes)

        def transposes(bank, items, reads):
            pv = psbf(bank)

            def f():
                for (co, a) in items:
                    ins = nc.tensor.transpose(out=pv[:, co:co + 128], in_=a, identity=ident[:, :])
                return ins
            P.op(PE, f, reads=reads + [cst_B], writes=[ps_B[bank]])

        rr = {"tp": 0, "mm": 0}

        def tp_bank():
            rr["tp"] ^= 1
            return rr["tp"]

        def mm_bank():
            rr["mm"] = (rr["mm"] + 1) % 3
            return 2 + rr["mm"]

        wq = []

        def wblock(name, loads, nelem):
            wq.append((name, loads, nelem))

        wstate = {"issued": 0, "next": 0}

        wosem = [P.new_dma_sem(f"wo{i}") for i in range(NW)]

        def w_issue_upto(n):
            while wstate["issued"] < min(n, len(wq)):
                i = wstate["issued"]
                name, loads, nelem = wq[i]
                si = i % NW
                bi = i % NWBLK
                npass = len(wq) // NWBLK
                two = npass >= 3
                from_f32 = (i < NWBLK) or (two and i < 2 * NWBLK and bi % 2 == 1)
                publish = (npass > 1) and ((i < NWBLK and not (two and bi % 2 == 1)) or (two and NWBLK <= i < 2 * NWBLK and bi % 2 == 1))
                if from_f32:
                    ld = loads(wslot[si])

                    def f(s, ld=ld):
                        for (o, a) in ld:
                            nc.gpsimd.dma_start(out=o, in_=a).then_inc(s, 16)
                    P.op(POOL, f, writes=[wslot_B[si]], dma_sem=wsem[si], ndma=len(ld))
                    if publish:
                        dma(SP, Wd[bi, :, 0:nelem], wslot[si][:, 0:nelem], [wslot_B[si]], [Wd_B[bi]], wosem[si])
                else:
                    def f(s, si=si, bi=bi, nelem=nelem):
                        nc.gpsimd.dma_start(out=wslot[si][:, 0:nelem], in_=Wd[bi, :, 0:nelem]).then_inc(s, 16)
                    P.op(POOL, f, reads=[Wd_B[bi]], writes=[wslot_B[si]], dma_sem=wsem[si], ndma=1)
                wstate["issued"] += 1

        def w_next(name):
            i = wstate["next"]
            assert wq[i][0] == name, (wq[i][0], name)
            assert n_layers == 1 or len(wq) % NWBLK == 0
            w_issue_upto(i + NW)
            wstate["next"] += 1
            return i % NW

        def kcview(w2d, c0, ncols):
            return w2d.rearrange("(k p) c -> p k c", p=128)[:, :, c0:c0 + ncols]

        def plan_weights(tiles):
            for tl in tiles:
                for L in range(n_layers):
                    if L == 0:
                        w = I["ret_w_in"]
                        for j in range(8):
                            wblock(f"in{L}_{j}", (lambda j=j, w=w: (lambda sl: [(sl[:, 0:8 * 384].rearrange("p (k c) -> p k c", k=8), kcview(w, 384 * j, 384))]))(), 8 * 384)
                        wblock(f"mq{L}", (lambda w=w: (lambda sl: [(sl[:, 0:8 * 256].rearrange("p (k c) -> p k c", k=8), kcview(w, 3072, 256))]))(), 8 * 256)
                    else:
                        w = I["diff_w_in"]
                        for j in range(6):
                            wblock(f"in{L}_{j}", (lambda j=j, w=w: (lambda sl: [(sl[:, 0:8 * 384].rearrange("p (k c) -> p k c", k=8), kcview(w, 384 * j, 384))]))(), 8 * 384)
                        wblock(f"mq{L}", (lambda w=w: (lambda sl: [(sl[:, 0:8 * 256].rearrange("p (k c) -> p k c", k=8), kcview(w, 2304, 256))]))(), 8 * 256)
                    for n in range(2):
                        wblock(f"wo{L}_{n}", (lambda n=n, L=L: (lambda sl: [(sl[:, 0:8 * 512].rearrange("p (k c) -> p k c", k=8), kcview(I["w_o"][L], 512 * n, 512))]))(), 8 * 512)
                    for b in range(11):
                        wblock(f"gu{L}_{b}", (lambda b=b, L=L: (lambda sl: [
                            (sl[:, 0:2048].rearrange("p (k c) -> p k c", k=8), kcview(I["w_gate"][L], 256 * b, 256)),
                            (sl[:, 2048:4096].rearrange("p (k c) -> p k c", k=8), kcview(I["w_up"][L], 256 * b, 256))]))(), 4096)
                    for n in range(4):
                        wblock(f"dn{L}_{n}", (lambda n=n, L=L: (lambda sl: [(sl[:, 0:FC * 256].rearrange("p (k c) -> p k c", k=FC), kcview(I["w_down"][L], 256 * n, 256))]))(), FC * 256)

        def startup():
            dma(POOL, ident[:, :], I["ident"], [], [cst_B], bsem("ident"))
            P.op(DVE, lambda: nc.vector.memset(ones_bf[:, :], 1.0), writes=[cst_B])
            P.op(DVE, lambda: nc.vector.memset(ones_f[:, :], 1.0), writes=[cst_B])
            P.op(DVE, lambda: nc.vector.memset(epsc[:, :], LN_EPS), writes=[cst_B])
            P.op(DVE, lambda: nc.vector.memset(onespad[:, :, :], 0.0), writes=[cst_B])
            P.op(DVE, lambda: nc.vector.memset(onespad[:, 0, 0:64], 1.0), writes=[cst_B])
            P.op(DVE, lambda: nc.vector.memset(onespad[:, 1, 64:128], 1.0), writes=[cst_B])
            P.op(DVE, lambda: nc.vector.memset(memKT[:, :, :, :], 0.0), writes=mem_B)
            P.op(DVE, lambda: nc.vector.memset(memV[:, :, :, :, :], 0.0), writes=mem_B)
            dma(SP, gng[:, :], I["ret_gn_g"][0].partition_broadcast(128), [], [cst_B], bsem("cst"))
            dma(SP, lamt[:, :, :], I["lamv"].rearrange("a d -> (a d)").partition_broadcast(128).rearrange("p (a d) -> p a d", a=4), [], [lam_B], bsem("lam"))
            dma(SP, sublg[:, 0:1], I["subln"], [], [lam_B], bsem("lam"))
            P.op(DVE, lambda: nc.vector.memset(lam[:, :], 0.0), writes=[lam_B])
            tt(DVE, lamt[:, 0, :], lamt[:, 0, :], lamt[:, 1, :], ALU.mult, [lam_B], [lam_B])
            tt(DVE, lamt[:, 2, :], lamt[:, 2, :], lamt[:, 3, :], ALU.mult, [lam_B], [lam_B])
            P.op(DVE, lambda: nc.vector.tensor_reduce(out=lam[:, 0:1], in_=lamt[:, 0, :], axis=AX.X, op=ALU.add), reads=[lam_B], writes=[lam_B])
            P.op(DVE, lambda: nc.vector.tensor_reduce(out=lam[:, 1:2], in_=lamt[:, 2, :], axis=AX.X, op=ALU.add), reads=[lam_B], writes=[lam_B])
            act(lam[:, 2:4], lam[:, 0:2], AF.Exp, [lam_B], [lam_B])
            tt(DVE, lam[:, 4:5], lam[:, 2:3], lam[:, 3:4], ALU.subtract, [lam_B], [lam_B])
            P.op(DVE, lambda: nc.vector.tensor_scalar(out=lam[:, 5:6], in0=lam[:, 4:5], scalar1=LAM_INIT, scalar2=-1.0, op0=ALU.add, op1=ALU.mult), reads=[lam_B], writes=[lam_B])
            P.op(DVE, lambda: nc.vector.tensor_scalar(out=sublg[:, 1:2], in0=sublg[:, 0:1], scalar1=(1.0 - LAM_INIT), scalar2=None, op0=ALU.mult), reads=[lam_B], writes=[lam_B])

        def load_ret_tables(kind):
            if kind == "p":
                dma(SP, DT[:, :], I["DT_p"], [], [rett_B], bsem("rett"))
                dma(SP, xi[:, 0, :], I["xi_p"][0].partition_broadcast(128), [], [rett_B], bsem("rett"))
                dma(SP, zeta[:, 0, :], I["zeta_p"], [], [rett_B], bsem("rett"))
            else:
                dma(SP, DT[:, :], I["DT_s"], [], [rett_B], bsem("rett"))
                dma(SP, xi[:, 0, :], I["xi_sA"][0].partition_broadcast(128), [], [rett_B], bsem("rett"))
                dma(SP, xi[:, 1, :], I["xi_sB"][0].partition_broadcast(128), [], [rett_B], bsem("rett"))
                dma(SP, zeta[:, 0, :], I["zeta_sA"], [], [rett_B], bsem("rett"))
                dma(SP, zeta[:, 1, :], I["zeta_sB"], [], [rett_B], bsem("rett"))

        def to_featmajor(src_ap, src_bufs, nchunk, dst_fn, dst_bufs_fn, cast_eng=DVE, evac_eng=ACT):
            xi_ = rr.setdefault("xbf", 0)
            rr["xbf"] = xi_ ^ 1
            xb = xbf[:, xi_, 0:128 * nchunk]
            cp(cast_eng, xb, src_ap, src_bufs, [xbf_B[xi_]])
            bank = tp_bank()
            transposes(bank, [(128 * c, xbf[:, xi_, 128 * c:128 * (c + 1)]) for c in range(nchunk)], [xbf_B[xi_]])
            pv = psbf(bank)[:, 0:128 * nchunk].rearrange("p (c t) -> p c t", c=nchunk)
            cp(evac_eng, dst_fn(), pv, [ps_B[bank]], dst_bufs_fn())

        def mem_setup_prompt():
            for mt in range(2):
                dma(SP, xres[:, mt, :], I["memp"][128 * mt:128 * (mt + 1), :], [], xres_B[mt], xsem[mt])
                to_featmajor(xres[:, mt, :], xres_B[mt], KC, lambda mt=mt: memT[:, :, 128 * mt:128 * (mt + 1)], lambda: [memT_B])
            for L in range(n_layers):
                wv = kcview(I["w_mem_kv"][L], 0, 512)
                mwB = hff_B[0:8]
                P.op(POOL, lambda s, wv=wv: nc.gpsimd.dma_start(out=arenaA[:, 0:4096].rearrange("p (k c) -> p k c", k=8), in_=wv).then_inc(s, 16),
                     writes=mwB, dma_sem=bsem("memw"), ndma=1)
                W = arenaA[:, 0:4096].rearrange("p (k c) -> p k c", k=8)
                for mt in range(2):
                    b = mm_bank()
                    mms(ps[b][:, :], [(memT[:, c, 128 * mt:128 * (mt + 1)], W[:, c, :]) for c in range(KC)], [memT_B] + mwB, [ps_B[b]])
                    cp(ACT, memst[:, :], ps[b][:, :], [ps_B[b]], [memst_B])
                    dma(SP, O["mk_p"][L, 128 * mt:128 * (mt + 1), :], memst[:, 0:256], [memst_B], [], bsem("memst"), is_store=True)
                    dma(SP, O["mv_p"][L, 128 * mt:128 * (mt + 1), :], memst[:, 256:512], [memst_B], [], bsem("memst"), is_store=True)
                    for m in range(4):
                        cp(DVE, memV[:, L, mt, m, 64 * (m % 2):64 * (m % 2) + 64], memst[:, 256 + 64 * m:256 + 64 * (m + 1)], [memst_B], [mem_B[L]])
                for mc in range(2):
                    b = mm_bank()
                    mms(ps[b][:, 0:256], [(W[:, c, 128 * mc:128 * (mc + 1)], memT[:, c, :]) for c in range(KC)], [memT_B] + mwB, [ps_B[b]])
                    for hh in range(2):
                        m = 2 * mc + hh
                        cp(ACT, memKT[64 * hh:64 * hh + 64, L, m, :], ps[b][64 * hh:64 * hh + 64, 0:256], [ps_B[b]], [mem_B[L]])

        def mem_setup_sample(L, bidx):
            for mt in range(2):
                dma(SP, memst[:, 0:256], I["c_mk"][L, bidx, 128 * mt:128 * (mt + 1), :], [], [memst_B], bsem("memst"))
                dma(SP, memst[:, 256:512], I["c_mv"][L, bidx, 128 * mt:128 * (mt + 1), :], [], [memst_B], bsem("memst"))
                for m in range(4):
                    cp(DVE, memV[:, L, mt, m, 64 * (m % 2):64 * (m % 2) + 64], memst[:, 256 + 64 * m:256 + 64 * (m + 1)], [memst_B], [mem_B[L]])
                xi_ = rr.setdefault("xbf", 0)
                rr["xbf"] = xi_ ^ 1
                cp(DVE, xbf[:, xi_, 0:256], memst[:, 0:256], [memst_B], [xbf_B[xi_]])
                bank = tp_bank()
                transposes(bank, [(128 * c, xbf[:, xi_, 128 * c:128 * (c + 1)]) for c in range(2)], [xbf_B[xi_]])
                pv = psbf(bank)
                for mc in range(2):
                    for hh in range(2):
                        m = 2 * mc + hh
                        cp(ACT, memKT[64 * hh:64 * hh + 64, L, m, 128 * mt:128 * (mt + 1)], pv[64 * hh:64 * hh + 64, 128 * mc:128 * (mc + 1)], [ps_B[bank]], [mem_B[L]])

        def mem_attention(L, NTs, colgroups):
            for (c0, ncol, setup) in colgroups:
                if setup is not None:
                    setup()
                for mc in range(2):
                    ob, sbk = 4 + mc, 6 + mc
                    first = True
                    pending = None
                    for hh in range(2):
                        m = 2 * mc + hh
                        for mt in range(2):
                            sbank = tp_bank()
                            mms(ps[sbank][:, 0:ncol], [(memKT[:, L, m, 128 * mt:128 * (mt + 1)], mqT[:, mc, c0:c0 + ncol])],
                                [mem_B[L], mqT_B[mc]], [ps_B[sbank]])
                            pi = rr.setdefault("pb", 0)
                            rr["pb"] = (pi + 1) % 4
                            act(Pb[:, pi, 0:ncol], ps[sbank][:, 0:ncol], AF.Exp, [ps_B[sbank]], [Pb_B[pi]], scale=0.125)
                            last = (hh == 1 and mt == 1)

                            def f(first=first, last=last, pi=pi, m=m, mt=mt, hh=hh, ob=ob, sbk=sbk, c0=c0, ncol=ncol):
                                nc.tensor.matmul(ps[ob][:, c0:c0 + ncol], memV[:, L, mt, m, :], Pb[:, pi, 0:ncol], start=first, stop=last)
                                return nc.tensor.matmul(ps[sbk][:, c0:c0 + ncol], onespad[:, hh, :], Pb[:, pi, 0:ncol], start=first, stop=last)
                            if pending is not None:
                                pending()
                            pending = (lambda f=f, pi=pi, ob=ob, sbk=sbk: P.op(PE, f, reads=[mem_B[L], Pb_B[pi], cst_B], writes=[ps_B[ob], ps_B[sbk]]))
                            first = False
                    pending()
                    ri = rr.setdefault("attr", 0)
                    rr["attr"] = ri ^ 1
                    act(att_r[:, ri, 0:ncol], ps[sbk][:, c0:c0 + ncol], AF.Ln, [ps_B[sbk]], [att_r_B[ri]])
                    act(att_r[:, ri, 0:ncol], att_r[:, ri, 0:ncol], AF.Exp, [att_r_B[ri]], [att_r_B[ri]], scale=-1.0)
                    subs = sorted(set(range(c0 // 128, (c0 + ncol - 1) // 128 + 1)))
                    tt(DVE, actT[:, 6 + mc, c0:c0 + ncol], ps[ob][:, c0:c0 + ncol], att_r[:, ri, 0:ncol], ALU.mult,
                       [ps_B[ob], att_r_B[ri]], [actT_B[6 + mc][s] for s in subs])

        def layer_norm(L, which, NTs, final_out=None):
            gk, bk = ("ln1_g", "ln1_b") if which == 1 else ("ln2_g", "ln2_b")
            dma(SP, lnp[:, 0, :], I[gk][L].partition_broadcast(128), [], [lnp_B], lnsem)
            dma(SP, lnp[:, 1, :], I[bk][L].partition_broadcast(128), [], [lnp_B], lnsem)
            for s in range(NTs):
                def bns(s=s):
                    nc.vector.bn_stats(out=stat[:, s, 0, :], in_=xres[:, s, 0:512])
                    return nc.vector.bn_stats(out=stat[:, s, 1, :], in_=xres[:, s, 512:1024])
                P.op(DVE, bns, reads=xres_B[s], writes=[stat_B[s]])
                P.op(DVE, lambda s=s: nc.vector.bn_aggr(out=aggr[:, s, 0, 0:2], in_=stat[:, s, 0:2, :]), reads=[stat_B[s]], writes=[aggr_B[s]])
            agv = aggr[:, 0:NTs, 0, :]
            act(agv[:, :, 2:3], agv[:, :, 1:2], AF.Ln, aggr_B[0:NTs] + [cst_B], aggr_B[0:NTs], bias=epsc[:, :])
            act(agv[:, :, 3:4], agv[:, :, 2:3], AF.Exp, aggr_B[0:NTs], aggr_B[0:NTs], scale=-0.5)
            for s in range(NTs):
                stt(DVE, xres[:, s, :], xres[:, s, :], aggr[:, s, 0, 0:1], lnp[:, 0, :], ALU.subtract, ALU.mult,
                    xres_B[s] + [aggr_B[s], lnp_B], xres_B[s])
                stt(DVE, xres[:, s, :], xres[:, s, :], aggr[:, s, 0, 3:4], lnp[:, 1, :], ALU.mult, ALU.add,
                    xres_B[s] + [aggr_B[s], lnp_B], xres_B[s])
                if final_out is not None:
                    dma(SP, final_out(s), xres[:, s, :], xres_B[s], [], ysem[s], is_store=True)
                else:
                    to_featmajor(xres[:, s, :], xres_B[s], KC, lambda s=s: actT[:, :, 128 * s:128 * (s + 1)],
                                 lambda s=s: [actT_B[c][s] for c in range(KC)])

        def wo_and_ffn(L, NTs, final_out):
            ncol = 128 * NTs
            for n in range(2):
                si = w_next(f"wo{L}_{n}")
                W = wslot[si][:, 0:4096].rearrange("p (k c) -> p k c", k=8)
                for s in range(NTs):
                    b = mm_bank()
                    mms(ps[b][:, :], [(actT[:, c, 128 * s:128 * (s + 1)], W[:, c, :]) for c in range(KC)],
                        [actT_B[c][s] for c in range(KC)] + [wslot_B[si]], [ps_B[b]])
                    xb_ = [xres_B[s][2 * n], xres_B[s][2 * n + 1]]
                    stt(DVE, xres[:, s, 512 * n:512 * (n + 1)], xres[:, s, 512 * n:512 * (n + 1)], ALPHA, ps[b][:, :],
                        ALU.mult, ALU.add, xb_ + [ps_B[b]], xb_)
            layer_norm(L, 1, NTs)
            allact = [actT_B[c][s] for c in range(KC) for s in range(NTs)]
            for bb in range(11):
                si = w_next(f"gu{L}_{bb}")
                Wg = wslot[si][:, 0:2048].rearrange("p (k c) -> p k c", k=8)
                Wu = wslot[si][:, 2048:4096].rearrange("p (k c) -> p k c", k=8)
                for j in range(2):
                    f_ = 2 * bb + j
                    gi = rr.setdefault("gu", 0)
                    rr["gu"] = gi ^ 1
                    gb_, ub_ = 4 + gi, 6 + gi
                    mms(ps[gb_][:, 0:ncol], [(Wg[:, c, 128 * j:128 * (j + 1)], actT[:, c, 0:ncol]) for c in range(KC)],
                        allact + [wslot_B[si]], [ps_B[gb_]])
                    mms(ps[ub_][:, 0:ncol], [(Wu[:, c, 128 * j:128 * (j + 1)], actT[:, c, 0:ncol]) for c in range(KC)],
                        allact + [wslot_B[si]], [ps_B[ub_]])
                    act(sgate[:, gi, 0:ncol], ps[gb_][:, 0:ncol], AF.Silu, [ps_B[gb_]], [sgate_B[gi]])
                    tt(DVE, hff[:, f_, 0:ncol], sgate[:, gi, 0:ncol], ps[ub_][:, 0:ncol], ALU.mult,
                       [sgate_B[gi], ps_B[ub_]], [hff_B[f_]])
            for n in range(4):
                si = w_next(f"dn{L}_{n}")
                W = wslot[si][:, 0:FC * 256].rearrange("p (k c) -> p k c", k=FC)
                for s in range(NTs):
                    b = mm_bank()
                    mms(ps[b][:, 0:256], [(hff[:, f_, 128 * s:128 * (s + 1)], W[:, f_, :]) for f_ in range(FC)],
                        hff_B + [wslot_B[si]], [ps_B[b]])
                    stt(DVE, xres[:, s, 256 * n:256 * (n + 1)], xres[:, s, 256 * n:256 * (n + 1)], ALPHA, ps[b][:, 0:256],
                        ALU.mult, ALU.add, [xres_B[s][n], ps_B[b]], [xres_B[s][n]])
            layer_norm(L, 2, NTs, final_out=final_out)

        def rope_ret(ti, rslot, s, out_bf, out_bufs):
            ri_ = rr.setdefault("ropeab", 0)
            rr["ropeab"] = ri_ ^ 1
            ropeA, ropeB, ropeA_B, ropeB_B = ropeA2[:, ri_, :], ropeB2[:, ri_, :], ropeA2_B[ri_], ropeB2_B[ri_]
            tv = t32[:, ti, :].rearrange("p (h a d) -> p h a d", h=3, a=2)
            tab = ropet[:, rslot, s, :]
            cosb = tab[:, 0:64].unsqueeze(1).unsqueeze(1).to_broadcast([128, 3, 2, 64])
            nsin = tab[:, 64:128].unsqueeze(1).to_broadcast([128, 3, 64])
            sin = tab[:, 128:192].unsqueeze(1).to_broadcast([128, 3, 64])
            Av = ropeA.rearrange("p (h a d) -> p h a d", h=3, a=2)
            Bv = ropeB.rearrange("p (h a d) -> p h a d", h=3, a=2)
            tt(DVE, Bv[:, :, 0, :], tv[:, :, 1, :], nsin, ALU.mult, [t32_B[ti], ropet_B[rslot]], [ropeB_B])
            tt(DVE, Bv[:, :, 1, :], tv[:, :, 0, :], sin, ALU.mult, [t32_B[ti], ropet_B[rslot]], [ropeB_B])
            tt(DVE, Av, tv, cosb, ALU.mult, [t32_B[ti], ropet_B[rslot]], [ropeA_B])
            tt(DVE, out_bf, ropeA, ropeB, ALU.add, [ropeA_B, ropeB_B], out_bufs)

        def rope_diff(ti, rslot, s):
            ri_ = rr.setdefault("ropeab", 0)
            rr["ropeab"] = ri_ ^ 1
            ropeA, ropeB, ropeA_B, ropeB_B = ropeA2[:, ri_, :], ropeB2[:, ri_, :], ropeA2_B[ri_], ropeB2_B[ri_]
            tv = t32[:, ti, :].rearrange("p (g d) -> p g d", g=6)
            tab = ropet[:, rslot, s, :]
            cc = tab[:, 192:208].unsqueeze(1).to_broadcast([128, 6, 16])
            nsin = tab[:, 208:216].unsqueeze(1).to_broadcast([128, 6, 8])
            sin = tab[:, 216:224].unsqueeze(1).to_broadcast([128, 6, 8])
            Av = ropeA[:, 0:96].rearrange("p (g d) -> p g d", g=6)
            Bv = ropeB[:, 0:96].rearrange("p (g d) -> p g d", g=6)
            tt(DVE, Av, tv[:, :, 0:16], cc, ALU.mult, [t32_B[ti], ropet_B[rslot]], [ropeA_B])
            tt(DVE, Bv[:, :, 0:8], tv[:, :, 8:16], nsin, ALU.mult, [t32_B[ti], ropet_B[rslot]], [ropeB_B])
            tt(DVE, Bv[:, :, 8:16], tv[:, :, 0:8], sin, ALU.mult, [t32_B[ti], ropet_B[rslot]], [ropeB_B])
            tt(DVE, tv[:, :, 0:16], Av, Bv, ALU.add, [ropeA_B, ropeB_B], [t32_B[ti]])

        def next_t32():
            ti = rr.setdefault("t32", 0)
            rr["t32"] = (ti + 1) % NT32
            return ti

        def next_krot():
            ki = rr.setdefault("krot", 0)
            rr["krot"] = (ki + 1) % NKROT
            return ki

        def layer0_mixer(tl):
            NTs, kind = tl["NT"], tl["kind"]
            ncol = 128 * NTs
            rslot = tl["rslot"]
            ngrp = 1 if kind == "p" else 2
            dec = hc["dec_p"] if kind == "p" else hc["dec_s"]
            pend = []
            for j in range(8):
                si = w_next(f"in0_{j}")
                W = wslot[si][:, 0:8 * 384].rearrange("p (k c) -> p k c", k=8)
                typ, half = "qkvg"[j // 2], j % 2
                for s in range(NTs):
                    b = mm_bank()
                    mms(ps[b][:, 0:384], [(actT[:, c, 128 * s:128 * (s + 1)], W[:, c, :]) for c in range(KC)],
                        [actT_B[c][s] for c in range(KC)] + [wslot_B[si]], [ps_B[b]])
                    if typ in "qk":
                        ti = next_t32()
                        cp(ACT, t32[:, ti, :], ps[b][:, 0:384], [ps_B[b]], [t32_B[ti]])
                        ki = next_krot()
                        rope_ret(ti, rslot, s, krot[:, ki, :], [krot_B[ki]])

                        def post(ki=ki, typ=typ, half=half, s=s):
                            bank = tp_bank()
                            transposes(bank, [(128 * hh, krot[:, ki, 128 * hh:128 * (hh + 1)]) for hh in range(3)], [krot_B[ki]])
                            pv = psbf(bank)[:, 0:384].rearrange("p (h t) -> p h t", h=3)
                            hs = range(3 * half, 3 * half + 3)
                            if typ == "q":
                                cp(ACT, qT[:, 3 * half:3 * half + 3, 128 * s:128 * (s + 1)], pv, [ps_B[bank]], [qT_B[h][s] for h in hs])
                                for g in range(ngrp):
                                    dst = qxT[:, 3 * half:3 * half + 3, 128 * (s + g):128 * (s + g + 1)]
                                    tt(DVE, dst, pv, xi[:, g, 384 * half:384 * (half + 1)].rearrange("p (h t) -> p h t", h=3), ALU.mult,
                                       [ps_B[bank], rett_B], [qxT_B[h][s + g] for h in hs])
                            else:
                                cp(ACT, kT[:, 3 * half:3 * half + 3, 128 * s:128 * (s + 1)], pv, [ps_B[bank]], [kT_B[h][s] for h in hs])
                        if typ == "k":
                            for g in range(ngrp):
                                tt(DVE, kz[:, s + g, 384 * half:384 * (half + 1)], krot[:, ki, :], zeta[:, g, 384 * half:384 * (half + 1)], ALU.mult,
                                   [krot_B[ki], rett_B], [kz_B[s + g]])
                        pend.append(post)
                        if len(pend) > PDEPTH:
                            pend.pop(0)()
                        continue
                    if pend:
                        pend.pop(0)()
                    if typ == "v":
                        cp(ACT, vtok[:, s, 384 * half:384 * (half + 1)], ps[b][:, 0:384], [ps_B[b]], [vtok_B[s]])
                    else:
                        act(sg[:, s, 384 * half:384 * (half + 1)], ps[b][:, 0:384], AF.Silu, [ps_B[b]], [sg_B[s]])
            while pend:
                pend.pop(0)()
            si = w_next("mq0")
            W = wslot[si][:, 0:2048].rearrange("p (k c) -> p k c", k=8)
            allact = [actT_B[c][s] for c in range(KC) for s in range(NTs)]
            for mc in range(2):
                b = mm_bank()
                mms(ps[b][:, 0:ncol], [(W[:, c, 128 * mc:128 * (mc + 1)], actT[:, c, 0:ncol]) for c in range(KC)], allact + [wslot_B[si]], [ps_B[b]])
                cp(ACT, mqT[:, mc, 0:ncol], ps[b][:, 0:ncol], [ps_B[b]], [mqT_B[mc]])
            def retA(s):
                for half in range(2):
                    hs = list(range(3 * half, 3 * half + 3))
                    mb = 5 + half

                    def fm(s=s, hs=hs, mb=mb):
                        for x_, h in enumerate(hs):
                            ins = nc.tensor.matmul(ps[mb][:, 128 * x_:128 * (x_ + 1)], kT[:, h, 128 * s:128 * (s + 1)], qT[:, h, 128 * s:128 * (s + 1)], start=True, stop=True)
                        return ins
                    P.op(PE, fm, reads=[kT_B[h][s] for h in hs] + [qT_B[h][s] for h in hs], writes=[ps_B[mb]])
                    tt(DVE, MTsb[:, half, :], ps[mb][:, 0:384], DT[:, 384 * half:384 * (half + 1)], ALU.mult, [ps_B[mb], rett_B], [MTsb_B[half]])

            def retB(s):
                for half in range(2):
                    hs = list(range(3 * half, 3 * half + 3))
                    ob, rb = 2 + half, (7 if half == 0 else 4)

                    def fo(s=s, hs=hs, ob=ob, half=half):
                        for x_, h in enumerate(hs):
                            nc.tensor.matmul(ps[ob][:, 128 * x_:128 * (x_ + 1)], MTsb[:, half, 128 * x_:128 * (x_ + 1)], vtok[:, s, 128 * h:128 * (h + 1)], start=True, stop=False)
                            for g in range(ngrp):
                                ins = nc.tensor.matmul(ps[ob][:, 128 * x_:128 * (x_ + 1)], qxT[:, h, 128 * (s + g):128 * (s + g + 1)], Rbf[:, g, 128 * h:128 * (h + 1)], start=False, stop=(g == ngrp - 1))
                        return ins
                    P.op(PE, fo, reads=[MTsb_B[half], vtok_B[s]] + [qxT_B[h][s + g] for h in hs for g in range(ngrp)] + [Rbf_B[g][h] for h in hs for g in range(ngrp)],
                         writes=[ps_B[ob]])
                    for g in range(ngrp):
                        rbank = rb if g == 0 else tp_bank()

                        def fr(s=s, hs=hs, g=g, rbank=rbank):
                            for x_, h in enumerate(hs):
                                ins = nc.tensor.matmul(ps[rbank][:, 128 * x_:128 * (x_ + 1)], kz[:, s + g, 128 * h:128 * (h + 1)], vtok[:, s, 128 * h:128 * (h + 1)], start=True, stop=True)
                            return ins
                        P.op(PE, fr, reads=[kz_B[s + g], vtok_B[s]], writes=[ps_B[rbank]])
                        for x_, h in enumerate(hs):
                            stt(DVE, R32[:, g, 128 * h:128 * (h + 1)], R32[:, g, 128 * h:128 * (h + 1)], dec[h], ps[rbank][:, 128 * x_:128 * (x_ + 1)],
                                ALU.mult, ALU.add, [R_B[g][h], ps_B[rbank]], [R_B[g][h]])
                        cp(ACT, Rbf[:, g, 384 * half:384 * (half + 1)], R32[:, g, 384 * half:384 * (half + 1)], [R_B[g][h] for h in hs], [Rbf_B[g][h] for h in hs])

            def retC1(s):
                def bns(s=s):
                    for h in range(H):
                        ins = nc.vector.bn_stats(out=stat[:, s, h, :], in_=ps[2 + h // 3][:, 128 * (h % 3):128 * (h % 3 + 1)])
                    return ins
                P.op(DVE, bns, reads=[ps_B[2], ps_B[3]], writes=[stat_B[s]])

                def bna(s=s):
                    for h in range(H):
                        ins = nc.vector.bn_aggr(out=aggr[:, s, h, 0:2], in_=stat[:, s, h:h + 1, :])
                    return ins
                P.op(DVE, bna, reads=[stat_B[s]], writes=[aggr_B[s]])
                av = aggr[:, s, 0:6, :]
                act(av[:, :, 2:3], av[:, :, 1:2], AF.Ln, [aggr_B[s], cst_B], [aggr_B[s]], bias=epsc[:, :])
                act(av[:, :, 3:4], av[:, :, 2:3], AF.Exp, [aggr_B[s]], [aggr_B[s]], scale=-0.5)
                stt(DVE, av[:, :, 2:3], av[:, :, 0:1], -1.0, av[:, :, 3:4], ALU.mult, ALU.mult, [aggr_B[s]], [aggr_B[s]])

            def retC2(s, hi):
                for half in range(2):
                    ob = 2 + half
                    def fn(s=s, ob=ob, half=half):
                        for x_ in range(3):
                            h = 3 * half + x_
                            ins = nc.scalar.activation(out=gnt[:, half, 128 * x_:128 * (x_ + 1)], in_=ps[ob][:, 128 * x_:128 * (x_ + 1)],
                                                       func=AF.Identity, scale=aggr[:, s, h, 3:4], bias=aggr[:, s, h, 2:3])
                        return ins
                    P.op(ACT, fn, reads=[ps_B[ob], aggr_B[s]], writes=[gnt_B[half]])
                    tt(DVE, gnt[:, half, :], gnt[:, half, :], gng[:, 384 * half:384 * (half + 1)], ALU.mult, [gnt_B[half], cst_B], [gnt_B[half]])
                    tt(DVE, hm[:, hi, 384 * half:384 * (half + 1)], gnt[:, half, :], sg[:, s, 384 * half:384 * (half + 1)], ALU.mult, [gnt_B[half], sg_B[s]], [hm_B[hi]])

            def retD(s, hi):
                bank = tp_bank()
                transposes(bank, [(128 * h, hm[:, hi, 128 * h:128 * (h + 1)]) for h in range(H)], [hm_B[hi]])
                pv = psbf(bank)[:, 0:768].rearrange("p (h t) -> p h t", h=H)
                cp(ACT, actT[:, 0:6, 128 * s:128 * (s + 1)], pv, [ps_B[bank]], [actT_B[c][s] for c in range(6)])

            retA(0)
            dpend = None
            for s in range(NTs):
                hi = rr.setdefault("hm", 0)
                rr["hm"] = hi ^ 1
                retB(s)
                retC1(s)
                if s + 1 < NTs:
                    retA(s + 1)
                if dpend is not None:
                    dpend()
                retC2(s, hi)
                dpend = (lambda s=s, hi=hi: retD(s, hi))
            dpend()

        S1BANKS, S2BANKS = (0, 1, 6), (2, 3, 7)

        def attention_head(h, ncol, keytiles, mid=None):
            nkt = len(keytiles)
            fifo = []
            for x_, kt in enumerate(keytiles):
                c0, ncl = kt["c0"], kt["nc"]
                r_ = rr.setdefault("att3", 0)
                rr["att3"] = (r_ + 1) % 3
                s1b, s2b = S1BANKS[r_], S2BANKS[r_]
                mms(ps[s1b][:, c0:c0 + ncl], [(kt["kt"], qT[:, h, c0:c0 + ncl])], kt["reads"] + [qT_B[h][s] for s in range(4)], [ps_B[s1b]])
                mms(ps[s2b][:, c0:c0 + ncl], [(kt["kt"], qxT[:, h, c0:c0 + ncl])], kt["reads"] + [qxT_B[h][s] for s in range(4)], [ps_B[s2b]])
                p1, p2 = r_, 3 + r_
                act(Pb[:, p1, c0:c0 + ncl], ps[s1b][:, c0:c0 + ncl], AF.Exp, [ps_B[s1b]], [Pb_B[p1]], scale=0.125)
                act(Pb[:, p2, c0:c0 + ncl], ps[s2b][:, c0:c0 + ncl], AF.Exp, [ps_B[s2b]], [Pb_B[p2]], scale=0.125)
                if kt.get("fix") is not None:
                    kt["fix"](p1)
                    kt["fix"](p2)
                first, last = (x_ == 0), (x_ == nkt - 1)
                fc0, fnc = kt.get("acc_c0", 0), kt.get("acc_nc", ncol)
                for (acc, acc_B, p) in ((accA, accA_B, p1), (accB, accB_B, p2)):
                    if first:
                        cp(DVE, acc[:, fc0:fc0 + fnc], Pb[:, p, fc0:fc0 + fnc], [Pb_B[p]], [acc_B])
                    else:
                        tt(DVE, acc[:, c0:c0 + ncl], acc[:, c0:c0 + ncl], Pb[:, p, c0:c0 + ncl], ALU.add, [acc_B, Pb_B[p]], [acc_B])

                def f(p1=p1, p2=p2, kt=kt, first=first, last=last, fc0=fc0, fnc=fnc):
                    nc.tensor.matmul(ps[4][:, fc0:fc0 + fnc], kt["v"], Pb[:, p1, fc0:fc0 + fnc], start=first, stop=last)
                    return nc.tensor.matmul(ps[5][:, fc0:fc0 + fnc], kt["v"], Pb[:, p2, fc0:fc0 + fnc], start=first, stop=last)
                fifo.append(lambda f=f, kt=kt, p1=p1, p2=p2: P.op(PE, f, reads=kt["reads"] + [Pb_B[p1], Pb_B[p2]], writes=[ps_B[4], ps_B[5]]))
                if len(fifo) > 2:
                    fifo.pop(0)()
                if kt.get("pre") is not None:
                    kt["pre"]()
                if mid is not None and x_ == min(2, nkt - 1):
                    mid()
                    mid = None
            while fifo:
                fifo.pop(0)()
            mms(ps[6][:, fc0:fc0 + fnc], [(ones_f[:, :], accA[:, fc0:fc0 + fnc])], [accA_B, cst_B], [ps_B[6]])
            mms(ps[7][:, fc0:fc0 + fnc], [(ones_f[:, :], accB[:, fc0:fc0 + fnc])], [accB_B, cst_B], [ps_B[7]])

        def attention_finish_a(h, c0, ncol):
            act(att_r[:, 0, 0:ncol], ps[6][:, c0:c0 + ncol], AF.Ln, [ps_B[6]], [att_r_B[0]])
            cp(DVE, att_o[:, 0, 0:ncol], ps[4][:, c0:c0 + ncol], [ps_B[4]], [att_o_B[0]])
            act(att_r[:, 1, 0:ncol], ps[7][:, c0:c0 + ncol], AF.Ln, [ps_B[7]], [att_r_B[1]])
            cp(DVE, att_o[:, 1, 0:ncol], ps[5][:, c0:c0 + ncol], [ps_B[5]], [att_o_B[1]])
            act(att_r[:, 0, 0:ncol], att_r[:, 0, 0:ncol], AF.Exp, [att_r_B[0]], [att_r_B[0]], scale=-1.0)
            act(att_r[:, 1, 0:ncol], att_r[:, 1, 0:ncol], AF.Exp, [att_r_B[1]], [att_r_B[1]], scale=-1.0)
            tt(DVE, att_o[:, 0, 0:ncol], att_o[:, 0, 0:ncol], att_r[:, 0, 0:ncol], ALU.mult, [att_o_B[0], att_r_B[0]], [att_o_B[0]])
            tt(DVE, att_o[:, 1, 0:ncol], att_o[:, 1, 0:ncol], att_r[:, 1, 0:ncol], ALU.mult, [att_o_B[1], att_r_B[1]], [att_o_B[1]])
            stt(DVE, att_o[:, 0, 0:ncol], att_o[:, 1, 0:ncol], lam[:, 5:6], att_o[:, 0, 0:ncol], ALU.mult, ALU.add,
                [att_o_B[0], att_o_B[1], lam_B], [att_o_B[0]])

        def attention_finish_b(h, c0, ncol):
            act(att_o[:, 1, 0:ncol], att_o[:, 0, 0:ncol], AF.Square, [att_o_B[0]], [att_o_B[1]])
            b = S1BANKS[rr.setdefault("att3", 0)]
            mms(ps[b][:, 0:ncol], [(ones_f[:, :], att_o[:, 1, 0:ncol])], [att_o_B[1], cst_B], [ps_B[b]])
            act(att_r[:, 0, 0:ncol], ps[b][:, 0:ncol], AF.Ln, [ps_B[b], cst_B], [att_r_B[0]], scale=1.0 / 128, bias=epsc[:, :])
            act(att_r[:, 1, 0:ncol], att_r[:, 0, 0:ncol], AF.Exp, [att_r_B[0]], [att_r_B[1]], scale=-0.5)
            subs = sorted(set(range(c0 // 128, (c0 + ncol - 1) // 128 + 1)))
            stt(DVE, actT[:, h, c0:c0 + ncol], att_o[:, 0, 0:ncol], sublg[:, 1:2], att_r[:, 1, 0:ncol], ALU.mult, ALU.mult,
                [att_o_B[0], att_r_B[1], lam_B], [actT_B[h][s] for s in subs])

        def layer1_mixer(tl):
            NTs, kind = tl["NT"], tl["kind"]
            ncol = 128 * NTs
            rslot = tl["rslot"]
            t0 = tl["t0"]
            P.op(POOL, lambda: nc.gpsimd.memset(qT[64:128, :, 0:ncol], 0.0), writes=[b for h in range(H) for b in qT_B[h]])
            P.op(POOL, lambda: nc.gpsimd.memset(qxT[0:64, :, 0:ncol], 0.0), writes=[b for h in range(H) for b in qxT_B[h]])
            pend = []
            for j in range(6):
                si = w_next(f"in1_{j}")
                W = wslot[si][:, 0:8 * 384].rearrange("p (k c) -> p k c", k=8)
                typ, half = "qkv"[j // 2], j % 2
                hs = list(range(3 * half, 3 * half + 3))
                for s in range(NTs):
                    b = mm_bank()
                    mms(ps[b][:, 0:384], [(actT[:, c, 128 * s:128 * (s + 1)], W[:, c, :]) for c in range(KC)],
                        [actT_B[c][s] for c in range(KC)] + [wslot_B[si]], [ps_B[b]])
                    ti = next_t32()
                    cp(ACT, t32[:, ti, :], ps[b][:, 0:384], [ps_B[b]], [t32_B[ti]])
                    if typ in "qk":
                        rope_diff(ti, rslot, s)
                        ki = next_krot()
                        cp(DVE, krot[:, ki, :], t32[:, ti, :], [t32_B[ti]], [krot_B[ki]])
                        if typ == "k":
                            dma(SP, tl["dk_out"](s, half), t32[:, ti, :], [t32_B[ti]], [], t32sem[ti], is_store=True)

                        def post(ki=ki, typ=typ, half=half, s=s, hs=hs):
                            bank = tp_bank()
                            transposes(bank, [(128 * hh, krot[:, ki, 128 * hh:128 * (hh + 1)]) for hh in range(3)], [krot_B[ki]])
                            pv = psbf(bank)[:, 0:384].rearrange("p (h t) -> p h t", h=3)
                            if typ == "q":
                                cp(ACT, qT[0:64, 3 * half:3 * half + 3, 128 * s:128 * (s + 1)], pv[0:64], [ps_B[bank]], [qT_B[h][s] for h in hs])
                                cp(ACT, qxT[64:128, 3 * half:3 * half + 3, 128 * s:128 * (s + 1)], pv[64:128], [ps_B[bank]], [qxT_B[h][s] for h in hs])
                            else:
                                cp(ACT, kT[:, 3 * half:3 * half + 3, 128 * s:128 * (s + 1)], pv, [ps_B[bank]], [kT_B[h][s] for h in hs])
                        pend.append(post)
                        if len(pend) > PDEPTH:
                            pend.pop(0)()
                    else:
                        if pend:
                            pend.pop(0)()
                        cp(DVE, vtok[:, s, 384 * half:384 * (half + 1)], t32[:, ti, :], [t32_B[ti]], [vtok_B[s]])
                        dma(SP, tl["dv_out"](s, half), t32[:, ti, :], [t32_B[ti]], [], t32sem[ti], is_store=True)
            while pend:
                pend.pop(0)()
            si = w_next("mq1")
            W = wslot[si][:, 0:2048].rearrange("p (k c) -> p k c", k=8)
            allact = [actT_B[c][s] for c in range(KC) for s in range(NTs)]
            for mc in range(2):
                b = mm_bank()
                mms(ps[b][:, 0:ncol], [(W[:, c, 128 * mc:128 * (mc + 1)], actT[:, c, 0:ncol]) for c in range(KC)], allact + [wslot_B[si]], [ps_B[b]])
                cp(ACT, mqT[:, mc, 0:ncol], ps[b][:, 0:ncol], [ps_B[b]], [mqT_B[mc]])
            if kind == "p":
                tix = t0 // TT
                def fpub(sem, t0=t0, tix=tix):
                    for h in range(H):
                        nc.sync.dma_start(out=KTd[h, :, t0:t0 + TT], in_=kT[:, h, :]).then_inc(sem, 16)
                        nc.sync.dma_start(out=Vd[h, :, 4 * tix:4 * tix + 4, :], in_=vtok[:, :, 128 * h:128 * (h + 1)]).then_inc(sem, 16)
                P.op(SP, fpub, reads=[kT_B[h][s] for h in range(H) for s in range(4)] + vtok_B, writes=KTd_B + Vd_B, dma_sem=kvwsem, ndma=2 * H)
                nblk = (t0 + TT + 1023) // 1024
                blocks = [(h, kb) for h in range(H) for kb in range(nblk)]
                kvst = {"issued": 0}

                def kv_issue_upto(n, blocks=blocks):
                    while kvst["issued"] < min(n, len(blocks)):
                        i = kvst["issued"]
                        h_, kb = blocks[i]
                        nk = min(1024, t0 + TT - 1024 * kb)
                        ki = i % NKV

                        def f(sem, ki=ki, kb=kb, nk=nk, h_=h_):
                            nc.sync.dma_start(out=kvK[ki][:, 0:nk], in_=KTd[h_, :, 1024 * kb:1024 * kb + nk]).then_inc(sem, 16)
                            nc.sync.dma_start(out=kvV[ki][:, 0:nk // 128, :], in_=Vd[h_, :, 8 * kb:8 * kb + nk // 128, :]).then_inc(sem, 16)
                        P.op(SP, f, reads=[KTd_B[h_], Vd_B[h_]], writes=[kv_B[ki]], dma_sem=kvsem[ki], ndma=2)
                        kvst["issued"] += 1

                midcb = [None]
                kv_issue_upto(NKV - 1)
                for h in range(H):
                    kts = []
                    for kb in range(nblk):
                        gi = h * nblk + kb
                        nk = min(1024, t0 + TT - 1024 * kb)
                        ki = gi % NKV
                        for x_ in range(nk // 128):
                            key0 = 1024 * kb + 128 * x_
                            d = dict(kt=kvK[ki][:, 128 * x_:128 * (x_ + 1)], v=kvV[ki][:, x_, :], reads=[kv_B[ki]], c0=0, nc=ncol)
                            if x_ == 2:
                                d["pre"] = (lambda gi=gi: kv_issue_upto(gi + NKV))
                            if key0 >= t0:
                                jj = (key0 - t0) // 128
                                d["c0"], d["nc"] = 128 * jj, ncol - 128 * jj

                                def fix(p, jj=jj):
                                    if jj > 0:
                                        P.op(POOL, lambda: nc.gpsimd.memset(Pb[:, p, 0:128 * jj], 0.0), writes=[Pb_B[p]])
                                    P.op(POOL, lambda: nc.gpsimd.memset(Pb[64:128, p, 128 * jj:128 * jj + 64], 0.0), writes=[Pb_B[p]])
                                d["fix"] = fix
                            kts.append(d)
                    attention_head(h, ncol, kts, mid=midcb[0])
                    attention_finish_a(h, 0, ncol)
                    midcb[0] = (lambda h=h: attention_finish_b(h, 0, ncol))
                midcb[0]()
                midcb[0] = None
                mem_attention(1, NTs, [(0, ncol, None)])
            else:
                for bq in range(2):
                    for ktile in range(8):
                        for half in range(2):
                            ti = next_t32()
                            dma(SP, t32[:, ti, :], I["c_dk"][bq, 128 * ktile:128 * (ktile + 1), 384 * half:384 * (half + 1)], [], [t32_B[ti]], t32sem[ti])
                            ki = next_krot()
                            cp(DVE, krot[:, ki, :], t32[:, ti, :], [t32_B[ti]], [krot_B[ki]])
                            bank = tp_bank()
                            transposes(bank, [(128 * hh, krot[:, ki, 128 * hh:128 * (hh + 1)]) for hh in range(3)], [krot_B[ki]])
                            pv = psbf(bank)[:, 0:384].rearrange("p (h t) -> p h t", h=3)
                            cp(ACT, sKT[:, 3 * half:3 * half + 3, 128 * ktile:128 * (ktile + 1)], pv, [ps_B[bank]], [sKT_B])
                    P.op(POOL, lambda s, bq=bq: nc.gpsimd.dma_start(out=sV[:, :, :], in_=I["c_dv"][bq].rearrange("(k p) c -> p k c", p=128)).then_inc(s, 16),
                         writes=[sV_B], dma_sem=bsem("sV"), ndma=1)
                    for h in range(H):
                        kts = []
                        for ktile in range(8):
                            kts.append(dict(kt=sKT[:, h, 128 * ktile:128 * (ktile + 1)], v=sV[:, ktile, 128 * h:128 * (h + 1)], reads=[sKT_B, sV_B],
                                            c0=64 * bq, nc=64, acc_c0=64 * bq, acc_nc=64))

                        def fix(p, bq=bq):
                            lo = 64 * (1 - bq)
                            P.op(POOL, lambda: nc.gpsimd.memset(Pb[lo:lo + 64, p, 64 * bq:64 * bq + 64], 0.0), writes=[Pb_B[p]])
                        kts.append(dict(kt=kT[:, h, 0:128], v=vtok[:, 0, 128 * h:128 * (h + 1)], reads=[kT_B[h][0], vtok_B[0]],
                                        c0=64 * bq, nc=64, acc_c0=64 * bq, acc_nc=64, fix=fix))
                        attention_head(h, 64, kts)
                        attention_finish_a(h, 64 * bq, 64)
                        attention_finish_b(h, 64 * bq, 64)
                mem_attention(1, 1, [(64 * bq, 64, (lambda bq=bq: mem_setup_sample(1, bq))) for bq in range(2)])

        def x_prefetch(tl):
            tl["xpre"] = True
            n = tl["NT"]

            def f(sem, tl=tl, n=n):
                for s in range(n):
                    nc.gpsimd.dma_start(out=xbfn[:, s, :], in_=tl["x_in"](s)).then_inc(sem, 16)
            P.op(POOL, f, writes=xbfn_B[0:n], dma_sem=xbfnsem, ndma=n)

        def run_tile(tl):
            NTs, kind, rslot = tl["NT"], tl["kind"], tl["rslot"]
            for s in range(NTs):
                dma(SP, xres[:, s, :], tl["x_in"](s), [], xres_B[s], xsem[s])
            dma(SP, ropet[:, rslot, 0:NTs, :], tl["rope_in"], [], [ropet_B[rslot]], ropesem[rslot])
            if not tl.get("xpre"):
                x_prefetch(tl)
            for s in range(NTs):
                bank = tp_bank()
                transposes(bank, [(128 * c, xbfn[:, s, 128 * c:128 * (c + 1)]) for c in range(KC)], [xbfn_B[s]])
                pv = psbf(bank)[:, 0:1024].rearrange("p (c t) -> p c t", c=KC)
                cp(ACT if s % 2 == 0 else DVE, actT[:, :, 128 * s:128 * (s + 1)], pv, [ps_B[bank]], [actT_B[c][s] for c in range(KC)])
            if kind == "s":
                load_ret_tables("s")
                for g in range(2):
                    dma(SP, R32[:, g, :].rearrange("p (h e) -> p h e", h=H), I["c_ret"][g].rearrange("h d e -> d h e"), [], R_B[g], bsem(f"R{g}"))
                    cp(DVE, Rbf[:, g, :], R32[:, g, :], R_B[g], Rbf_B[g])
            layer0_mixer(tl)
            if kind == "p":
                mem_attention(0, NTs, [(0, 128 * NTs, None)])
                if tl["last"]:
                    dma(SP, O["rs_p"].rearrange("h d e -> d h e"), R32[:, 0, :].rearrange("p (h e) -> p h e", h=H), R_B[0], [], bsem("R0"), is_store=True)
            else:
                mem_attention(0, 1, [(64 * bq, 64, (lambda bq=bq: mem_setup_sample(0, bq))) for bq in range(2)])
                for g in range(2):
                    dma(SP, O["rs_s"][g].rearrange("h d e -> d h e"), R32[:, g, :].rearrange("p (h e) -> p h e", h=H), R_B[g], [], bsem(f"R{g}"), is_store=True)
            if n_layers == 1:
                wo_and_ffn(0, NTs, tl["y_out"])
                return
            wo_and_ffn(0, NTs, None)
            layer1_mixer(tl)
            if tl.get("next") is not None:
                x_prefetch(tl["next"])
            wo_and_ffn(1, NTs, tl["y_out"])

        sKT = None
        sV = None
        sKT_B = Buf("sKT")
        sV_B = Buf("sV")

        ptiles = []
        for t in range(n_ptiles):
            t0 = t * TT
            ptiles.append(dict(kind="p", NT=4, t0=t0, rslot=t % 2, last=(t == n_ptiles - 1),
                               x_in=(lambda s, t0=t0: I["xp"][t0 + 128 * s:t0 + 128 * (s + 1), :]),
                               rope_in=I["rope_p"][t0:t0 + TT, :].rearrange("(s p) c -> p s c", p=128),
                               y_out=(lambda s, t0=t0: O["y_p"][t0 + 128 * s:t0 + 128 * (s + 1), :]),
                               dk_out=(lambda s, half, t0=t0: O["dk_p"][t0 + 128 * s:t0 + 128 * (s + 1), 384 * half:384 * (half + 1)]),
                               dv_out=(lambda s, half, t0=t0: O["dv_p"][t0 + 128 * s:t0 + 128 * (s + 1), 384 * half:384 * (half + 1)])))
        tiles = list(ptiles)
        if do_sample:
            tiles.append(dict(kind="s", NT=1, t0=0, rslot=n_ptiles % 2, last=True,
                              x_in=(lambda s: I["xs"][:, :]),
                              rope_in=I["rope_s"].rearrange("(s p) c -> p s c", p=128),
                              y_out=(lambda s: O["y_s"][:, :]),
                              dk_out=(lambda s, half: O["dk_s"][:, 384 * half:384 * (half + 1)]),
                              dv_out=(lambda s, half: O["dv_s"][:, 384 * half:384 * (half + 1)])))
            sKT = xres[:, 1:4, :].rearrange("p s d -> p (s d)").bitcast(BF16).rearrange("p (h t) -> p h t", h=H)
            sV = arenaB[:, :].rearrange("p (k c) -> p k c", k=8)
            alias([sKT_B], [b for s_ in range(1, 4) for b in xres_B[s_]])
            alias([sV_B], kz_B + sg_B + kv_B)
        for a_, b_ in zip(tiles[:-1], tiles[1:]):
            a_["next"] = b_
        plan_weights(tiles)
        startup()

        def prompt_setup():
            mem_setup_prompt()
            load_ret_tables("p")
            P.op(DVE, lambda: nc.vector.memset(R32[:, :, :], 0.0), writes=R_B[0] + R_B[1])
            P.op(DVE, lambda: nc.vector.memset(Rbf[:, :, :], 0.0), writes=Rbf_B[0] + Rbf_B[1])
        for tl in tiles:
            if tl["kind"] == "p" and tl["t0"] == 0:
                prompt_setup()
            run_tile(tl)
        info = P.emit()
    return nc, info


_CACHE = {}


def _core_inputs(c, inp, hc):
    f = lambda a: np.ascontiguousarray(np.asarray(a, dtype=np.float32))
    m = {
        "xp": f(inp["x_prompt"][c]),
        "xs": f(inp["x_sample"][2 * c:2 * c + 2].reshape(128, D)),
        "memp": f(inp["mem_prompt"][c]),
        "c_ret": f(inp["cache_ret_state"][0, 2 * c:2 * c + 2]),
        "c_dk": f(inp["cache_diff_k"][0, 2 * c:2 * c + 2].reshape(2, PAST, MIX)),
        "c_dv": f(inp["cache_diff_v"][0, 2 * c:2 * c + 2].reshape(2, PAST, MIX)),
        "c_mk": f(inp["cache_mem_k"][:, 2 * c:2 * c + 2].reshape(2, 2, NMEM, 256)),
        "c_mv": f(inp["cache_mem_v"][:, 2 * c:2 * c + 2].reshape(2, 2, NMEM, 256)),
    }
    return m


def kernel(**inp):
    hc = _host_consts()
    f = lambda a: np.ascontiguousarray(np.asarray(a, dtype=np.float32))
    shared = {
        "ret_w_in": f(inp["ret_w_in"][0]), "ret_gn_g": f(inp["ret_gn_g"]), "diff_w_in": f(inp["diff_w_in"][0]),
        "lamv": f(np.stack([np.asarray(inp["diff_lambda_q1"])[0], np.asarray(inp["diff_lambda_k1"])[0],
                            np.asarray(inp["diff_lambda_q2"])[0], np.asarray(inp["diff_lambda_k2"])[0]])),
        "subln": f(np.asarray(inp["diff_subln_g"])[0].reshape(128, 1)),
        "w_mem_kv": f(inp["w_mem_kv"]), "w_o": f(inp["w_o"]),
        "ln1_g": f(inp["ln1_g"]), "ln1_b": f(inp["ln1_b"]), "w_gate": f(inp["w_gate"]), "w_up": f(inp["w_up"]),
        "w_down": f(inp["w_down"]), "ln2_g": f(inp["ln2_g"]), "ln2_b": f(inp["ln2_b"]),
    }
    for n in CONST_SHAPES:
        shared[n] = hc[n]
    if "nc" not in _CACHE:
        _CACHE["nc"] = build_program()[0]
    nc = _CACHE["nc"]
    in_maps = []
    for c in range(8):
        m = dict(shared)
        m.update(_core_inputs(c, inp, hc))
        in_maps.append(m)
    res = run_bass_kernel_spmd(nc, in_maps, core_ids=list(range(8)))
    R = res.results
    y_p = np.stack([R[c]["y_p"] for c in range(8)])
    y_s = np.concatenate([R[c]["y_s"].reshape(2, 64, D) for c in range(8)])
    rs_p = np.stack([R[c]["rs_p"] for c in range(8)])[None]
    rs_s = np.concatenate([R[c]["rs_s"] for c in range(8)])[None]
    dk_p = np.stack([R[c]["dk_p"].reshape(SEQ, H, 128) for c in range(8)])[None]
    dv_p = np.stack([R[c]["dv_p"].reshape(SEQ, H, 128) for c in range(8)])[None]
    dk_s = np.concatenate([R[c]["dk_s"].reshape(2, 64, H, 128) for c in range(8)])[None]
    dv_s = np.concatenate([R[c]["dv_s"].reshape(2, 64, H, 128) for c in range(8)])[None]
    mk_p = np.stack([R[c]["mk_p"].reshape(2, NMEM, 4, 64) for c in range(8)], axis=1)
    mv_p = np.stack([R[c]["mv_p"].reshape(2, NMEM, 4, 64) for c in range(8)], axis=1)
    outs = (y_p, y_s, rs_p, rs_s, dk_p, dv_p, dk_s, dv_s, mk_p, mv_p)
    return tuple(np.ascontiguousarray(o, dtype=np.float32) for o in outs)
```
